# Optimizing a Trainium2 kernel written in Bass

```python
import jax, jax.numpy as jnp
from jax import lax
import numpy as np

D_MODEL = 1024
BATCH = 8
SEQ = 2048
DEPTH = 2

SSD_HEADS = 16
SSD_HEAD_DIM = 64
SSD_INNER = SSD_HEADS * SSD_HEAD_DIM
SSD_GROUPS = 2
SSD_HPG = SSD_HEADS // SSD_GROUPS
SSD_STATE = 128
SSD_CONV = 4
SSD_CHUNK = 128
SSD_XBC = SSD_INNER + 2 * SSD_GROUPS * SSD_STATE
ATTN_Q_HEADS = 8
ATTN_KV_HEADS = 2
ATTN_HEAD_DIM = 64
ATTN_REP = ATTN_Q_HEADS // ATTN_KV_HEADS
ATTN_WINDOW = 128
ATTN_BLOCK = 128
ROPE_THETA = 10000.0
POOL_WINDOWS = (2, 4, 8, 16)
POOL_GROUPS = 4
POOL_GROUP_DIM = 128
POOL_WIDTH = POOL_GROUPS * POOL_GROUP_DIM
CONV_WIDTH = 512
CONV_KERNEL = 31
N_BRANCH = 4
ATTN_WIDTH = ATTN_Q_HEADS * ATTN_HEAD_DIM
KV_WIDTH = ATTN_KV_HEADS * ATTN_HEAD_DIM
BRANCH_WIDTHS = (SSD_INNER, ATTN_WIDTH, POOL_WIDTH, CONV_WIDTH)
MIX_WIDTH = SSD_INNER + ATTN_WIDTH + POOL_WIDTH + CONV_WIDTH
IN_WIDTHS = (SSD_INNER, SSD_XBC, SSD_HEADS, ATTN_WIDTH, KV_WIDTH, KV_WIDTH,
             POOL_WIDTH, 2 * CONV_WIDTH, N_BRANCH * D_MODEL)
IN_WIDTH = (SSD_INNER + SSD_XBC + SSD_HEADS + ATTN_WIDTH + 2 * KV_WIDTH
            + POOL_WIDTH + 2 * CONV_WIDTH + N_BRANCH * D_MODEL)
MLP_HIDDEN = 4 * D_MODEL
NORM_EPS = 1e-6

kernel_name = 'hybrid_ssd_swa_pool_conformer_block'

F32 = jnp.float32


def split_cols(a, widths):
    outs, start = [], 0
    for w in widths:
        outs.append(a[..., start:start + w])
        start += w
    return outs


def rms_norm(x, g):
    xf = x.astype(F32)
    y = xf * lax.rsqrt(jnp.mean(xf * xf, axis=-1, keepdims=True) + NORM_EPS)
    return (y * g.astype(F32)).astype(x.dtype)


def layer_norm(x, g, b):
    xf = x.astype(F32)
    mu = jnp.mean(xf, axis=-1, keepdims=True)
    xc = xf - mu
    y = xc * lax.rsqrt(jnp.mean(xc * xc, axis=-1, keepdims=True) + NORM_EPS)
    return (y * g.astype(F32) + b.astype(F32)).astype(x.dtype)


def causal_depthwise_conv(x, w, b):
    k_width, chans = w.shape
    y = lax.conv_general_dilated(
        x, w[:, None, :].astype(x.dtype), window_strides=(1,),
        padding=((k_width - 1, 0),), dimension_numbers=('NWC', 'WIO', 'NWC'),
        feature_group_count=chans)
    return y + b.astype(x.dtype)


def rope_tables(seq):
    inv = 1.0 / (ROPE_THETA ** (jnp.arange(0, ATTN_HEAD_DIM, 2, dtype=F32) / ATTN_HEAD_DIM))
    ang = jnp.arange(seq, dtype=F32)[:, None] * inv[None, :]
    return jnp.cos(ang), jnp.sin(ang)


def apply_rope(x, cos, sin):
    xf = x.astype(F32)
    x1, x2 = jnp.split(xf, 2, axis=-1)
    c, s = cos[:, None, :], sin[:, None, :]
    return jnp.concatenate([x1 * c - x2 * s, x2 * c + x1 * s], axis=-1).astype(x.dtype)


def segsum(a):
    t = a.shape[-1]
    rep = jnp.broadcast_to(a[..., None], a.shape + (t,))
    rep = jnp.where(jnp.tril(jnp.ones((t, t), bool), -1), rep, 0.0)
    seg = jnp.cumsum(rep, axis=-2)
    return jnp.where(jnp.tril(jnp.ones((t, t), bool), 0), seg, -jnp.inf)


def ssd_chunked(xs, dt, a, bm, cm):
    bsz, seq = xs.shape[0], xs.shape[1]
    nc, L = seq // SSD_CHUNK, SSD_CHUNK
    xdt = (xs * dt[..., None]).reshape(bsz, nc, L, SSD_GROUPS, SSD_HPG, SSD_HEAD_DIM)
    adt = jnp.moveaxis((dt * a).reshape(bsz, nc, L, SSD_GROUPS, SSD_HPG), 2, -1)
    bc = bm.reshape(bsz, nc, L, SSD_GROUPS, SSD_STATE)
    cc = cm.reshape(bsz, nc, L, SSD_GROUPS, SSD_STATE)
    a_cum = jnp.cumsum(adt, axis=-1)
    decay = jnp.exp(segsum(adt))
    cb = jnp.einsum('bclgn,bcsgn->bcgls', cc, bc)
    y_diag = jnp.einsum('bcgrls,bcsgrp->bclgrp', cb[:, :, :, None] * decay, xdt)
    decay_states = jnp.exp(a_cum[..., -1:] - a_cum)
    xw = xdt * jnp.moveaxis(decay_states, -1, 2)[..., None]
    states = jnp.einsum('bclgn,bclgrp->bcgrpn', bc, xw)
    chunk_a = jnp.pad(jnp.moveaxis(a_cum[..., -1], 1, -1), ((0, 0), (0, 0), (0, 0), (1, 0)))
    decay_chunk = jnp.exp(segsum(chunk_a))
    states = jnp.concatenate([jnp.zeros_like(states[:, :1]), states], axis=1)
    states = jnp.einsum('bgrzc,bcgrpn->bzgrpn', decay_chunk, states)[:, :-1]
    y_off = jnp.einsum('bclgn,bcgrpn,bcgrl->bclgrp', cc, states, jnp.exp(a_cum))
    return (y_diag + y_off).reshape(bsz, seq, SSD_GROUPS, SSD_HPG, SSD_HEAD_DIM)


def ssd_branch(z, xbc, dt_raw, conv_w, conv_b, dt_bias, a_log, d_skip, norm_g):
    bsz, seq, _ = xbc.shape
    xbc = jax.nn.silu(causal_depthwise_conv(xbc, conv_w, conv_b))
    xs, bm, cm = split_cols(xbc, (SSD_INNER, SSD_GROUPS * SSD_STATE, SSD_GROUPS * SSD_STATE))
    xs = xs.astype(F32).reshape(bsz, seq, SSD_GROUPS, SSD_HPG, SSD_HEAD_DIM)
    bm = bm.astype(F32).reshape(bsz, seq, SSD_GROUPS, SSD_STATE)
    cm = cm.astype(F32).reshape(bsz, seq, SSD_GROUPS, SSD_STATE)
    dt = jax.nn.softplus(dt_raw.astype(F32) + dt_bias.astype(F32)).reshape(bsz, seq, SSD_GROUPS, SSD_HPG)
    a = -jnp.exp(a_log.astype(F32)).reshape(SSD_GROUPS, SSD_HPG)
    y = ssd_chunked(xs, dt, a, bm, cm) + xs * d_skip.astype(F32).reshape(SSD_GROUPS, SSD_HPG)[..., None]
    y = y.reshape(bsz, seq, SSD_INNER) * jax.nn.silu(z.astype(F32))
    y = y.reshape(bsz, seq, SSD_GROUPS, SSD_INNER // SSD_GROUPS)
    y = y * lax.rsqrt(jnp.mean(y * y, axis=-1, keepdims=True) + NORM_EPS)
    y = y.reshape(bsz, seq, SSD_INNER) * norm_g.astype(F32)
    return y.astype(z.dtype)


def swa_branch(q, k, v, q_norm_g, k_norm_g, sinks, cos, sin):
    bsz, seq, _ = q.shape
    nb, T, hd = seq // ATTN_BLOCK, ATTN_BLOCK, ATTN_HEAD_DIM
    q = apply_rope(rms_norm(q.reshape(bsz, seq, ATTN_Q_HEADS, hd), q_norm_g), cos, sin)
    k = apply_rope(rms_norm(k.reshape(bsz, seq, ATTN_KV_HEADS, hd), k_norm_g), cos, sin)
    v = v.reshape(bsz, seq, ATTN_KV_HEADS, hd)
    qb = q.reshape(bsz, nb, T, ATTN_KV_HEADS, ATTN_REP, hd)
    pad = ((0, 0), (T, 0), (0, 0), (0, 0))
    kp = jnp.pad(k, pad).reshape(bsz, nb + 1, T, ATTN_KV_HEADS, hd)
    vp = jnp.pad(v, pad).reshape(bsz, nb + 1, T, ATTN_KV_HEADS, hd)
    kb = jnp.concatenate([kp[:, :-1], kp[:, 1:]], axis=2)
    vb = jnp.concatenate([vp[:, :-1], vp[:, 1:]], axis=2)
    scores = jnp.einsum('bnqhrd,bnjhd->bnhrqj', qb, kb).astype(F32) * (hd ** -0.5)
    qi = jnp.arange(T)[:, None]
    kj = jnp.arange(2 * T)[None, :]
    diff = qi - kj + T
    key_pos = jnp.arange(nb)[:, None, None] * T + kj[None] - T
    valid = (diff >= 0)[None] & (diff < ATTN_WINDOW)[None] & (key_pos >= 0)
    scores = jnp.where(valid[None, :, None, None], scores, -jnp.inf)
    sink = sinks.astype(F32).reshape(ATTN_KV_HEADS, ATTN_REP)[None, None, :, :, None, None]
    m = jnp.maximum(jnp.max(scores, axis=-1, keepdims=True), sink)
    p = jnp.exp(scores - m)
    probs = (p / (jnp.sum(p, axis=-1, keepdims=True) + jnp.exp(sink - m))).astype(v.dtype)
    out = jnp.einsum('bnhrqj,bnjhd->bnqhrd', probs, vb)
    return out.reshape(bsz, seq, ATTN_WIDTH)


def pool_branch(u, w_pool, scale):
    bsz, seq, _ = u.shape
    uf = u.astype(F32).reshape(bsz, seq, POOL_GROUPS, POOL_GROUP_DIM)
    cs = jnp.cumsum(uf, axis=1)
    pos = jnp.arange(seq, dtype=F32)[None, :, None]
    means = []
    for gi, w in enumerate(POOL_WINDOWS):
        cs_g = cs[:, :, gi]
        prev = jnp.pad(cs_g, ((0, 0), (w, 0), (0, 0)))[:, :seq]
        cnt = jnp.minimum(pos + 1.0, float(w))
        means.append((cs_g - prev) / cnt)
    pooled = jnp.stack(means, axis=2) - uf
    y = jnp.einsum('bskc,kcd->bskd', pooled, w_pool.astype(F32)).reshape(bsz, seq, POOL_WIDTH)
    return (y * scale.astype(F32)).astype(u.dtype)


def conformer_conv_branch(u2, dw_w, dw_b, ln_g, ln_b):
    a, g = jnp.split(u2, 2, axis=-1)
    u = a * jax.nn.sigmoid(g)
    u = causal_depthwise_conv(u, dw_w, dw_b)
    u = layer_norm(u, ln_g, ln_b)
    return jax.nn.silu(u)


def setup_inputs(seed: int = 0) -> dict:
    key = jax.random.key(seed)
    ks = jax.random.split(key, 24)
    n = jax.random.normal
    L = DEPTH
    dt0 = jnp.exp(jax.random.uniform(ks[6], (L, SSD_HEADS), minval=np.log(1e-3), maxval=np.log(1e-1)))
    return {
        'x': n(ks[0], (BATCH, SEQ, D_MODEL), F32),
        'norm_mix_g': 1.0 + 0.02 * n(ks[1], (L, D_MODEL), F32),
        'w_in': n(ks[2], (L, D_MODEL, IN_WIDTH), F32) * D_MODEL ** -0.5,
        'ssd_conv_w': n(ks[3], (L, SSD_CONV, SSD_XBC), F32) * SSD_CONV ** -0.5,
        'ssd_conv_b': 0.02 * n(ks[4], (L, SSD_XBC), F32),
        'ssd_dt_bias': dt0 + jnp.log(-jnp.expm1(-dt0)),
        'ssd_a_log': jnp.log(jax.random.uniform(ks[5], (L, SSD_HEADS), minval=1.0, maxval=16.0)),
        'ssd_d': 1.0 + 0.1 * n(ks[7], (L, SSD_HEADS), F32),
        'ssd_norm_g': 1.0 + 0.02 * n(ks[8], (L, SSD_INNER), F32),
        'q_norm_g': 1.0 + 0.02 * n(ks[9], (L, ATTN_HEAD_DIM), F32),
        'k_norm_g': 1.0 + 0.02 * n(ks[10], (L, ATTN_HEAD_DIM), F32),
        'attn_sinks': n(ks[11], (L, ATTN_Q_HEADS), F32),
        'pool_w': n(ks[12], (L, POOL_GROUPS, POOL_GROUP_DIM, POOL_GROUP_DIM), F32) * POOL_GROUP_DIM ** -0.5,
        'pool_scale': 1.0 + 0.1 * n(ks[13], (L, POOL_WIDTH), F32),
        'conv_dw_w': n(ks[14], (L, CONV_KERNEL, CONV_WIDTH), F32) * CONV_KERNEL ** -0.5,
        'conv_dw_b': 0.02 * n(ks[15], (L, CONV_WIDTH), F32),
        'conv_ln_g': 1.0 + 0.02 * n(ks[16], (L, CONV_WIDTH), F32),
        'conv_ln_b': 0.02 * n(ks[17], (L, CONV_WIDTH), F32),
        'gate_b': 0.01 * n(ks[18], (L, N_BRANCH * D_MODEL), F32),
        'w_branch': n(ks[19], (L, MIX_WIDTH, D_MODEL), F32) * MIX_WIDTH ** -0.5,
        'w_out': n(ks[20], (L, D_MODEL, D_MODEL), F32) * D_MODEL ** -0.5,
        'norm_mlp_g': 1.0 + 0.02 * n(ks[21], (L, D_MODEL), F32),
        'w_mlp_up': n(ks[22], (L, D_MODEL, MLP_HIDDEN), F32) * D_MODEL ** -0.5,
        'w_mlp_down': n(ks[23], (L, MLP_HIDDEN, D_MODEL), F32) * MLP_HIDDEN ** -0.5,
    }


def reference(x, norm_mix_g, w_in, ssd_conv_w, ssd_conv_b, ssd_dt_bias, ssd_a_log, ssd_d,
              ssd_norm_g, q_norm_g, k_norm_g, attn_sinks, pool_w, pool_scale, conv_dw_w,
              conv_dw_b, conv_ln_g, conv_ln_b, gate_b, w_branch, w_out, norm_mlp_g,
              w_mlp_up, w_mlp_down):
    bsz, seq, _ = x.shape
    cos, sin = rope_tables(seq)
    for i in range(DEPTH):
        h = rms_norm(x, norm_mix_g[i])
        proj = h @ w_in[i]
        (z, xbc, dt_raw, q, k, v, u_pool, u_conv, g_raw) = split_cols(proj, IN_WIDTHS)
        y_ssd = ssd_branch(z, xbc, dt_raw, ssd_conv_w[i], ssd_conv_b[i], ssd_dt_bias[i],
                           ssd_a_log[i], ssd_d[i], ssd_norm_g[i])
        y_attn = swa_branch(q, k, v, q_norm_g[i], k_norm_g[i], attn_sinks[i], cos, sin)
        y_pool = pool_branch(u_pool, pool_w[i], pool_scale[i])
        y_conv = conformer_conv_branch(u_conv, conv_dw_w[i], conv_dw_b[i], conv_ln_g[i], conv_ln_b[i])
        gates = jax.nn.sigmoid((g_raw + gate_b[i]).astype(F32)).astype(x.dtype)
        gates = gates.reshape(bsz, seq, N_BRANCH, D_MODEL)
        w_rows = split_cols(jnp.swapaxes(w_branch[i], 0, 1), BRANCH_WIDTHS)
        merged = (gates[:, :, 0] * (y_ssd @ w_rows[0].T)
                  + gates[:, :, 1] * (y_attn @ w_rows[1].T)
                  + gates[:, :, 2] * (y_pool @ w_rows[2].T)
                  + gates[:, :, 3] * (y_conv @ w_rows[3].T))
        x = x + merged @ w_out[i]
        h2 = rms_norm(x, norm_mlp_g[i])
        x = x + jnp.square(jax.nn.relu(h2 @ w_mlp_up[i])) @ w_mlp_down[i]
    return x
```

```python
import numpy as np
from contextlib import ExitStack
import concourse.bass as bass
import concourse.mybir as mybir
from concourse.bass_utils import run_bass_kernel_spmd

F32 = mybir.dt.float32
BF16 = mybir.dt.bfloat16
ALU = mybir.AluOpType
AF = mybir.ActivationFunctionType
AX = mybir.AxisListType

D = 1024
S = 2048
DEPTH = 2
NT = S // 128
NG = S // 512
EPS = 1e-6
IN_WIDTH = 8976
C_Z, C_XS, C_B, C_C, C_DT, C_Q, C_K, C_V, C_POOL, C_CONV, C_GATE = (
    0, 1024, 2048, 2304, 2560, 2576, 3088, 3216, 3344, 3856, 4880)


class R:
    __slots__ = ("w", "rd")

    def __init__(self):
        self.w = None
        self.rd = {}


def rgrid(*shape):
    if len(shape) == 1:
        return [R() for _ in range(shape[0])]
    return [rgrid(*shape[1:]) for _ in range(shape[0])]


def flat(x):
    if isinstance(x, R):
        return [x]
    out = []
    for e in x:
        out.extend(flat(e))
    return out


import sys


def _where():
    f = sys._getframe(2)
    out = []
    while f is not None and len(out) < 4:
        out.append(f.f_lineno)
        f = f.f_back
    return out


class Op:
    __slots__ = ("eng", "cs", "fn", "deps", "sig", "cnt", "waits", "snap", "where")


ENGS = ("pe", "act", "dve", "pool", "sp")


class Sched:
    def __init__(self):
        self.ops = []

    def add(self, eng, fn, reads=(), writes=(), dma=False):
        op = Op()
        op.eng = eng
        op.cs = ("d_" + eng) if dma else eng
        op.fn = fn
        op.where = _where()
        op.sig = dma
        op.cnt = 0
        deps = {}
        rl = flat(reads)
        wl = flat(writes)
        for r in rl:
            if r.w is not None:
                deps[id(r.w)] = r.w
        for w in wl:
            if w.w is not None:
                deps[id(w.w)] = w.w
            for o in w.rd.values():
                deps[id(o)] = o
        if eng == "pe" and not dma:
            deps = {i: o for i, o in deps.items() if o.cs != "pe"}
        op.deps = list(deps.values())
        for r in rl:
            r.rd[op.cs] = op
        for w in wl:
            w.w = op
            w.rd = {}
        self.ops.append(op)
        return op

    def finalize(self):
        for op in self.ops:
            for d in op.deps:
                d.sig = True
        counts = {}
        for op in self.ops:
            if op.sig:
                counts[op.cs] = counts.get(op.cs, 0) + 1
            op.cnt = counts.get(op.cs, 0)
        known = {e: {} for e in ENGS}
        for op in self.ops:
            kn = known[op.eng]
            need = {}
            for d in op.deps:
                if d.cnt > need.get(d.cs, 0):
                    need[d.cs] = d.cnt
            waits = []
            for d in sorted(op.deps, key=lambda o: -o.cnt):
                if kn.get(d.cs, 0) >= d.cnt:
                    continue
                if need.get(d.cs, 0) != d.cnt:
                    continue
                waits.append((d.cs, d.cnt))
                for c, v in d.snap.items():
                    if kn.get(c, 0) < v:
                        kn[c] = v
                kn[d.cs] = max(kn.get(d.cs, 0), d.cnt)
            op.waits = waits
            if op.sig:
                sn = dict(kn)
                op.snap = sn
            else:
                op.snap = None
        self.counts = counts

    def emit(self, nc, block, sems, final_waits=()):
        by = {e: [] for e in ENGS}
        for op in self.ops:
            by[op.eng].append(op)

        def seg(cs, cnt):
            sg = SEG_DMA if cs.startswith("d_") else SEG
            inc = 16 if cs.startswith("d_") else 1
            i = (cnt - 1) // sg
            return sems[cs][i], (cnt - i * sg) * inc

        def run(e, ops):
            segdone = {}
            for op in ops:
                for cs, cnt in op.waits:
                    if cs.startswith("d_"):
                        s_idx = (cnt - 1) // SEG_DMA
                        for k in range(segdone.get(cs, 0), s_idx):
                            e.wait_ge(sems[cs][k], SEG_DMA * 16)
                        segdone[cs] = max(segdone.get(cs, 0), s_idx)
                    sm, val = seg(cs, cnt)
                    e.wait_ge(sm, val)
                if op.cs.startswith("d_") and op.cnt > MAX_DMA_OUT:
                    sm, val = seg(op.cs, op.cnt - MAX_DMA_OUT)
                    e.wait_ge(sm, val)
                try:
                    ins = op.fn(e)
                except Exception:
                    print("FAILED OP at lines", op.where, op.eng)
                    raise
                if op.sig:
                    inc = 16 if op.cs.startswith("d_") else 1
                    sm, _ = seg(op.cs, op.cnt)
                    ins.then_inc(sm, inc)

        @block.tensor
        def _(e):
            run(e, by["pe"])

        @block.scalar
        def _(e):
            run(e, by["act"])

        @block.vector
        def _(e):
            run(e, by["dve"])

        @block.gpsimd
        def _(e):
            run(e, by["pool"])

        @block.sync
        def _(e):
            run(e, by["sp"])
            for cs, n in final_waits:
                k = 0
                while k * SEG_DMA < n:
                    last = min(n, (k + 1) * SEG_DMA)
                    sm, val = seg(cs, last)
                    e.wait_ge(sm, val)
                    k += 1


import os as _os
SEG = int(_os.environ.get('SEG', '2000'))
SEG_DMA = int(_os.environ.get('SEG_DMA', '1000000'))
MAX_DMA_OUT = int(_os.environ.get('MAX_DMA_OUT', '4'))


import os
ATT_LEVEL = int(os.environ.get('ATT_LEVEL', '9'))
STOP_LAYER = int(os.environ.get('STOP_LAYER', '0'))
NW = 3
L = DEPTH
FM_GMIX, FM_GMLP, FM_GATEB, FM_CW, FM_CB, FM_PSC, FM_DWW, FM_DWB, FM_LNG, FM_LNB = (
    0, L * 8, L * 16, L * 48, L * 96, L * 108, L * 112, L * 236, L * 240, L * 244)
NFM = L * 248
RP_DTB, RP_ALOG, RP_D, RP_QG, RP_KG, RP_SINK = 0, L * 16, L * 32, L * 48, L * 112, L * 176
NRP = L * 180
CB_ID, CB_UT, CB_MCUR, CB_MPREV, CB_ONL, CB_ONR = 0, 128, 256, 768, 1280, 1408
NCB = 1536


class Arena:
    def __init__(self, b, name, nbytes):
        self.t = b.sb(name, [128, nbytes // 4], F32)
        self.n = nbytes
        self.off = 0

    def reset(self):
        self.off = 0

    def alloc(self, shape, dt):
        es = 4 if dt == F32 else 2
        n = 1
        for d in shape[1:]:
            n *= d
        nb = (n * es + 31) // 32 * 32
        off = self.off
        self.off += nb
        assert self.off <= self.n, (self.off, self.n)
        ap = self.t[:, off // 4:(off + nb) // 4]
        if dt != F32:
            ap = ap.bitcast(dt)
        ap = ap[:, :n]
        if len(shape) == 3:
            ap = ap.rearrange("p (a b) -> p a b", a=shape[1])
        elif len(shape) == 4:
            ap = ap.rearrange("p (a b c) -> p a b c", a=shape[1], b=shape[2])
        if shape[0] < 128:
            ap = ap[:shape[0]]
        return ap


class Builder:
    def __init__(self, debug=None, nlayers=DEPTH, stop_after=None):
        self.debug = debug or []
        self.nlayers = nlayers
        self.stop_after = stop_after
        self.nc = bass.Bass("TRN2", target_bir_lowering=False)
        self.sc = Sched()
        self.es = ExitStack()
        self.psum_i = 0
        self.w_i = 0
        self.dbg_d = {}

    def dram_in(self, name, shape, dtype=F32):
        return self.nc.dram_tensor(name, list(shape), dtype, kind="ExternalInput").ap()

    def sb(self, name, shape, dtype=F32):
        return self.es.enter_context(self.nc.sbuf_tensor(name, list(shape), dtype))[:]

    def op(self, eng, fn, reads=(), writes=(), dma=False):
        return self.sc.add(eng, fn, reads, writes, dma)

    def psum(self):
        i = self.psum_i
        self.psum_i = (i + 1) % 8
        return self.pbanks[i], self.pres[i]

    def mm(self, out, lhsT, rhs, start, stop, reads, writes):
        return self.op("pe", lambda e: e.matmul(out, lhsT, rhs, start=start, stop=stop), reads, writes)

    def tr(self, out, in_, ident, reads, writes):
        return self.op("pe", lambda e: e.transpose(out, in_, ident), reads, writes)

    def act(self, out, in_, func, reads, writes, bias=None, scale=None):
        kw = {}
        if bias is not None:
            kw["bias"] = bias
        if scale is not None:
            kw["scale"] = scale
        return self.op("act", lambda e: e.activation(out=out, in_=in_, func=func, **kw), reads, writes)

    def tt(self, eng, out, in0, in1, op, reads, writes):
        return self.op(eng, lambda e: e.tensor_tensor(out=out, in0=in0, in1=in1, op=op), reads, writes)

    def tsc(self, eng, out, in0, s1, s2, op0, op1, reads, writes):
        if s2 is None:
            return self.op(eng, lambda e: e.tensor_scalar(out, in0, s1, None, op0), reads, writes)
        return self.op(eng, lambda e: e.tensor_scalar(out, in0, s1, s2, op0, op1), reads, writes)

    def stt(self, out, in0, scalar, in1, op0, op1, reads, writes):
        return self.op("dve", lambda e: e.scalar_tensor_tensor(out=out, in0=in0, scalar=scalar, in1=in1,
                                                              op0=op0, op1=op1), reads, writes)

    def cp(self, eng, out, in_, reads, writes):
        if eng == "act":
            return self.op("act", lambda e: e.activation(out=out, in_=in_, func=AF.Copy), reads, writes)
        return self.op(eng, lambda e: e.tensor_copy(out=out, in_=in_), reads, writes)

    def fence(self, old, new):
        d = self.dummy
        self.op("dve", lambda e: e.memset(d, 0.0), writes=[old, new])

    def tap(self, name, ap, res):
        for n, shape, dt in self.debug:
            if n == name:
                dst = self.dbg_d[name]
                self.op("sp", lambda e: e.dma_start(out=dst, in_=ap), reads=res, dma=True)

    def wload(self, src2d, r0, kp, nk, c0, ncols):
        i = self.w_i
        self.w_i = (i + 1) % NW
        buf, res = self.wbufs[i], self.wres[i]
        src = src2d[r0:r0 + kp * nk, c0:c0 + ncols].rearrange("(k p) c -> p k c", p=kp)
        dst = buf[:kp, :nk, :ncols]
        self.op("pool", lambda e: e.dma_start(out=dst, in_=src), writes=res, dma=True)
        return buf, res

    def build(self):
        with self.es:
            self._build()
        return self.nc

    def _build(self):
        nc = self.nc
        sc = self.sc
        xT_d = self.dram_in("xT", [D, S])
        w_in_d = self.dram_in("w_in", [L, D, IN_WIDTH])
        w_br_d = self.dram_in("w_branch", [L, 2560, D])
        w_out_d = self.dram_in("w_out", [L, D, D])
        w_up_d = self.dram_in("w_up", [L, D, 4096])
        w_dn_d = self.dram_in("w_down", [L, 4096, D])
        tabfm_d = self.dram_in("tab_fm", [128, NFM])
        tabrp_d = self.dram_in("tab_rp", [128, NRP])
        gnorm_d = self.dram_in("gnorm_rep", [L, 128, 1024])
        ropec_d = self.dram_in("rope_c", [128, 16 * 32])
        ropes_d = self.dram_in("rope_s2", [128, 16 * 64])
        cf32_d = self.dram_in("c_f32", [128, 384])
        cbf_d = self.dram_in("c_bf", [128, NCB], BF16)
        poolw_d = self.dram_in("pool_w", [128, L * 4 * 128])
        out_d = nc.dram_tensor("yT", [D, S], F32, kind="ExternalOutput").ap()
        xs_d = nc.dram_tensor("x_spill", [D, S], F32, kind="ExternalOutput").ap()
        for name, shape, dt in self.debug:
            self.dbg_d[name] = nc.dram_tensor("dbg_" + name, list(shape), dt, kind="ExternalOutput").ap()

        AX_ = Arena(self, "arena_x", 65536)
        AH_ = Arena(self, "arena_h", 32768)
        AY_ = Arena(self, "arena_y", 32768)
        AT_ = Arena(self, "arena_t", 30720)
        xT = AX_.alloc([128, 8, S], F32)
        xT_r = rgrid(8, NG)
        hT = AH_.alloc([128, 8, S], BF16)
        hT_r = rgrid(8, NG)
        self.wbufs = [self.sb("wbuf%d" % i, [128, 8, 512], BF16) for i in range(NW)]
        self.wres = [R() for _ in range(NW)]
        tabfm = self.sb("tabfm", [128, NFM])
        tabrp = self.sb("tabrp", [128, NRP])
        gnorm = self.sb("gnorm", [128, 1024])
        gnorm_r = R()
        ropec = self.sb("ropec", [128, 16, 32])
        ropes = self.sb("ropes", [128, 16, 2, 32])
        cf32 = self.sb("cf32", [128, 384])
        cbf = self.sb("cbf", [128, NCB], BF16)
        poolw = self.sb("poolw", [128, L * 4, 128], BF16)
        gts = [self.sb("gt%d" % i, [128, 512]) for i in range(2)]
        gts_r = [R(), R()]
        self.dummy = self.sb("fence_dummy", [128, 8])
        cR = R()
        U_f = cf32[:, 0:128]
        L_f = cf32[:, 128:256]
        ones_f = cf32[:, 256:384]
        ident = cbf[:, CB_ID:CB_ID + 128]
        UT_b = cbf[:, CB_UT:CB_UT + 128]
        mcur = cbf[:, CB_MCUR:CB_MCUR + 512]
        mprev = cbf[:, CB_MPREV:CB_MPREV + 512]
        onesLR = [cbf[:, CB_ONL:CB_ONL + 128], cbf[:, CB_ONR:CB_ONR + 128]]
        ones_bf = self.sb("ones_bf", [128, 128], BF16)

        self.pbanks = [self.es.enter_context(nc.psum_tensor("ps%d" % i, [128, 512], F32))[:] for i in range(8)]
        self.pres = [R() for _ in range(8)]
        def fmcol(base, idx):
            return tabfm[:, base + idx: base + idx + 1]

        xT_dv = xT_d.rearrange("(k p) t -> p k t", p=128)
        for k in range(8):
            self.op("sp", lambda e, k=k: e.dma_start(out=xT[:, k, :], in_=xT_dv[:, k, :]), writes=xT_r[k], dma=True)
        for dst, src in ((tabfm, tabfm_d), (tabrp, tabrp_d), (cf32, cf32_d), (cbf, cbf_d),
                         (ropec, ropec_d.rearrange("p (t c) -> p t c", c=32)),
                         (ropes, ropes_d.rearrange("p (t a c) -> p t a c", a=2, c=32))):
            self.op("sp", lambda e, dst=dst, src=src: e.dma_start(out=dst, in_=src), writes=cR, dma=True)
        self.op("pool", lambda e: e.dma_start(out=poolw, in_=poolw_d.rearrange("p (g d) -> p g d", d=128)),
                writes=cR, dma=True)
        self.op("dve", lambda e: e.memset(ones_bf, 1.0), writes=cR)

        def rmsnorm(gbase):
            AT_.reset()
            sq = AT_.alloc([128, 8, 512], BF16)
            rstd = AT_.alloc([128, 512], F32)
            sq_r, rstd_r = R(), R()
            self.fence(self.at_live, [sq_r, rstd_r])
            self.at_live = [sq_r, rstd_r]
            for tg in range(NG):
                ts = slice(tg * 512, (tg + 1) * 512)
                self.act(sq, xT[:, :, ts], AF.Square, reads=[xT_r[k][tg] for k in range(8)], writes=sq_r)
                pb, pr = self.psum()
                for k in range(8):
                    self.mm(pb, ones_bf, sq[:, k, :], k == 0, k == 7, reads=[cR, sq_r], writes=pr)
                self.act(rstd, pb, AF.Sqrt, reads=pr, writes=rstd_r, scale=1.0 / D, bias=EPS)
                self.op("dve", lambda e: e.reciprocal(rstd, rstd), reads=rstd_r, writes=rstd_r)
                for k in range(8):
                    self.stt(hT[:, k, ts], xT[:, k, ts], fmcol(gbase, k), rstd, ALU.mult, ALU.mult,
                             reads=[xT_r[k][tg], cR, rstd_r], writes=hT_r[k][tg])

        def merge(l, b, nkc, kp, ysrc, yres, wload_b, first):
            for ct in range(2):
                wg, wg_r = self.wload(w_in_d[l], 0, 128, 8, C_GATE + b * 1024 + ct * 512, 512)
                wb, wb_r = wload_b(ct)
                for f4 in range(4):
                    fo = ct * 4 + f4
                    fs = slice(f4 * 128, (f4 + 1) * 128)
                    for tg in range(NG):
                        ts = slice(tg * 512, (tg + 1) * 512)
                        pg, pgr = self.psum()
                        py, pyr = self.psum()
                        for k in range(8):
                            self.mm(pg, wg[:, k, fs], hT[:, k, ts], k == 0, k == 7,
                                    reads=[wg_r, hT_r[k][tg]], writes=pgr)
                        for kc in range(nkc):
                            self.mm(py, wb[:kp, kc, fs], ysrc(kc, ts), kc == 0, kc == nkc - 1,
                                    reads=[wb_r, yres(kc, tg)], writes=pyr)
                        gi = (fo * NG + tg) % 2
                        gt, gt_r = gts[gi], gts_r[gi]
                        self.act(gt, pg, AF.Sigmoid, reads=[pgr, cR], writes=gt_r,
                                 bias=fmcol(FM_GATEB, l * 32 + b * 8 + fo))
                        if first:
                            self.tt("dve", xT[:, fo, ts], gt, py, ALU.mult, reads=[gt_r, pyr], writes=xT_r[fo][tg])
                        else:
                            self.tt("dve", gt, gt, py, ALU.mult, reads=[gt_r, pyr], writes=gt_r)
                            self.tt("pool", xT[:, fo, ts], xT[:, fo, ts], gt, ALU.add,
                                    reads=[gt_r, xT_r[fo][tg]], writes=xT_r[fo][tg])

        self.at_live = []
        self.ay_live = []
        xs_r = rgrid(8)

        for l in range(self.nlayers):
            if l > 0:
                self.op("sp", lambda e, l=l: e.dma_start(out=gnorm, in_=gnorm_d[l]), writes=gnorm_r, dma=True)
            else:
                self.op("sp", lambda e: e.dma_start(out=gnorm, in_=gnorm_d[0]), writes=gnorm_r, dma=True)
            rmsnorm(FM_GMIX + l * 8)
            self.tap("hT%d" % l, hT, hT_r)
            if self.stop_after == "norm1" and l == STOP_LAYER:
                break
            xs_dv = xs_d.rearrange("(k p) t -> p k t", p=128)
            if l > 0:
                for k in range(8):
                    self.op("sp", lambda e, k=k: e.dma_start(out=xs_dv[:, k, :], in_=xT[:, k, :]),
                            reads=xT_r[k], writes=xs_r[k], dma=True)
            x_src = xT_dv if l == 0 else xs_dv
            if self.stop_after == "spill" and l == STOP_LAYER:
                break

            AX_.reset()
            AT_.reset()
            AY_.reset()
            BT = AX_.alloc([128, 2, S], BF16)
            CT = AX_.alloc([128, 2, S], BF16)
            Btok = AX_.alloc([128, 2, NT, 128], BF16)
            cbm = AX_.alloc([128, 2, NT, 128], BF16)
            y2h = AX_.alloc([128, 4, NT, 128], BF16)
            ytok = AX_.alloc([128, NT, 128], F32)
            xs_tok = AX_.alloc([128, NT, 128], BF16)
            xdt = AX_.alloc([128, NT, 128], BF16)
            BT_r, CT_r, Btok_r, cbm_r = rgrid(2, NG), rgrid(2, NG), rgrid(2), rgrid(2)
            y2h_r, ytok_r, xs_tok_r, xdt_r = rgrid(4), rgrid(NT), R(), R()
            ax_new = [BT_r, CT_r, Btok_r, cbm_r, y2h_r, ytok_r, xs_tok_r, xdt_r]
            self.fence(xT_r, ax_new)

            upad = AT_.alloc([128, 3 + S], BF16)
            diag4 = AT_.alloc([128, 4, 128], BF16)
            xsT = AT_.alloc([128, S], BF16)
            xw = AT_.alloc([128, NT, 128], BF16)
            xsD = AT_.alloc([128, NT, 128], BF16)
            dtt = AT_.alloc([128, 256], F32)
            adt = AT_.alloc([128, 256], F32)
            eac = AT_.alloc([128, 256], F32)
            dsd = AT_.alloc([128, 256], F32)
            cdec = AT_.alloc([128, 256], F32)
            tmpa = AT_.alloc([128, 256], F32)
            tmpb = AT_.alloc([128, 256], F32)
            Abuf = AT_.alloc([128, 2, 128], F32)
            Ebuf = AT_.alloc([128, 2, 128], F32)
            Mbuf = AT_.alloc([128, 2, 128], BF16)
            Sst = AT_.alloc([128, 128], F32)
            Sbf = AT_.alloc([128, 128], BF16)
            yo = AT_.alloc([128, 128], F32)
            zs = AT_.alloc([128, 512], F32)
            ssq_hp = AT_.alloc([128, 16], F32)
            ssq_g = AT_.alloc([128, 16], F32)
            rstd_g = AT_.alloc([128, 16], F32)
            upad_r, diag4_r, xsT_r, xw_r, xsD_r = R(), R(), rgrid(NG), R(), R()
            dt_r, A_r, E_r, M_r, S_r, Sbf_r, yo_r, zs_r, ssq_r = R(), R(), R(), R(), R(), R(), R(), R(), R()
            A2_r, E2_r = R(), R()
            at_new = [upad_r, diag4_r, xsT_r, xw_r, xsD_r, dt_r, A_r, E_r, M_r, S_r, Sbf_r, yo_r, zs_r, ssq_r, A2_r, E2_r]
            self.fence(self.at_live, at_new)
            self.at_live = at_new
            ySSD = AY_.alloc([128, 8, S], BF16)
            ySSD_r = rgrid(8, NG)
            self.fence(self.ay_live, ySSD_r)
            self.ay_live = [ySSD_r]

            self.op("dve", lambda e: e.memset(upad[:, 0:3], 0.0), writes=upad_r)
            Ab = [Abuf, tmpa.rearrange("p (a b) -> p a b", a=2)]
            Eb = [Ebuf, tmpb.rearrange("p (a b) -> p a b", a=2)]
            A_rs, E_rs = [A_r, A2_r], [E_r, E2_r]
            Sb = [Sbf, zs[:, 0:64].bitcast(BF16)]
            Sb_rs = [Sbf_r, zs_r]

            wdt, wdt_r = self.wload(w_in_d[l], 0, 128, 8, C_DT, 16)
            pdt, pdt_r = self.psum()
            for tt_ in range(NT):
                for k in range(8):
                    self.mm(pdt[:, tt_ * 16:(tt_ + 1) * 16], hT[:, k, tt_ * 128:(tt_ + 1) * 128], wdt[:, k, 0:16],
                            k == 0, k == 7, reads=[wdt_r, hT_r[k][tt_ // 4]], writes=pdt_r)
            rep16 = lambda base: tabrp[:, base + l * 16: base + (l + 1) * 16].unsqueeze(1).broadcast_to([128, 16, 16])
            v3 = lambda a: a.rearrange("p (t h) -> p t h", h=16)
            self.tt("dve", v3(tmpa), v3(pdt[:, 0:256]), rep16(RP_DTB), ALU.add, reads=[pdt_r, cR], writes=dt_r)
            self.act(tmpb, tmpa, AF.Abs, reads=dt_r, writes=dt_r)
            self.act(tmpb, tmpb, AF.Exp, reads=dt_r, writes=dt_r, scale=-1.0)
            self.act(tmpb, tmpb, AF.Ln, reads=dt_r, writes=dt_r, bias=1.0)
            self.stt(dtt, tmpa, 0.0, tmpb, ALU.max, ALU.add, reads=dt_r, writes=dt_r)
            self.act(v3(tmpa), rep16(RP_ALOG), AF.Exp, reads=[dt_r, cR], writes=dt_r)
            self.stt(adt, dtt, -1.0, tmpa, ALU.mult, ALU.mult, reads=dt_r, writes=dt_r)
            pac, pac_r = self.psum()
            pal, pal_r = self.psum()
            for c in range(NT):
                cs_ = slice(c * 16, (c + 1) * 16)
                self.mm(pac[:, cs_], U_f, adt[:, cs_], True, True, reads=[cR, dt_r], writes=pac_r)
            for c in range(NT):
                cs_ = slice(c * 16, (c + 1) * 16)
                self.mm(pal[:, cs_], ones_f, adt[:, cs_], True, True, reads=[cR, dt_r], writes=pal_r)
            self.act(eac, pac[:, 0:256], AF.Exp, reads=pac_r, writes=dt_r)
            self.act(cdec, pal[:, 0:256], AF.Exp, reads=pal_r, writes=dt_r)
            self.cp("act", tmpa, pac[:, 0:256], reads=pac_r, writes=dt_r)
            self.tt("dve", tmpb, pal[:, 0:256], tmpa, ALU.subtract, reads=[pal_r, dt_r], writes=dt_r)
            self.act(dsd, tmpb, AF.Exp, reads=dt_r, writes=dt_r)
            self.tap("dt%d" % l, dtt, dt_r)
            self.tap("eac%d" % l, eac, dt_r)

            def conv4(fc, src_w, src_wr, wcol, dst_fn, dst_res_fn):
                self.tt("dve", diag4, ident.unsqueeze(1).broadcast_to([128, 4, 128]),
                        tabfm[:, FM_CW + (l * 12 + fc) * 4: FM_CW + (l * 12 + fc) * 4 + 4].unsqueeze(2).broadcast_to([128, 4, 128]),
                        ALU.mult, reads=cR, writes=diag4_r)
                for tg in range(NG):
                    ts = slice(tg * 512, (tg + 1) * 512)
                    pb, pr = self.psum()
                    for k in range(8):
                        self.mm(pb, src_w[:, k, wcol:wcol + 128], hT[:, k, ts], k == 0, k == 7,
                                reads=[src_wr, hT_r[k][tg]], writes=pr)
                    self.cp("act", upad[:, 3 + tg * 512: 3 + (tg + 1) * 512], pb, reads=pr, writes=upad_r)
                for tg in range(NG):
                    pb, pr = self.psum()
                    for tap_ in range(4):
                        self.mm(pb, diag4[:, tap_, :], upad[:, tg * 512 + tap_: tg * 512 + tap_ + 512],
                                tap_ == 0, tap_ == 3, reads=[diag4_r, upad_r], writes=pr)
                    self.act(dst_fn(tg), pb, AF.Silu, reads=[pr, cR], writes=dst_res_fn(tg),
                             bias=fmcol(FM_CB, l * 12 + fc))

            wbc, wbc_r = self.wload(w_in_d[l], 0, 128, 8, C_B, 512)
            for j in range(4):
                dst_t, dst_r = (BT, BT_r) if j < 2 else (CT, CT_r)
                g = j % 2
                conv4(8 + j, wbc, wbc_r, j * 128,
                      lambda tg, dst_t=dst_t, g=g: dst_t[:, g, tg * 512:(tg + 1) * 512],
                      lambda tg, dst_r=dst_r, g=g: dst_r[g][tg])
            for g in range(2):
                for t4 in range(4):
                    pb, pr = self.psum()
                    pbb = pb.bitcast(BF16)
                    for i4 in range(4):
                        tt_ = t4 * 4 + i4
                        self.tr(pbb[:, i4 * 128:(i4 + 1) * 128], BT[:, g, tt_ * 128:(tt_ + 1) * 128], ident,
                                reads=[BT_r[g][t4], cR], writes=pr)
                    self.cp("act", Btok[:, g, t4 * 4:(t4 + 1) * 4, :],
                            pbb[:, 0:512].rearrange("p (a b) -> p a b", a=4), reads=pr, writes=Btok_r[g])
                for t4 in range(4):
                    pb, pr = self.psum()
                    for i4 in range(4):
                        c = t4 * 4 + i4
                        self.mm(pb[:, i4 * 128:(i4 + 1) * 128], BT[:, g, c * 128:(c + 1) * 128],
                                CT[:, g, c * 128:(c + 1) * 128], True, True,
                                reads=[BT_r[g][t4], CT_r[g][t4]], writes=pr)
                    self.tt("dve", cbm[:, g, t4 * 4:(t4 + 1) * 4, :], pb.rearrange("p (a b) -> p a b", a=4),
                            UT_b.unsqueeze(1).broadcast_to([128, 4, 128]), ALU.mult, reads=[pr, cR], writes=cbm_r[g])
            self.tap("BT%d" % l, BT, BT_r)
            self.tap("cbm%d" % l, cbm, cbm_r)

            wxs = [None, None]
            wz = [None, None]
            for hp in range(8):
                g = hp // 4
                r0 = 2 * hp
                if hp % 4 == 0:
                    wxs_t, wxs_r = self.wload(w_in_d[l], 0, 128, 8, C_XS + (hp // 4) * 512, 512)
                conv4(hp, wxs_t, wxs_r, (hp % 4) * 128,
                      lambda tg: xsT[:, tg * 512:(tg + 1) * 512], lambda tg: xsT_r[tg])
                for t4 in range(4):
                    pb, pr = self.psum()
                    pbb = pb.bitcast(BF16)
                    for i4 in range(4):
                        tt_ = t4 * 4 + i4
                        self.tr(pbb[:, i4 * 128:(i4 + 1) * 128], xsT[:, tt_ * 128:(tt_ + 1) * 128], ident,
                                reads=[xsT_r[t4], cR], writes=pr)
                    self.cp("act", xs_tok[:, t4 * 4:(t4 + 1) * 4, :],
                            pbb[:, 0:512].rearrange("p (a b) -> p a b", a=4), reads=pr, writes=xs_tok_r)
                v4 = lambda a: a.rearrange("p t (h d) -> p t h d", h=2)
                hb = lambda a: a.rearrange("p (t h) -> p t h", h=16)[:, :, r0:r0 + 2].unsqueeze(3).broadcast_to([128, NT, 2, 64])
                self.tt("dve", v4(xdt), v4(xs_tok), hb(dtt), ALU.mult, reads=[xs_tok_r, dt_r], writes=xdt_r)
                self.tt("dve", v4(xw), v4(xdt), hb(dsd), ALU.mult, reads=[xdt_r, dt_r], writes=xw_r)
                dcol = tabrp[:, RP_D + l * 16 + r0: RP_D + l * 16 + r0 + 2]
                self.tt("dve", v4(xsD), v4(xs_tok),
                        dcol.unsqueeze(1).unsqueeze(3).broadcast_to([128, NT, 2, 64]), ALU.mult,
                        reads=[xs_tok_r, cR], writes=xsD_r)
                self.op("dve", lambda e: e.memset(Sst, 0.0), writes=S_r)

                def stage1(c):
                    pseg, pseg_r = self.psum()
                    bb = c % 2
                    for j in range(2):
                        col = c * 16 + r0 + j
                        self.tsc("pool", Ab[bb][:, j, :], L_f, adt[:, col:col + 1], None, ALU.mult, None,
                                 reads=[cR, dt_r], writes=A_rs[bb])
                        self.mm(pseg[:, j * 128:(j + 1) * 128], Ab[bb][:, j, :], U_f, True, True,
                                reads=[A_rs[bb], cR], writes=pseg_r)
                    self.act(Eb[bb], pseg[:, 0:256].rearrange("p (a b) -> p a b", a=2), AF.Exp,
                             reads=pseg_r, writes=E_rs[bb])

                def stage2(c):
                    bb = c % 2
                    self.tt("dve", Mbuf, Eb[bb], cbm[:, g, c, :].unsqueeze(1).broadcast_to([128, 2, 128]), ALU.mult,
                            reads=[E_rs[bb], cbm_r[g]], writes=M_r)
                    if c < NT - 1:
                        pst, pst_r = self.psum()
                        self.mm(pst[:, 0:128], Btok[:, g, c, :], xw[:, c, :], True, True,
                                reads=[Btok_r[g], xw_r], writes=pst_r)
                        for j in range(2):
                            col = c * 16 + r0 + j
                            self.stt(Sst[:, j * 64:(j + 1) * 64], Sst[:, j * 64:(j + 1) * 64], cdec[:, col:col + 1],
                                     pst[:, j * 64:(j + 1) * 64], ALU.mult, ALU.add,
                                     reads=[S_r, dt_r, pst_r], writes=S_r)
                        self.cp("pool", Sb[(c + 1) % 2], Sst, reads=S_r, writes=Sb_rs[(c + 1) % 2])
                    pyd, pyd_r = self.psum()
                    self.mm(pyd[:, 0:128], ident, xsD[:, c, :], True, False, reads=[cR, xsD_r], writes=pyd_r)
                    for j in range(2):
                        self.mm(pyd[:, j * 64:(j + 1) * 64], Mbuf[:, j, :], xdt[:, c, j * 64:(j + 1) * 64],
                                False, j == 1, reads=[M_r, xdt_r], writes=pyd_r)
                    if c > 0:
                        pyo, pyo_r = self.psum()
                        self.mm(pyo[:, 0:128], CT[:, g, c * 128:(c + 1) * 128], Sb[c % 2], True, True,
                                reads=[CT_r[g][c // 4], Sb_rs[c % 2]], writes=pyo_r)
                        for j in range(2):
                            col = c * 16 + r0 + j
                            self.act(yo[:, j * 64:(j + 1) * 64], pyo[:, j * 64:(j + 1) * 64], AF.Identity,
                                     reads=[pyo_r, dt_r], writes=yo_r, scale=eac[:, col:col + 1])
                        self.tt("dve", ytok[:, c, :], pyd[:, 0:128], yo, ALU.add, reads=[pyd_r, yo_r], writes=ytok_r[c])
                    else:
                        self.cp("act", ytok[:, c, :], pyd[:, 0:128], reads=pyd_r, writes=ytok_r[c])

                stage1(0)
                for c in range(NT):
                    if c + 1 < NT:
                        stage1(c + 1)
                    stage2(c)
                if hp == 0:
                    self.tap("ytok%d" % l, ytok, ytok_r)
                if hp % 4 == 0:
                    wz_t, wz_r = self.wload(w_in_d[l], 0, 128, 8, C_Z + (hp // 4) * 512, 512)
                for t4 in range(4):
                    pb, pr = self.psum()
                    for i4 in range(4):
                        tt_ = t4 * 4 + i4
                        for k in range(8):
                            self.mm(pb[:, i4 * 128:(i4 + 1) * 128], hT[:, k, tt_ * 128:(tt_ + 1) * 128],
                                    wz_t[:, k, (hp % 4) * 128:(hp % 4 + 1) * 128], k == 0, k == 7,
                                    reads=[wz_r, hT_r[k][t4]], writes=pr)
                    self.act(zs, pb, AF.Silu, reads=pr, writes=zs_r)
                    ysl = ytok[:, t4 * 4:(t4 + 1) * 4, :]
                    z3 = zs.rearrange("p (a b) -> p a b", a=4)
                    yr_ = [ytok_r[t4 * 4 + i] for i in range(4)]
                    self.tt("dve", ysl, ysl, z3, ALU.mult, reads=[zs_r] + yr_, writes=yr_)
                    self.tt("dve", z3, ysl, ysl, ALU.mult, reads=yr_, writes=zs_r)
                    self.op("dve", lambda e, t4=t4, z3=z3: e.tensor_reduce(out=ssq_hp[:, t4 * 4:(t4 + 1) * 4], in_=z3,
                                                                           axis=AX.X, op=ALU.add),
                            reads=zs_r, writes=ssq_r)
                    self.cp("act", y2h[:, hp % 4, t4 * 4:(t4 + 1) * 4, :], ysl, reads=yr_, writes=y2h_r[hp % 4])
                if hp % 4 == 0:
                    self.cp("dve", ssq_g, ssq_hp, reads=ssq_r, writes=ssq_r)
                else:
                    self.tt("dve", ssq_g, ssq_g, ssq_hp, ALU.add, reads=ssq_r, writes=ssq_r)
                if hp % 4 == 3:
                    self.act(rstd_g, ssq_g, AF.Sqrt, reads=ssq_r, writes=ssq_r, scale=1.0 / 512, bias=EPS)
                    self.op("dve", lambda e: e.reciprocal(rstd_g, rstd_g), reads=ssq_r, writes=ssq_r)
                    for hq in range(4):
                        fc = g * 4 + hq
                        yv = y2h[:, hq, :, :]
                        self.tt("dve", yv, yv, rstd_g.unsqueeze(2).broadcast_to([128, NT, 128]), ALU.mult,
                                reads=[y2h_r[hq], ssq_r], writes=y2h_r[hq])
                        self.tt("dve", xsD, yv, gnorm[:, fc * 128:(fc + 1) * 128].unsqueeze(1).broadcast_to([128, NT, 128]),
                                ALU.mult, reads=[y2h_r[hq], gnorm_r], writes=xsD_r)
                        for t4 in range(4):
                            pb, pr = self.psum()
                            pbb = pb.bitcast(BF16)
                            for i4 in range(4):
                                tt_ = t4 * 4 + i4
                                self.tr(pbb[:, i4 * 128:(i4 + 1) * 128], xsD[:, tt_, :], ident,
                                        reads=[xsD_r, cR], writes=pr)
                            self.cp("act", ySSD[:, fc, t4 * 512:(t4 + 1) * 512], pbb[:, 0:512],
                                    reads=pr, writes=ySSD_r[fc][t4])
            self.tap("ySSD%d" % l, ySSD, ySSD_r)
            if self.stop_after == "ssd" and l == STOP_LAYER:
                break

            self.fence(ax_new, xT_r)
            merge(l, 0, 8, 128, lambda kc, ts: ySSD[:, kc, ts], lambda kc, tg: ySSD_r[kc][tg],
                  lambda ct: self.wload(w_br_d[l], 0, 128, 8, ct * 512, 512), True)
            self.tap("m0_%d" % l, xT, xT_r)
            if self.stop_after == "ssdmerge" and l == STOP_LAYER:
                break

            AY_.reset()
            AT_.reset()
            yAT = AY_.alloc([128, 4, S], BF16)
            kT = AY_.alloc([128, 2, S], BF16)
            yAT_r, kT_r = rgrid(NT), rgrid(NT)
            self.fence(self.ay_live, [yAT_r, kT_r])
            self.ay_live = [yAT_r, kT_r]
            vpad = AT_.alloc([128, NT, 2, 128], BF16)
            qTt = AT_.alloc([128, 2, 8, 128], BF16)
            qsq = AT_.alloc([128, 512], F32)
            qn = AT_.alloc([128, 512], F32)
            t1 = AT_.alloc([128, 512], F32)
            qr = AT_.alloc([128, 512], BF16)
            kr = AT_.alloc([128, 128], BF16)
            Pb = AT_.alloc([128, 4, 512], BF16)
            den = AT_.alloc([128, 512], F32)
            st8 = AT_.alloc([128, 8], F32)
            skx = AT_.alloc([128, 4], F32)
            vpad_r, qTt_r = rgrid(NT), rgrid(2)
            qw_r, P_r, den_r, st_r, skx_r, kr_r = R(), rgrid(4), R(), R(), R(), R()
            at_new = [vpad_r, qTt_r, qw_r, P_r, den_r, st_r, skx_r, kr_r]
            self.fence(self.at_live, at_new)
            self.at_live = at_new
            self.op("dve", lambda e: e.memset(vpad, 0.0), writes=vpad_r)
            self.op("dve", lambda e: e.memset(kT[64:128, :, :], 0.0), writes=kT_r)
            self.op("dve", lambda e: e.memset(qTt[64:128, :, :, :], 0.0), writes=qTt_r)
            self.act(skx, tabrp[:, RP_SINK + l * 4: RP_SINK + l * 4 + 4], AF.Exp, reads=cR, writes=skx_r)
            wq, wq_r = self.wload(w_in_d[l], 0, 128, 8, C_Q, 512)
            wkv, wkv_r = self.wload(w_in_d[l], 0, 128, 8, C_K, 256)
            for i in range(NT):
                tsl = slice(i * 128, (i + 1) * 128)
                qb = i % 2
                pq, pq_r = self.psum()
                for k in range(8):
                    self.mm(pq, hT[:, k, tsl], wq[:, k, 0:512], k == 0, k == 7, reads=[wq_r, hT_r[k][i // 4]], writes=pq_r)
                pkv, pkv_r = self.psum()
                for k in range(8):
                    self.mm(pkv[:, 0:256], hT[:, k, tsl], wkv[:, k, 0:256], k == 0, k == 7,
                            reads=[wkv_r, hT_r[k][i // 4]], writes=pkv_r)

                def normrope(src, nh, gcol, dst, dst_r, src_r):
                    w = nh * 64
                    self.act(qsq[:, 0:w], src, AF.Square, reads=src_r, writes=qw_r)
                    self.op("dve", lambda e: e.tensor_reduce(out=st8[:, 0:nh], in_=qsq[:, 0:w].rearrange("p (h d) -> p h d", d=64),
                                                              axis=AX.X, op=ALU.add), reads=qw_r, writes=st_r)
                    self.act(st8[:, 0:nh], st8[:, 0:nh], AF.Sqrt, reads=st_r, writes=st_r, scale=1.0 / 64, bias=EPS)
                    self.op("dve", lambda e: e.reciprocal(st8[:, 0:nh], st8[:, 0:nh]), reads=st_r, writes=st_r)
                    h3 = lambda a: a.rearrange("p (h d) -> p h d", d=64)
                    self.tt("dve", h3(qn[:, 0:w]), h3(src), st8[:, 0:nh].unsqueeze(2).broadcast_to([128, nh, 64]), ALU.mult,
                            reads=[src_r, st_r], writes=qw_r)
                    self.tt("dve", h3(qn[:, 0:w]), h3(qn[:, 0:w]),
                            tabrp[:, gcol + l * 64: gcol + (l + 1) * 64].unsqueeze(1).broadcast_to([128, nh, 64]), ALU.mult,
                            reads=[qw_r, cR], writes=qw_r)
                    h4 = lambda a: a.rearrange("p (h a d) -> p h a d", a=2, d=32)
                    self.tt("dve", h4(t1[:, 0:w]), h4(qn[:, 0:w]),
                            ropec[:, i, :].unsqueeze(1).unsqueeze(1).broadcast_to([128, nh, 2, 32]), ALU.mult,
                            reads=[qw_r, cR], writes=qw_r)
                    for a in range(2):
                        self.tt("dve", h4(qsq[:, 0:w])[:, :, a, :], h4(qn[:, 0:w])[:, :, 1 - a, :],
                                ropes[:, i, a, :].unsqueeze(1).broadcast_to([128, nh, 32]), ALU.mult,
                                reads=[qw_r, cR], writes=qw_r)
                    self.tt("dve", dst, t1[:, 0:w], qsq[:, 0:w], ALU.add, reads=qw_r, writes=[qw_r, dst_r])

                if ATT_LEVEL < 2:
                    continue
                normrope(pq, 8, RP_QG, qr, qw_r, pq_r)
                if ATT_LEVEL < 3:
                    continue
                pb, pr = self.psum()
                pbb = pb.bitcast(BF16)
                for hd in range(8):
                    self.tr(pbb[0:64, hd * 128:(hd + 1) * 128], qr[:, hd * 64:(hd + 1) * 64], ident, reads=[qw_r, cR], writes=pr)
                self.cp("act", qTt[0:64, qb, :, :], pbb[0:64, 0:1024].rearrange("p (a b) -> p a b", a=8),
                        reads=pr, writes=qTt_r[qb])
                if ATT_LEVEL < 4:
                    continue
                for h in range(2):
                    self.cp("act", vpad[:, i, h, h * 64:(h + 1) * 64], pkv[:, 128 + h * 64: 192 + h * 64],
                            reads=pkv_r, writes=vpad_r[i])
                normrope(pkv[:, 0:128], 2, RP_KG, kr, kr_r, pkv_r)
                pb, pr = self.psum()
                pbb = pb.bitcast(BF16)
                for h in range(2):
                    self.tr(pbb[0:64, h * 128:(h + 1) * 128], kr[:, h * 64:(h + 1) * 64], ident, reads=[kr_r, cR], writes=pr)
                self.cp("act", kT[0:64, :, tsl], pbb[0:64, 0:256].rearrange("p (a b) -> p a b", a=2), reads=pr, writes=kT_r[i])
                if ATT_LEVEL < 5:
                    continue
                blks = [i] if i == 0 else [i - 1, i]
                pidx = []
                for h in range(2):
                    for blk in blks:
                        bs = slice(blk * 128, (blk + 1) * 128)
                        pi = len(pidx)
                        pidx.append((h, blk, pi))
                        ps_, ps_r = self.psum()
                        self.mm(ps_.rearrange("p (a b) -> p a b", a=4), kT[:, h, bs], qTt[:, qb, 4 * h:4 * h + 4, :],
                                True, True, reads=[kT_r[blk], qTt_r[qb]], writes=ps_r)
                        self.act(Pb[:, pi, :], ps_, AF.Exp, reads=ps_r, writes=P_r[pi], scale=0.125)
                        self.tt("dve", Pb[:, pi, :], Pb[:, pi, :], mcur if blk == i else mprev, ALU.mult,
                                reads=[P_r[pi], cR], writes=P_r[pi])
                if ATT_LEVEL < 6:
                    continue
                pnum, pnum_r = self.psum()
                pden, pden_r = self.psum()
                n = len(pidx)
                for q_, (h, blk, pi) in enumerate(pidx):
                    self.mm(pnum, vpad[:, blk, h, :], Pb[:, pi, :], q_ == 0, q_ == n - 1,
                            reads=[vpad_r[blk], P_r[pi]], writes=pnum_r)
                for q_, (h, blk, pi) in enumerate(pidx):
                    self.mm(pden, onesLR[h], Pb[:, pi, :], q_ == 0, q_ == n - 1, reads=[cR, P_r[pi]], writes=pden_r)
                d3 = lambda a: a.rearrange("p (r q) -> p r q", r=4)
                self.tt("dve", d3(den), d3(pden), skx.unsqueeze(2).broadcast_to([128, 4, 128]), ALU.add,
                        reads=[pden_r, skx_r], writes=den_r)
                self.op("dve", lambda e: e.reciprocal(den, den), reads=den_r, writes=den_r)
                self.tt("dve", yAT[:, :, tsl], d3(pnum), d3(den), ALU.mult, reads=[pnum_r, den_r], writes=yAT_r[i])
            self.tap("yAT%d" % l, yAT, yAT_r)
            if self.stop_after == "attn" and l == STOP_LAYER:
                break

            def wload_attn(ct):
                i_ = self.w_i
                self.w_i = (i_ + 1) % NW
                buf, res = self.wbufs[i_], self.wres[i_]
                for h in range(2):
                    src = w_br_d[l][1024 + h * 256:1024 + (h + 1) * 256, ct * 512:(ct + 1) * 512].rearrange(
                        "(r d) c -> d r c", d=64)
                    dst = buf[h * 64:(h + 1) * 64, 0:4, :]
                    self.op("pool", lambda e, dst=dst, src=src: e.dma_start(out=dst, in_=src), writes=res, dma=True)
                return buf, res

            merge(l, 1, 4, 128, lambda kc, ts: yAT[:, kc, ts], lambda kc, tg: [yAT_r[tg * 4 + i] for i in range(4)],
                  wload_attn, False)
            self.tap("m1_%d" % l, xT, xT_r)
            if self.stop_after == "attnmerge" and l == STOP_LAYER:
                break

            AY_.reset()
            AT_.reset()
            yPT = AY_.alloc([128, 4, S], BF16)
            yPT_r = rgrid(4, NG)
            self.fence(self.ay_live, yPT_r)
            self.ay_live = [yPT_r]
            PADW = 8
            pa = AT_.alloc([128, PADW + S], F32)
            pbuf = AT_.alloc([128, PADW + S], F32)
            ub = AT_.alloc([128, 16 + S], F32)
            pooled = AT_.alloc([128, S], BF16)
            inv16 = AT_.alloc([128, 16], F32)
            pa_r, pb_r, ub_r, pooled_r, inv_r = R(), R(), R(), rgrid(NG), R()
            at_new = [pa_r, pb_r, ub_r, pooled_r, inv_r]
            self.fence(self.at_live, at_new)
            self.at_live = at_new
            self.op("dve", lambda e: e.memset(pa[:, 0:PADW], 0.0), writes=pa_r)
            self.op("dve", lambda e: e.memset(pbuf[:, 0:PADW], 0.0), writes=pb_r)
            self.op("dve", lambda e: e.memset(ub[:, 0:16], 0.0), writes=ub_r)
            pcs, pcs_r = self.psum()
            self.mm(pcs[:, 0:128], ones_f, U_f, True, True, reads=cR, writes=pcs_r)
            self.op("dve", lambda e, pcs=pcs, inv16=inv16: e.reciprocal(inv16, pcs[:, 0:16]), reads=pcs_r, writes=inv_r)
            wpl, wpl_r = self.wload(w_in_d[l], 0, 128, 8, C_POOL, 512)
            for gi in range(4):
                w_ = (2, 4, 8, 16)[gi]
                for tg in range(NG):
                    ts = slice(tg * 512, (tg + 1) * 512)
                    pb_, pr = self.psum()
                    for k in range(8):
                        self.mm(pb_, wpl[:, k, gi * 128:(gi + 1) * 128], hT[:, k, ts], k == 0, k == 7,
                                reads=[wpl_r, hT_r[k][tg]], writes=pr)
                    self.cp("act", ub[:, 16 + tg * 512: 16 + (tg + 1) * 512], pb_, reads=pr, writes=ub_r)
                u_ = ub[:, 16:16 + S]
                src, src_r, sh = ub, ub_r, 1
                srcoff = 16
                bufs = [(pa, pa_r), (pbuf, pb_r)]
                bi = 0
                while sh < w_:
                    dstb, dst_r = bufs[bi]
                    bi ^= 1
                    self.tt("dve", dstb[:, PADW:PADW + S], src[:, srcoff:srcoff + S], src[:, srcoff - sh:srcoff - sh + S],
                            ALU.add, reads=src_r, writes=dst_r)
                    src, src_r, srcoff = dstb, dst_r, PADW
                    sh *= 2
                sw = src[:, srcoff:srcoff + S]
                for tg in range(NG):
                    ts = slice(tg * 512, (tg + 1) * 512)
                    self.stt(pooled[:, ts], sw[:, ts], 1.0 / w_, u_[:, ts], ALU.mult, ALU.subtract,
                             reads=[src_r, ub_r], writes=pooled_r[tg])
                self.tt("dve", pa[:, 0:w_ - 1], sw[:, 0:w_ - 1], inv16[:, 0:w_ - 1], ALU.mult,
                        reads=[src_r, inv_r], writes=pa_r)
                self.tt("dve", pooled[:, 0:w_ - 1], pa[:, 0:w_ - 1], u_[:, 0:w_ - 1], ALU.subtract,
                        reads=[pa_r, ub_r, pooled_r[0]], writes=pooled_r[0])
                self.op("dve", lambda e: e.memset(pa[:, 0:PADW], 0.0), reads=pa_r, writes=pa_r)
                for tg in range(NG):
                    ts = slice(tg * 512, (tg + 1) * 512)
                    pb_, pr = self.psum()
                    self.mm(pb_, poolw[:, l * 4 + gi, :], pooled[:, ts], True, True, reads=[cR, pooled_r[tg]], writes=pr)
                    self.act(yPT[:, gi, ts], pb_, AF.Identity, reads=[pr, cR], writes=yPT_r[gi][tg],
                             scale=fmcol(FM_PSC, l * 4 + gi))
            self.tap("yPT%d" % l, yPT, yPT_r)
            merge(l, 2, 4, 128, lambda kc, ts: yPT[:, kc, ts], lambda kc, tg: yPT_r[kc][tg],
                  lambda ct: self.wload(w_br_d[l], 1536, 128, 4, ct * 512, 512), False)
            self.tap("m2_%d" % l, xT, xT_r)
            if self.stop_after == "pool" and l == STOP_LAYER:
                break

            AY_.reset()
            AT_.reset()
            yCT = AY_.alloc([128, 4, S], BF16)
            cv = AY_.alloc([128, 4, S], BF16)
            yCT_r, cv_r = rgrid(4, NG), rgrid(4, NG)
            self.fence(self.ay_live, [yCT_r, cv_r])
            self.ay_live = [yCT_r, cv_r]
            up31 = AT_.alloc([128, 30 + S], BF16)
            diag31 = AT_.alloc([128, 31, 128], BF16)
            sg = AT_.alloc([128, 512], F32)
            mean = AT_.alloc([128, 512], F32)
            rstd2 = AT_.alloc([128, 512], F32)
            tn = AT_.alloc([128, 512], F32)
            cvsq = AT_.alloc([128, 512], BF16)
            up_r, dg_r, sg_r, mean_r, rs_r, tn_r, cvsq_r = R(), R(), R(), R(), R(), R(), R()
            at_new = [up_r, dg_r, sg_r, mean_r, rs_r, tn_r, cvsq_r]
            self.fence(self.at_live, at_new)
            self.at_live = at_new
            self.op("dve", lambda e: e.memset(up31[:, 0:30], 0.0), writes=up_r)
            for half in range(2):
                wa, wa_r = self.wload(w_in_d[l], 0, 128, 8, C_CONV + half * 256, 256)
                wg_, wg_r_ = self.wload(w_in_d[l], 0, 128, 8, C_CONV + 512 + half * 256, 256)
                for jj in range(2):
                    j = half * 2 + jj
                    cs2 = slice(jj * 128, (jj + 1) * 128)
                    for tg in range(NG):
                        ts = slice(tg * 512, (tg + 1) * 512)
                        pa_, par = self.psum()
                        pg_, pgr = self.psum()
                        for k in range(8):
                            self.mm(pg_, wg_[:, k, cs2], hT[:, k, ts], k == 0, k == 7, reads=[wg_r_, hT_r[k][tg]], writes=pgr)
                        for k in range(8):
                            self.mm(pa_, wa[:, k, cs2], hT[:, k, ts], k == 0, k == 7, reads=[wa_r, hT_r[k][tg]], writes=par)
                        self.act(sg, pg_, AF.Sigmoid, reads=pgr, writes=sg_r)
                        self.tt("dve", up31[:, 30 + tg * 512: 30 + (tg + 1) * 512], pa_, sg, ALU.mult,
                                reads=[par, sg_r], writes=up_r)
                    self.tt("dve", diag31, ident.unsqueeze(1).broadcast_to([128, 31, 128]),
                            tabfm[:, FM_DWW + (l * 4 + j) * 31: FM_DWW + (l * 4 + j + 1) * 31].unsqueeze(2).broadcast_to([128, 31, 128]),
                            ALU.mult, reads=cR, writes=dg_r)
                    for tg in range(NG):
                        ts = slice(tg * 512, (tg + 1) * 512)
                        pb_, pr = self.psum()
                        for tap_ in range(31):
                            self.mm(pb_, diag31[:, tap_, :], up31[:, tg * 512 + tap_: tg * 512 + tap_ + 512],
                                    tap_ == 0, tap_ == 30, reads=[dg_r, up_r], writes=pr)
                        self.act(cv[:, j, ts], pb_, AF.Identity, reads=[pr, cR], writes=cv_r[j][tg],
                                 bias=fmcol(FM_DWB, l * 4 + j))
            for tg in range(NG):
                ts = slice(tg * 512, (tg + 1) * 512)
                pm, pm_r = self.psum()
                pq2, pq2_r = self.psum()
                for j in range(4):
                    self.mm(pm, ones_bf, cv[:, j, ts], j == 0, j == 3, reads=[cR, cv_r[j][tg]], writes=pm_r)
                for j in range(4):
                    self.act(cvsq, cv[:, j, ts], AF.Square, reads=cv_r[j][tg], writes=cvsq_r)
                    self.mm(pq2, ones_bf, cvsq, j == 0, j == 3, reads=[cR, cvsq_r], writes=pq2_r)
                self.act(mean, pm, AF.Identity, reads=pm_r, writes=mean_r, scale=1.0 / 512)
                self.tt("dve", tn, mean, mean, ALU.mult, reads=mean_r, writes=tn_r)
                self.stt(rstd2, pq2, 1.0 / 512, tn, ALU.mult, ALU.subtract, reads=[pq2_r, tn_r], writes=rs_r)
                self.act(rstd2, rstd2, AF.Sqrt, reads=rs_r, writes=rs_r, bias=EPS)
                self.op("dve", lambda e: e.reciprocal(rstd2, rstd2), reads=rs_r, writes=rs_r)
                for j in range(4):
                    self.tt("dve", tn, cv[:, j, ts], mean, ALU.subtract, reads=[cv_r[j][tg], mean_r], writes=tn_r)
                    self.tt("dve", tn, tn, rstd2, ALU.mult, reads=[tn_r, rs_r], writes=tn_r)
                    self.act(yCT[:, j, ts], tn, AF.Silu, reads=[tn_r, cR], writes=yCT_r[j][tg],
                             scale=fmcol(FM_LNG, l * 4 + j), bias=fmcol(FM_LNB, l * 4 + j))
            self.tap("yCT%d" % l, yCT, yCT_r)
            merge(l, 3, 4, 128, lambda kc, ts: yCT[:, kc, ts], lambda kc, tg: yCT_r[kc][tg],
                  lambda ct: self.wload(w_br_d[l], 2048, 128, 4, ct * 512, 512), False)
            self.tap("m3_%d" % l, xT, xT_r)
            if self.stop_after == "conv" and l == STOP_LAYER:
                break

            for k in range(8):
                for tg in range(NG):
                    ts = slice(tg * 512, (tg + 1) * 512)
                    self.cp("act" if (k + tg) % 2 else "dve", hT[:, k, ts], xT[:, k, ts],
                            reads=xT_r[k][tg], writes=hT_r[k][tg])
            for k in range(8):
                self.op("sp", lambda e, k=k, x_src=x_src: e.dma_start(out=xT[:, k, :], in_=x_src[:, k, :]),
                        reads=xs_r[k], writes=xT_r[k], dma=True)
            for ct in range(2):
                wo, wo_r = self.wload(w_out_d[l], 0, 128, 8, ct * 512, 512)
                for f4 in range(4):
                    fo = ct * 4 + f4
                    for tg in range(NG):
                        ts = slice(tg * 512, (tg + 1) * 512)
                        pb_, pr = self.psum()
                        for k in range(8):
                            self.mm(pb_, wo[:, k, f4 * 128:(f4 + 1) * 128], hT[:, k, ts], k == 0, k == 7,
                                    reads=[wo_r, hT_r[k][tg]], writes=pr)
                        self.tt("dve", xT[:, fo, ts], xT[:, fo, ts], pb_, ALU.add, reads=[pr, xT_r[fo][tg]],
                                writes=xT_r[fo][tg])
            self.tap("xmid%d" % l, xT, xT_r)
            if self.stop_after == "wout" and l == STOP_LAYER:
                break

            rmsnorm(FM_GMLP + l * 8)
            AY_.reset()
            AT_.reset()
            aT = AY_.alloc([128, 8, S], BF16)
            aT_r = rgrid(8, NG)
            self.fence(self.ay_live, aT_r)
            self.ay_live = [aT_r]
            rl = [AT_.alloc([128, 512], F32) for _ in range(2)]
            rl_r = [R(), R()]
            self.fence(self.at_live, rl_r)
            self.at_live = rl_r
            for hb in range(4):
                for ct in range(2):
                    wu, wu_r = self.wload(w_up_d[l], 0, 128, 8, hb * 1024 + ct * 512, 512)
                    for f4 in range(4):
                        fc = ct * 4 + f4
                        for tg in range(NG):
                            ts = slice(tg * 512, (tg + 1) * 512)
                            pb_, pr = self.psum()
                            for k in range(8):
                                self.mm(pb_, wu[:, k, f4 * 128:(f4 + 1) * 128], hT[:, k, ts], k == 0, k == 7,
                                        reads=[wu_r, hT_r[k][tg]], writes=pr)
                            ri = (fc * NG + tg) % 2
                            self.act(rl[ri], pb_, AF.Relu, reads=pr, writes=rl_r[ri])
                            self.tt("dve", aT[:, fc, ts], rl[ri], rl[ri], ALU.mult, reads=rl_r[ri], writes=aT_r[fc][tg])
                for ct in range(2):
                    wd, wd_r = self.wload(w_dn_d[l], hb * 1024, 128, 8, ct * 512, 512)
                    for f4 in range(4):
                        fo = ct * 4 + f4
                        for tg in range(NG):
                            ts = slice(tg * 512, (tg + 1) * 512)
                            pb_, pr = self.psum()
                            for k in range(8):
                                self.mm(pb_, wd[:, k, f4 * 128:(f4 + 1) * 128], aT[:, k, ts], k == 0, k == 7,
                                        reads=[wd_r, aT_r[k][tg]], writes=pr)
                            self.tt("dve", xT[:, fo, ts], xT[:, fo, ts], pb_, ALU.add, reads=[pr, xT_r[fo][tg]],
                                    writes=xT_r[fo][tg])
            self.tap("xout%d" % l, xT, xT_r)

        out_v = out_d.rearrange("(k p) t -> p k t", p=128)
        for k in range(8):
            self.op("sp", lambda e, k=k: e.dma_start(out=out_v[:, k, :], in_=xT[:, k, :]), reads=xT_r[k], dma=True)
        sc.finalize()
        fw = [(cs, sc.counts[cs]) for cs in ("d_sp", "d_pool", "d_act") if sc.counts.get(cs, 0)]
        sems = {}
        for cs, n in sc.counts.items():
            sg = SEG_DMA if cs.startswith("d_") else SEG
            sems[cs] = [self.es.enter_context(nc.semaphore("s_%s_%d" % (cs, i))) for i in range((n + sg - 1) // sg)]
        with nc.Block() as block:
            sc.emit(nc, block, sems, fw)


def build_program(debug=None, nlayers=DEPTH, stop_after=None):
    b = Builder(debug=debug, nlayers=nlayers, stop_after=stop_after)
    nc = b.build()
    return nc, b


def _fm(v, nchunks):
    v = np.asarray(v, np.float32)
    return np.ascontiguousarray(v.reshape(L, nchunks, 128).transpose(2, 0, 1).reshape(128, L * nchunks))


def _rep(v):
    v = np.asarray(v, np.float32).reshape(1, -1)
    return np.ascontiguousarray(np.repeat(v, 128, axis=0))


def make_in_maps(inputs, ncores=8):
    import ml_dtypes
    f = lambda k: np.asarray(inputs[k], np.float32)
    x = f("x")
    cw = f("ssd_conv_w")
    cw_t = cw.reshape(L, 4, 12, 128).transpose(3, 0, 2, 1).reshape(128, L * 48)
    dww = f("conv_dw_w")
    dww_t = dww.reshape(L, 31, 4, 128).transpose(3, 0, 2, 1).reshape(128, L * 124)
    tab_fm = np.concatenate([
        _fm(f("norm_mix_g"), 8), _fm(f("norm_mlp_g"), 8), _fm(f("gate_b"), 32), cw_t,
        _fm(f("ssd_conv_b"), 12), _fm(f("pool_scale"), 4), dww_t, _fm(f("conv_dw_b"), 4),
        _fm(f("conv_ln_g"), 4), _fm(f("conv_ln_b"), 4)], axis=1).astype(np.float32)
    assert tab_fm.shape == (128, NFM), tab_fm.shape
    sk = f("attn_sinks")
    sk_t = np.zeros((128, L * 4), np.float32)
    for l in range(L):
        sk_t[0:64, l * 4:(l + 1) * 4] = sk[l, 0:4][None, :]
        sk_t[64:128, l * 4:(l + 1) * 4] = sk[l, 4:8][None, :]
    tab_rp = np.concatenate([_rep(f("ssd_dt_bias")), _rep(f("ssd_a_log")), _rep(f("ssd_d")),
                             _rep(f("q_norm_g")), _rep(f("k_norm_g")), sk_t], axis=1).astype(np.float32)
    assert tab_rp.shape == (128, NRP), tab_rp.shape
    gnorm_rep = np.ascontiguousarray(np.repeat(f("ssd_norm_g")[:, None, :], 128, axis=1))
    inv = (1.0 / (10000.0 ** (np.arange(0, 64, 2, dtype=np.float32) / np.float32(64.0)))).astype(np.float32)
    ang = (np.arange(S, dtype=np.float32)[:, None] * inv[None, :]).astype(np.float32)
    cos = np.cos(ang).astype(np.float32).reshape(NT, 128, 32).transpose(1, 0, 2)
    sin = np.sin(ang).astype(np.float32).reshape(NT, 128, 32).transpose(1, 0, 2)
    rope_c = np.ascontiguousarray(cos.reshape(128, NT * 32))
    rope_s2 = np.ascontiguousarray(np.stack([-sin, sin], axis=2).reshape(128, NT * 64))
    k = np.arange(128)
    U = (k[:, None] <= k[None, :]).astype(np.float32)
    Lm = (k[:, None] > k[None, :]).astype(np.float32)
    c_f32 = np.concatenate([U, Lm, np.ones((128, 128), np.float32)], axis=1)
    mcur = np.tile(U, (1, 4))
    mprev = np.tile(Lm, (1, 4))
    onl = np.zeros((128, 128), np.float32)
    onl[:, 0:64] = 1.0
    onr = np.zeros((128, 128), np.float32)
    onr[:, 64:128] = 1.0
    c_bf = np.concatenate([np.eye(128, dtype=np.float32), U, mcur, mprev, onl, onr], axis=1).astype(ml_dtypes.bfloat16)
    assert c_bf.shape == (128, NCB)
    pw = f("pool_w")
    pool_w = np.ascontiguousarray(pw.transpose(2, 0, 1, 3).reshape(128, L * 4 * 128))
    shared = {
        "w_in": f("w_in"), "w_branch": f("w_branch"), "w_out": f("w_out"),
        "w_up": f("w_mlp_up"), "w_down": f("w_mlp_down"),
        "tab_fm": tab_fm, "tab_rp": tab_rp, "gnorm_rep": gnorm_rep,
        "rope_c": rope_c, "rope_s2": rope_s2, "c_f32": c_f32, "c_bf": c_bf, "pool_w": pool_w,
    }
    maps = []
    for c in range(ncores):
        m = dict(shared)
        m["xT"] = np.ascontiguousarray(x[c].T)
        maps.append(m)
    return maps


def kernel(**inputs):
    nc, b = build_program()
    in_maps = make_in_maps(inputs)
    res = run_bass_kernel_spmd(nc, in_maps, core_ids=list(range(8)))
    out = np.stack([np.ascontiguousarray(r["yT"].T) for r in res.results], axis=0)
    return out.astype(np.float32)
```

```python
import numpy as np
from contextlib import ExitStack
import concourse.bass as bass
import concourse.mybir as mybir
from concourse.bass_utils import run_bass_kernel_spmd

F32 = mybir.dt.float32
BF16 = mybir.dt.bfloat16
ALU = mybir.AluOpType
AF = mybir.ActivationFunctionType
AX = mybir.AxisListType

D = 1024
S = 2048
DEPTH = 2
NT = S // 128
NG = S // 512
EPS = 1e-6
IN_WIDTH = 8976
C_Z, C_XS, C_B, C_C, C_DT, C_Q, C_K, C_V, C_POOL, C_CONV, C_GATE = (
    0, 1024, 2048, 2304, 2560, 2576, 3088, 3216, 3344, 3856, 4880)


class R:
    __slots__ = ("w", "rd")

    def __init__(self):
        self.w = None
        self.rd = {}


def rgrid(*shape):
    if len(shape) == 1:
        return [R() for _ in range(shape[0])]
    return [rgrid(*shape[1:]) for _ in range(shape[0])]


def flat(x):
    if isinstance(x, R):
        return [x]
    out = []
    for e in x:
        out.extend(flat(e))
    return out


import sys


def _where():
    f = sys._getframe(2)
    out = []
    while f is not None and len(out) < 4:
        out.append(f.f_lineno)
        f = f.f_back
    return out


class Op:
    __slots__ = ("eng", "cs", "fn", "deps", "sig", "cnt", "waits", "snap", "where")


ENGS = ("pe", "act", "dve", "pool", "sp")


class Sched:
    def __init__(self):
        self.ops = []

    def add(self, eng, fn, reads=(), writes=(), dma=False):
        op = Op()
        op.eng = eng
        op.cs = ("d_" + eng) if dma else eng
        op.fn = fn
        op.where = _where()
        op.sig = dma
        op.cnt = 0
        deps = {}
        rl = flat(reads)
        wl = flat(writes)
        for r in rl:
            if r.w is not None:
                deps[id(r.w)] = r.w
        for w in wl:
            if w.w is not None:
                deps[id(w.w)] = w.w
            for o in w.rd.values():
                deps[id(o)] = o
        if eng == "pe" and not dma:
            deps = {i: o for i, o in deps.items() if o.cs != "pe"}
        op.deps = list(deps.values())
        for r in rl:
            r.rd[op.cs] = op
        for w in wl:
            w.w = op
            w.rd = {}
        self.ops.append(op)
        return op

    def finalize(self):
        for op in self.ops:
            for d in op.deps:
                d.sig = True
        counts = {}
        for op in self.ops:
            if op.sig:
                counts[op.cs] = counts.get(op.cs, 0) + 1
            op.cnt = counts.get(op.cs, 0)
        known = {e: {} for e in ENGS}
        for op in self.ops:
            kn = known[op.eng]
            need = {}
            for d in op.deps:
                if d.cnt > need.get(d.cs, 0):
                    need[d.cs] = d.cnt
            waits = []
            for d in sorted(op.deps, key=lambda o: -o.cnt):
                if kn.get(d.cs, 0) >= d.cnt:
                    continue
                if need.get(d.cs, 0) != d.cnt:
                    continue
                waits.append((d.cs, d.cnt))
                for c, v in d.snap.items():
                    if kn.get(c, 0) < v:
                        kn[c] = v
                kn[d.cs] = max(kn.get(d.cs, 0), d.cnt)
            op.waits = waits
            if op.sig:
                sn = dict(kn)
                op.snap = sn
            else:
                op.snap = None
        self.counts = counts

    def emit(self, nc, block, sems, final_waits=()):
        by = {e: [] for e in ENGS}
        for op in self.ops:
            by[op.eng].append(op)

        def seg(cs, cnt):
            sg = SEG_DMA if cs.startswith("d_") else SEG
            inc = 16 if cs.startswith("d_") else 1
            i = (cnt - 1) // sg
            return sems[cs][i], (cnt - i * sg) * inc

        def run(e, ops):
            segdone = {}
            for op in ops:
                for cs, cnt in op.waits:
                    if cs.startswith("d_"):
                        s_idx = (cnt - 1) // SEG_DMA
                        for k in range(segdone.get(cs, 0), s_idx):
                            e.wait_ge(sems[cs][k], SEG_DMA * 16)
                        segdone[cs] = max(segdone.get(cs, 0), s_idx)
                    sm, val = seg(cs, cnt)
                    e.wait_ge(sm, val)
                if op.cs.startswith("d_") and op.cnt > MAX_DMA_OUT:
                    sm, val = seg(op.cs, op.cnt - MAX_DMA_OUT)
                    e.wait_ge(sm, val)
                try:
                    ins = op.fn(e)
                except Exception:
                    print("FAILED OP at lines", op.where, op.eng)
                    raise
                if op.sig:
                    inc = 16 if op.cs.startswith("d_") else 1
                    sm, _ = seg(op.cs, op.cnt)
                    ins.then_inc(sm, inc)

        @block.tensor
        def _(e):
            run(e, by["pe"])

        @block.scalar
        def _(e):
            run(e, by["act"])

        @block.vector
        def _(e):
            run(e, by["dve"])

        @block.gpsimd
        def _(e):
            run(e, by["pool"])

        @block.sync
        def _(e):
            run(e, by["sp"])
            for cs, n in final_waits:
                k = 0
                while k * SEG_DMA < n:
                    last = min(n, (k + 1) * SEG_DMA)
                    sm, val = seg(cs, last)
                    e.wait_ge(sm, val)
                    k += 1


import os as _os
SEG = int(_os.environ.get('SEG', '2000'))
SEG_DMA = int(_os.environ.get('SEG_DMA', '1000000'))
MAX_DMA_OUT = int(_os.environ.get('MAX_DMA_OUT', '4'))


import os
ATT_LEVEL = int(os.environ.get('ATT_LEVEL', '9'))
STOP_LAYER = int(os.environ.get('STOP_LAYER', '0'))
NW = 3
L = DEPTH
FM_GMIX, FM_GMLP, FM_GATEB, FM_CW, FM_CB, FM_PSC, FM_DWW, FM_DWB, FM_LNG, FM_LNB = (
    0, L * 8, L * 16, L * 48, L * 96, L * 108, L * 112, L * 236, L * 240, L * 244)
NFM = L * 248
RP_DTB, RP_ALOG, RP_D, RP_QG, RP_KG, RP_SINK = 0, L * 16, L * 32, L * 48, L * 112, L * 176
NRP = L * 180
CB_ID, CB_UT, CB_MCUR, CB_MPREV, CB_ONL, CB_ONR = 0, 128, 256, 768, 1280, 1408
NCB = 1536


class Arena:
    def __init__(self, b, name, nbytes):
        self.t = b.sb(name, [128, nbytes // 4], F32)
        self.n = nbytes
        self.off = 0

    def reset(self):
        self.off = 0

    def alloc(self, shape, dt):
        es = 4 if dt == F32 else 2
        n = 1
        for d in shape[1:]:
            n *= d
        nb = (n * es + 31) // 32 * 32
        off = self.off
        self.off += nb
        assert self.off <= self.n, (self.off, self.n)
        ap = self.t[:, off // 4:(off + nb) // 4]
        if dt != F32:
            ap = ap.bitcast(dt)
        ap = ap[:, :n]
        if len(shape) == 3:
            ap = ap.rearrange("p (a b) -> p a b", a=shape[1])
        elif len(shape) == 4:
            ap = ap.rearrange("p (a b c) -> p a b c", a=shape[1], b=shape[2])
        if shape[0] < 128:
            ap = ap[:shape[0]]
        return ap


class Builder:
    def __init__(self, debug=None, nlayers=DEPTH, stop_after=None):
        self.debug = debug or []
        self.nlayers = nlayers
        self.stop_after = stop_after
        self.nc = bass.Bass("TRN2", target_bir_lowering=False)
        self.sc = Sched()
        self.es = ExitStack()
        self.psum_i = 0
        self.w_i = 0
        self.dbg_d = {}

    def dram_in(self, name, shape, dtype=F32):
        return self.nc.dram_tensor(name, list(shape), dtype, kind="ExternalInput").ap()

    def sb(self, name, shape, dtype=F32):
        return self.es.enter_context(self.nc.sbuf_tensor(name, list(shape), dtype))[:]

    def op(self, eng, fn, reads=(), writes=(), dma=False):
        return self.sc.add(eng, fn, reads, writes, dma)

    def psum(self):
        i = self.psum_i
        self.psum_i = (i + 1) % 8
        return self.pbanks[i], self.pres[i]

    def mm(self, out, lhsT, rhs, start, stop, reads, writes):
        return self.op("pe", lambda e: e.matmul(out, lhsT, rhs, start=start, stop=stop), reads, writes)

    def tr(self, out, in_, ident, reads, writes):
        return self.op("pe", lambda e: e.transpose(out, in_, ident), reads, writes)

    def act(self, out, in_, func, reads, writes, bias=None, scale=None):
        kw = {}
        if bias is not None:
            kw["bias"] = bias
        if scale is not None:
            kw["scale"] = scale
        return self.op("act", lambda e: e.activation(out=out, in_=in_, func=func, **kw), reads, writes)

    def tt(self, eng, out, in0, in1, op, reads, writes):
        return self.op(eng, lambda e: e.tensor_tensor(out=out, in0=in0, in1=in1, op=op), reads, writes)

    def tsc(self, eng, out, in0, s1, s2, op0, op1, reads, writes):
        if s2 is None:
            return self.op(eng, lambda e: e.tensor_scalar(out, in0, s1, None, op0), reads, writes)
        return self.op(eng, lambda e: e.tensor_scalar(out, in0, s1, s2, op0, op1), reads, writes)

    def stt(self, out, in0, scalar, in1, op0, op1, reads, writes):
        return self.op("dve", lambda e: e.scalar_tensor_tensor(out=out, in0=in0, scalar=scalar, in1=in1,
                                                              op0=op0, op1=op1), reads, writes)

    def cp(self, eng, out, in_, reads, writes):
        if eng == "act":
            return self.op("act", lambda e: e.activation(out=out, in_=in_, func=AF.Copy), reads, writes)
        return self.op(eng, lambda e: e.tensor_copy(out=out, in_=in_), reads, writes)

    def fence(self, old, new):
        d = self.dummy
        self.op("dve", lambda e: e.memset(d, 0.0), writes=[old, new])

    def tap(self, name, ap, res):
        for n, shape, dt in self.debug:
            if n == name:
                dst = self.dbg_d[name]
                self.op("sp", lambda e: e.dma_start(out=dst, in_=ap), reads=res, dma=True)

    def wload(self, src2d, r0, kp, nk, c0, ncols):
        i = self.w_i
        self.w_i = (i + 1) % NW
        buf, res = self.wbufs[i], self.wres[i]
        src = src2d[r0:r0 + kp * nk, c0:c0 + ncols].rearrange("(k p) c -> p k c", p=kp)
        dst = buf[:kp, :nk, :ncols]
        self.op("pool", lambda e: e.dma_start(out=dst, in_=src), writes=res, dma=True)
        return buf, res

    def build(self):
        with self.es:
            self._build()
        return self.nc

    def _build(self):
        nc = self.nc
        sc = self.sc
        xT_d = self.dram_in("xT", [D, S])
        w_in_d = self.dram_in("w_in", [L, D, IN_WIDTH])
        w_br_d = self.dram_in("w_branch", [L, 2560, D])
        w_out_d = self.dram_in("w_out", [L, D, D])
        w_up_d = self.dram_in("w_up", [L, D, 4096])
        w_dn_d = self.dram_in("w_down", [L, 4096, D])
        tabfm_d = self.dram_in("tab_fm", [128, NFM])
        tabrp_d = self.dram_in("tab_rp", [128, NRP])
        gnorm_d = self.dram_in("gnorm_rep", [L, 128, 1024])
        ropec_d = self.dram_in("rope_c", [128, 16 * 32])
        ropes_d = self.dram_in("rope_s2", [128, 16 * 64])
        cf32_d = self.dram_in("c_f32", [128, 384])
        cbf_d = self.dram_in("c_bf", [128, NCB], BF16)
        poolw_d = self.dram_in("pool_w", [128, L * 4 * 128])
        out_d = nc.dram_tensor("yT", [D, S], F32, kind="ExternalOutput").ap()
        xs_d = nc.dram_tensor("x_spill", [D, S], F32, kind="ExternalOutput").ap()
        for name, shape, dt in self.debug:
            self.dbg_d[name] = nc.dram_tensor("dbg_" + name, list(shape), dt, kind="ExternalOutput").ap()

        AX_ = Arena(self, "arena_x", 65536)
        AH_ = Arena(self, "arena_h", 32768)
        AY_ = Arena(self, "arena_y", 32768)
        AT_ = Arena(self, "arena_t", 30720)
        xT = AX_.alloc([128, 8, S], F32)
        xT_r = rgrid(8, NG)
        hT = AH_.alloc([128, 8, S], BF16)
        hT_r = rgrid(8, NG)
        self.wbufs = [self.sb("wbuf%d" % i, [128, 8, 512], BF16) for i in range(NW)]
        self.wres = [R() for _ in range(NW)]
        tabfm = self.sb("tabfm", [128, NFM])
        tabrp = self.sb("tabrp", [128, NRP])
        gnorm = self.sb("gnorm", [128, 1024])
        gnorm_r = R()
        ropec = self.sb("ropec", [128, 16, 32])
        ropes = self.sb("ropes", [128, 16, 2, 32])
        cf32 = self.sb("cf32", [128, 384])
        cbf = self.sb("cbf", [128, NCB], BF16)
        poolw = self.sb("poolw", [128, L * 4, 128], BF16)
        gts = [self.sb("gt%d" % i, [128, 512]) for i in range(2)]
        gts_r = [R(), R()]
        self.dummy = self.sb("fence_dummy", [128, 8])
        cR = R()
        U_f = cf32[:, 0:128]
        L_f = cf32[:, 128:256]
        ones_f = cf32[:, 256:384]
        ident = cbf[:, CB_ID:CB_ID + 128]
        UT_b = cbf[:, CB_UT:CB_UT + 128]
        mcur = cbf[:, CB_MCUR:CB_MCUR + 512]
        mprev = cbf[:, CB_MPREV:CB_MPREV + 512]
        onesLR = [cbf[:, CB_ONL:CB_ONL + 128], cbf[:, CB_ONR:CB_ONR + 128]]
        ones_bf = self.sb("ones_bf", [128, 128], BF16)

        self.pbanks = [self.es.enter_context(nc.psum_tensor("ps%d" % i, [128, 512], F32))[:] for i in range(8)]
        self.pres = [R() for _ in range(8)]
        def fmcol(base, idx):
            return tabfm[:, base + idx: base + idx + 1]

        xT_dv = xT_d.rearrange("(k p) t -> p k t", p=128)
        for k in range(8):
            self.op("sp", lambda e, k=k: e.dma_start(out=xT[:, k, :], in_=xT_dv[:, k, :]), writes=xT_r[k], dma=True)
        for dst, src in ((tabfm, tabfm_d), (tabrp, tabrp_d), (cf32, cf32_d), (cbf, cbf_d),
                         (ropec, ropec_d.rearrange("p (t c) -> p t c", c=32)),
                         (ropes, ropes_d.rearrange("p (t a c) -> p t a c", a=2, c=32))):
            self.op("sp", lambda e, dst=dst, src=src: e.dma_start(out=dst, in_=src), writes=cR, dma=True)
        self.op("pool", lambda e: e.dma_start(out=poolw, in_=poolw_d.rearrange("p (g d) -> p g d", d=128)),
                writes=cR, dma=True)
        self.op("dve", lambda e: e.memset(ones_bf, 1.0), writes=cR)

        def rmsnorm(gbase):
            AT_.reset()
            sq = AT_.alloc([128, 8, 512], BF16)
            rstd = AT_.alloc([128, 512], F32)
            sq_r, rstd_r = R(), R()
            self.fence(self.at_live, [sq_r, rstd_r])
            self.at_live = [sq_r, rstd_r]
            for tg in range(NG):
                ts = slice(tg * 512, (tg + 1) * 512)
                self.act(sq, xT[:, :, ts], AF.Square, reads=[xT_r[k][tg] for k in range(8)], writes=sq_r)
                pb, pr = self.psum()
                for k in range(8):
                    self.mm(pb, ones_bf, sq[:, k, :], k == 0, k == 7, reads=[cR, sq_r], writes=pr)
                self.act(rstd, pb, AF.Sqrt, reads=pr, writes=rstd_r, scale=1.0 / D, bias=EPS)
                self.op("dve", lambda e: e.reciprocal(rstd, rstd), reads=rstd_r, writes=rstd_r)
                for k in range(8):
                    self.stt(hT[:, k, ts], xT[:, k, ts], fmcol(gbase, k), rstd, ALU.mult, ALU.mult,
                             reads=[xT_r[k][tg], cR, rstd_r], writes=hT_r[k][tg])

        def merge(l, b, nkc, kp, ysrc, yres, wload_b, first):
            for ct in range(2):
                wg, wg_r = self.wload(w_in_d[l], 0, 128, 8, C_GATE + b * 1024 + ct * 512, 512)
                wb, wb_r = wload_b(ct)
                for f4 in range(4):
                    fo = ct * 4 + f4
                    fs = slice(f4 * 128, (f4 + 1) * 128)
                    for tg in range(NG):
                        ts = slice(tg * 512, (tg + 1) * 512)
                        pg, pgr = self.psum()
                        py, pyr = self.psum()
                        for k in range(8):
                            self.mm(pg, wg[:, k, fs], hT[:, k, ts], k == 0, k == 7,
                                    reads=[wg_r, hT_r[k][tg]], writes=pgr)
                        for kc in range(nkc):
                            self.mm(py, wb[:kp, kc, fs], ysrc(kc, ts), kc == 0, kc == nkc - 1,
                                    reads=[wb_r, yres(kc, tg)], writes=pyr)
                        gi = (fo * NG + tg) % 2
                        gt, gt_r = gts[gi], gts_r[gi]
                        self.act(gt, pg, AF.Sigmoid, reads=[pgr, cR], writes=gt_r,
                                 bias=fmcol(FM_GATEB, l * 32 + b * 8 + fo))
                        if first:
                            self.tt("dve", xT[:, fo, ts], gt, py, ALU.mult, reads=[gt_r, pyr], writes=xT_r[fo][tg])
                        else:
                            self.tt("dve", gt, gt, py, ALU.mult, reads=[gt_r, pyr], writes=gt_r)
                            self.tt("pool", xT[:, fo, ts], xT[:, fo, ts], gt, ALU.add,
                                    reads=[gt_r, xT_r[fo][tg]], writes=xT_r[fo][tg])

        self.at_live = []
        self.ay_live = []
        xs_r = rgrid(8)

        for l in range(self.nlayers):
            if l > 0:
                self.op("sp", lambda e, l=l: e.dma_start(out=gnorm, in_=gnorm_d[l]), writes=gnorm_r, dma=True)
            else:
                self.op("sp", lambda e: e.dma_start(out=gnorm, in_=gnorm_d[0]), writes=gnorm_r, dma=True)
            rmsnorm(FM_GMIX + l * 8)
            self.tap("hT%d" % l, hT, hT_r)
            if self.stop_after == "norm1" and l == STOP_LAYER:
                break
            xs_dv = xs_d.rearrange("(k p) t -> p k t", p=128)
            if l > 0:
                for k in range(8):
                    self.op("sp", lambda e, k=k: e.dma_start(out=xs_dv[:, k, :], in_=xT[:, k, :]),
                            reads=xT_r[k], writes=xs_r[k], dma=True)
            x_src = xT_dv if l == 0 else xs_dv
            if self.stop_after == "spill" and l == STOP_LAYER:
                break

            AX_.reset()
            AT_.reset()
            AY_.reset()
            BT = AX_.alloc([128, 2, S], BF16)
            CT = AX_.alloc([128, 2, S], BF16)
            Btok = AX_.alloc([128, 2, NT, 128], BF16)
            cbm = AX_.alloc([128, 2, NT, 128], BF16)
            y2h = AX_.alloc([128, 4, NT, 128], BF16)
            ytok = AX_.alloc([128, NT, 128], F32)
            xs_tok = AX_.alloc([128, NT, 128], BF16)
            xdt = AX_.alloc([128, NT, 128], BF16)
            BT_r, CT_r, Btok_r, cbm_r = rgrid(2, NG), rgrid(2, NG), rgrid(2), rgrid(2)
            y2h_r, ytok_r, xs_tok_r, xdt_r = rgrid(4), rgrid(NT), R(), R()
            ax_new = [BT_r, CT_r, Btok_r, cbm_r, y2h_r, ytok_r, xs_tok_r, xdt_r]
            self.fence(xT_r, ax_new)

            upad = AT_.alloc([128, 3 + S], BF16)
            diag4 = AT_.alloc([128, 4, 128], BF16)
            xsT = AT_.alloc([128, S], BF16)
            xw = AT_.alloc([128, NT, 128], BF16)
            xsD = AT_.alloc([128, NT, 128], BF16)
            dtt = AT_.alloc([128, 256], F32)
            adt = AT_.alloc([128, 256], F32)
            eac = AT_.alloc([128, 256], F32)
            dsd = AT_.alloc([128, 256], F32)
            cdec = AT_.alloc([128, 256], F32)
            tmpa = AT_.alloc([128, 256], F32)
            tmpb = AT_.alloc([128, 256], F32)
            Abuf = AT_.alloc([128, 2, 128], F32)
            Ebuf = AT_.alloc([128, 2, 128], F32)
            Mbuf = AT_.alloc([128, 2, 128], BF16)
            Sst = AT_.alloc([128, 128], F32)
            Sbf = AT_.alloc([128, 128], BF16)
            yo = AT_.alloc([128, 128], F32)
            zs = AT_.alloc([128, 512], F32)
            ssq_hp = AT_.alloc([128, 16], F32)
            ssq_g = AT_.alloc([128, 16], F32)
            rstd_g = AT_.alloc([128, 16], F32)
            upad_r, diag4_r, xsT_r, xw_r, xsD_r = R(), R(), rgrid(NG), R(), R()
            dt_r, A_r, E_r, M_r, S_r, Sbf_r, yo_r, zs_r, ssq_r = R(), R(), R(), R(), R(), R(), R(), R(), R()
            A2_r, E2_r = R(), R()
            at_new = [upad_r, diag4_r, xsT_r, xw_r, xsD_r, dt_r, A_r, E_r, M_r, S_r, Sbf_r, yo_r, zs_r, ssq_r, A2_r, E2_r]
            self.fence(self.at_live, at_new)
            self.at_live = at_new
            ySSD = AY_.alloc([128, 8, S], BF16)
            ySSD_r = rgrid(8, NG)
            self.fence(self.ay_live, ySSD_r)
            self.ay_live = [ySSD_r]

            self.op("dve", lambda e: e.memset(upad[:, 0:3], 0.0), writes=upad_r)
            Ab = [Abuf, tmpa.rearrange("p (a b) -> p a b", a=2)]
            Eb = [Ebuf, tmpb.rearrange("p (a b) -> p a b", a=2)]
            A_rs, E_rs = [A_r, A2_r], [E_r, E2_r]
            Sb = [Sbf, zs[:, 0:64].bitcast(BF16)]
            Sb_rs = [Sbf_r, zs_r]

            wdt, wdt_r = self.wload(w_in_d[l], 0, 128, 8, C_DT, 16)
            pdt, pdt_r = self.psum()
            for tt_ in range(NT):
                for k in range(8):
                    self.mm(pdt[:, tt_ * 16:(tt_ + 1) * 16], hT[:, k, tt_ * 128:(tt_ + 1) * 128], wdt[:, k, 0:16],
                            k == 0, k == 7, reads=[wdt_r, hT_r[k][tt_ // 4]], writes=pdt_r)
            rep16 = lambda base: tabrp[:, base + l * 16: base + (l + 1) * 16].unsqueeze(1).broadcast_to([128, 16, 16])
            v3 = lambda a: a.rearrange("p (t h) -> p t h", h=16)
            self.tt("dve", v3(tmpa), v3(pdt[:, 0:256]), rep16(RP_DTB), ALU.add, reads=[pdt_r, cR], writes=dt_r)
            self.act(tmpb, tmpa, AF.Abs, reads=dt_r, writes=dt_r)
            self.act(tmpb, tmpb, AF.Exp, reads=dt_r, writes=dt_r, scale=-1.0)
            self.act(tmpb, tmpb, AF.Ln, reads=dt_r, writes=dt_r, bias=1.0)
            self.stt(dtt, tmpa, 0.0, tmpb, ALU.max, ALU.add, reads=dt_r, writes=dt_r)
            self.act(v3(tmpa), rep16(RP_ALOG), AF.Exp, reads=[dt_r, cR], writes=dt_r)
            self.stt(adt, dtt, -1.0, tmpa, ALU.mult, ALU.mult, reads=dt_r, writes=dt_r)
            pac, pac_r = self.psum()
            pal, pal_r = self.psum()
            for c in range(NT):
                cs_ = slice(c * 16, (c + 1) * 16)
                self.mm(pac[:, cs_], U_f, adt[:, cs_], True, True, reads=[cR, dt_r], writes=pac_r)
            for c in range(NT):
                cs_ = slice(c * 16, (c + 1) * 16)
                self.mm(pal[:, cs_], ones_f, adt[:, cs_], True, True, reads=[cR, dt_r], writes=pal_r)
            self.act(eac, pac[:, 0:256], AF.Exp, reads=pac_r, writes=dt_r)
            self.act(cdec, pal[:, 0:256], AF.Exp, reads=pal_r, writes=dt_r)
            self.cp("act", tmpa, pac[:, 0:256], reads=pac_r, writes=dt_r)
            self.tt("dve", tmpb, pal[:, 0:256], tmpa, ALU.subtract, reads=[pal_r, dt_r], writes=dt_r)
            self.act(dsd, tmpb, AF.Exp, reads=dt_r, writes=dt_r)
            self.tap("dt%d" % l, dtt, dt_r)
            self.tap("eac%d" % l, eac, dt_r)

            def conv4(fc, src_w, src_wr, wcol, dst_fn, dst_res_fn):
                self.tt("dve", diag4, ident.unsqueeze(1).broadcast_to([128, 4, 128]),
                        tabfm[:, FM_CW + (l * 12 + fc) * 4: FM_CW + (l * 12 + fc) * 4 + 4].unsqueeze(2).broadcast_to([128, 4, 128]),
                        ALU.mult, reads=cR, writes=diag4_r)
                for tg in range(NG):
                    ts = slice(tg * 512, (tg + 1) * 512)
                    pb, pr = self.psum()
                    for k in range(8):
                        self.mm(pb, src_w[:, k, wcol:wcol + 128], hT[:, k, ts], k == 0, k == 7,
                                reads=[src_wr, hT_r[k][tg]], writes=pr)
                    self.cp("act", upad[:, 3 + tg * 512: 3 + (tg + 1) * 512], pb, reads=pr, writes=upad_r)
                for tg in range(NG):
                    pb, pr = self.psum()
                    for tap_ in range(4):
                        self.mm(pb, diag4[:, tap_, :], upad[:, tg * 512 + tap_: tg * 512 + tap_ + 512],
                                tap_ == 0, tap_ == 3, reads=[diag4_r, upad_r], writes=pr)
                    self.act(dst_fn(tg), pb, AF.Silu, reads=[pr, cR], writes=dst_res_fn(tg),
                             bias=fmcol(FM_CB, l * 12 + fc))

            wbc, wbc_r = self.wload(w_in_d[l], 0, 128, 8, C_B, 512)
            for j in range(4):
                dst_t, dst_r = (BT, BT_r) if j < 2 else (CT, CT_r)
                g = j % 2
                conv4(8 + j, wbc, wbc_r, j * 128,
                      lambda tg, dst_t=dst_t, g=g: dst_t[:, g, tg * 512:(tg + 1) * 512],
                      lambda tg, dst_r=dst_r, g=g: dst_r[g][tg])
            for g in range(2):
                for t4 in range(4):
                    pb, pr = self.psum()
                    pbb = pb.bitcast(BF16)
                    for i4 in range(4):
                        tt_ = t4 * 4 + i4
                        self.tr(pbb[:, i4 * 128:(i4 + 1) * 128], BT[:, g, tt_ * 128:(tt_ + 1) * 128], ident,
                                reads=[BT_r[g][t4], cR], writes=pr)
                    self.cp("act", Btok[:, g, t4 * 4:(t4 + 1) * 4, :],
                            pbb[:, 0:512].rearrange("p (a b) -> p a b", a=4), reads=pr, writes=Btok_r[g])
                for t4 in range(4):
                    pb, pr = self.psum()
                    for i4 in range(4):
                        c = t4 * 4 + i4
                        self.mm(pb[:, i4 * 128:(i4 + 1) * 128], BT[:, g, c * 128:(c + 1) * 128],
                                CT[:, g, c * 128:(c + 1) * 128], True, True,
                                reads=[BT_r[g][t4], CT_r[g][t4]], writes=pr)
                    self.tt("dve", cbm[:, g, t4 * 4:(t4 + 1) * 4, :], pb.rearrange("p (a b) -> p a b", a=4),
                            UT_b.unsqueeze(1).broadcast_to([128, 4, 128]), ALU.mult, reads=[pr, cR], writes=cbm_r[g])
            self.tap("BT%d" % l, BT, BT_r)
            self.tap("cbm%d" % l, cbm, cbm_r)

            wxs = [None, None]
            wz = [None, None]
            for hp in range(8):
                g = hp // 4
                r0 = 2 * hp
                if hp % 4 == 0:
                    wxs_t, wxs_r = self.wload(w_in_d[l], 0, 128, 8, C_XS + (hp // 4) * 512, 512)
                conv4(hp, wxs_t, wxs_r, (hp % 4) * 128,
                      lambda tg: xsT[:, tg * 512:(tg + 1) * 512], lambda tg: xsT_r[tg])
                for t4 in range(4):
                    pb, pr = self.psum()
                    pbb = pb.bitcast(BF16)
                    for i4 in range(4):
                        tt_ = t4 * 4 + i4
                        self.tr(pbb[:, i4 * 128:(i4 + 1) * 128], xsT[:, tt_ * 128:(tt_ + 1) * 128], ident,
                                reads=[xsT_r[t4], cR], writes=pr)
                    self.cp("act", xs_tok[:, t4 * 4:(t4 + 1) * 4, :],
                            pbb[:, 0:512].rearrange("p (a b) -> p a b", a=4), reads=pr, writes=xs_tok_r)
                v4 = lambda a: a.rearrange("p t (h d) -> p t h d", h=2)
                hb = lambda a: a.rearrange("p (t h) -> p t h", h=16)[:, :, r0:r0 + 2].unsqueeze(3).broadcast_to([128, NT, 2, 64])
                self.tt("dve", v4(xdt), v4(xs_tok), hb(dtt), ALU.mult, reads=[xs_tok_r, dt_r], writes=xdt_r)
                self.tt("dve", v4(xw), v4(xdt), hb(dsd), ALU.mult, reads=[xdt_r, dt_r], writes=xw_r)
                dcol = tabrp[:, RP_D + l * 16 + r0: RP_D + l * 16 + r0 + 2]
                self.tt("dve", v4(xsD), v4(xs_tok),
                        dcol.unsqueeze(1).unsqueeze(3).broadcast_to([128, NT, 2, 64]), ALU.mult,
                        reads=[xs_tok_r, cR], writes=xsD_r)
                self.op("dve", lambda e: e.memset(Sst, 0.0), writes=S_r)

                def stage1(c):
                    pseg, pseg_r = self.psum()
                    bb = c % 2
                    for j in range(2):
                        col = c * 16 + r0 + j
                        self.tsc("dve", Ab[bb][:, j, :], L_f, adt[:, col:col + 1], None, ALU.mult, None,
                                 reads=[cR, dt_r], writes=A_rs[bb])
                        self.mm(pseg[:, j * 128:(j + 1) * 128], Ab[bb][:, j, :], U_f, True, True,
                                reads=[A_rs[bb], cR], writes=pseg_r)
                    self.act(Eb[bb], pseg[:, 0:256].rearrange("p (a b) -> p a b", a=2), AF.Exp,
                             reads=pseg_r, writes=E_rs[bb])

                def stage2(c):
                    bb = c % 2
                    if c < NT - 1:
                        pst, pst_r = self.psum()
                        self.mm(pst[:, 0:128], Btok[:, g, c, :], xw[:, c, :], True, True,
                                reads=[Btok_r[g], xw_r], writes=pst_r)
                        for j in range(2):
                            col = c * 16 + r0 + j
                            self.stt(Sst[:, j * 64:(j + 1) * 64], Sst[:, j * 64:(j + 1) * 64], cdec[:, col:col + 1],
                                     pst[:, j * 64:(j + 1) * 64], ALU.mult, ALU.add,
                                     reads=[S_r, dt_r, pst_r], writes=S_r)
                        self.cp("act", Sb[(c + 1) % 2], Sst, reads=S_r, writes=Sb_rs[(c + 1) % 2])
                    self.tt("dve", Mbuf, Eb[bb], cbm[:, g, c, :].unsqueeze(1).broadcast_to([128, 2, 128]), ALU.mult,
                            reads=[E_rs[bb], cbm_r[g]], writes=M_r)
                    if c > 0:
                        pyo, pyo_r = self.psum()
                        self.mm(pyo[:, 0:128], CT[:, g, c * 128:(c + 1) * 128], Sb[c % 2], True, True,
                                reads=[CT_r[g][c // 4], Sb_rs[c % 2]], writes=pyo_r)
                        for j in range(2):
                            col = c * 16 + r0 + j
                            self.act(yo[:, j * 64:(j + 1) * 64], pyo[:, j * 64:(j + 1) * 64], AF.Identity,
                                     reads=[pyo_r, dt_r], writes=yo_r, scale=eac[:, col:col + 1])
                    pyd, pyd_r = self.psum()
                    self.mm(pyd[:, 0:128], ident, xsD[:, c, :], True, False, reads=[cR, xsD_r], writes=pyd_r)
                    for j in range(2):
                        self.mm(pyd[:, j * 64:(j + 1) * 64], Mbuf[:, j, :], xdt[:, c, j * 64:(j + 1) * 64],
                                False, j == 1, reads=[M_r, xdt_r], writes=pyd_r)
                    if c + 1 < NT:
                        stage1(c + 1)
                    if c > 0:
                        self.tt("dve", ytok[:, c, :], pyd[:, 0:128], yo, ALU.add, reads=[pyd_r, yo_r], writes=ytok_r[c])
                    else:
                        self.cp("act", ytok[:, c, :], pyd[:, 0:128], reads=pyd_r, writes=ytok_r[c])

                stage1(0)
                for c in range(NT):
                    stage2(c)
                if hp == 0:
                    self.tap("ytok%d" % l, ytok, ytok_r)
                if hp % 4 == 0:
                    wz_t, wz_r = self.wload(w_in_d[l], 0, 128, 8, C_Z + (hp // 4) * 512, 512)
                for t4 in range(4):
                    pb, pr = self.psum()
                    for i4 in range(4):
                        tt_ = t4 * 4 + i4
                        for k in range(8):
                            self.mm(pb[:, i4 * 128:(i4 + 1) * 128], hT[:, k, tt_ * 128:(tt_ + 1) * 128],
                                    wz_t[:, k, (hp % 4) * 128:(hp % 4 + 1) * 128], k == 0, k == 7,
                                    reads=[wz_r, hT_r[k][t4]], writes=pr)
                    self.act(zs, pb, AF.Silu, reads=pr, writes=zs_r)
                    ysl = ytok[:, t4 * 4:(t4 + 1) * 4, :]
                    z3 = zs.rearrange("p (a b) -> p a b", a=4)
                    yr_ = [ytok_r[t4 * 4 + i] for i in range(4)]
                    self.tt("dve", ysl, ysl, z3, ALU.mult, reads=[zs_r] + yr_, writes=yr_)
                    self.tt("dve", z3, ysl, ysl, ALU.mult, reads=yr_, writes=zs_r)
                    self.op("dve", lambda e, t4=t4, z3=z3: e.tensor_reduce(out=ssq_hp[:, t4 * 4:(t4 + 1) * 4], in_=z3,
                                                                           axis=AX.X, op=ALU.add),
                            reads=zs_r, writes=ssq_r)
                    self.cp("act", y2h[:, hp % 4, t4 * 4:(t4 + 1) * 4, :], ysl, reads=yr_, writes=y2h_r[hp % 4])
                if hp % 4 == 0:
                    self.cp("dve", ssq_g, ssq_hp, reads=ssq_r, writes=ssq_r)
                else:
                    self.tt("dve", ssq_g, ssq_g, ssq_hp, ALU.add, reads=ssq_r, writes=ssq_r)
                if hp % 4 == 3:
                    self.act(rstd_g, ssq_g, AF.Sqrt, reads=ssq_r, writes=ssq_r, scale=1.0 / 512, bias=EPS)
                    self.op("dve", lambda e: e.reciprocal(rstd_g, rstd_g), reads=ssq_r, writes=ssq_r)
                    for hq in range(4):
                        fc = g * 4 + hq
                        yv = y2h[:, hq, :, :]
                        self.tt("dve", yv, yv, rstd_g.unsqueeze(2).broadcast_to([128, NT, 128]), ALU.mult,
                                reads=[y2h_r[hq], ssq_r], writes=y2h_r[hq])
                        self.tt("dve", xsD, yv, gnorm[:, fc * 128:(fc + 1) * 128].unsqueeze(1).broadcast_to([128, NT, 128]),
                                ALU.mult, reads=[y2h_r[hq], gnorm_r], writes=xsD_r)
                        for t4 in range(4):
                            pb, pr = self.psum()
                            pbb = pb.bitcast(BF16)
                            for i4 in range(4):
                                tt_ = t4 * 4 + i4
                                self.tr(pbb[:, i4 * 128:(i4 + 1) * 128], xsD[:, tt_, :], ident,
                                        reads=[xsD_r, cR], writes=pr)
                            self.cp("act", ySSD[:, fc, t4 * 512:(t4 + 1) * 512], pbb[:, 0:512],
                                    reads=pr, writes=ySSD_r[fc][t4])
            self.tap("ySSD%d" % l, ySSD, ySSD_r)
            if self.stop_after == "ssd" and l == STOP_LAYER:
                break

            self.fence(ax_new, xT_r)
            merge(l, 0, 8, 128, lambda kc, ts: ySSD[:, kc, ts], lambda kc, tg: ySSD_r[kc][tg],
                  lambda ct: self.wload(w_br_d[l], 0, 128, 8, ct * 512, 512), True)
            self.tap("m0_%d" % l, xT, xT_r)
            if self.stop_after == "ssdmerge" and l == STOP_LAYER:
                break

            AY_.reset()
            AT_.reset()
            yAT = AY_.alloc([128, 4, S], BF16)
            kT = AY_.alloc([128, 2, S], BF16)
            yAT_r, kT_r = rgrid(NT), rgrid(NT)
            self.fence(self.ay_live, [yAT_r, kT_r])
            self.ay_live = [yAT_r, kT_r]
            vpad = AT_.alloc([128, NT, 2, 128], BF16)
            qTt = AT_.alloc([128, 2, 8, 128], BF16)
            qsq = AT_.alloc([128, 512], F32)
            qn = AT_.alloc([128, 512], F32)
            t1 = AT_.alloc([128, 512], F32)
            qr = AT_.alloc([128, 512], BF16)
            kr = AT_.alloc([128, 128], BF16)
            Pb = AT_.alloc([128, 4, 512], BF16)
            den = AT_.alloc([128, 512], F32)
            st8 = AT_.alloc([128, 8], F32)
            skx = AT_.alloc([128, 4], F32)
            vpad_r, qTt_r = rgrid(NT), rgrid(2)
            qw_r, P_r, den_r, st_r, skx_r, kr_r = R(), rgrid(4), R(), R(), R(), R()
            at_new = [vpad_r, qTt_r, qw_r, P_r, den_r, st_r, skx_r, kr_r]
            self.fence(self.at_live, at_new)
            self.at_live = at_new
            self.op("dve", lambda e: e.memset(vpad, 0.0), writes=vpad_r)
            self.op("dve", lambda e: e.memset(kT[64:128, :, :], 0.0), writes=kT_r)
            self.op("dve", lambda e: e.memset(qTt[64:128, :, :, :], 0.0), writes=qTt_r)
            self.act(skx, tabrp[:, RP_SINK + l * 4: RP_SINK + l * 4 + 4], AF.Exp, reads=cR, writes=skx_r)
            wq, wq_r = self.wload(w_in_d[l], 0, 128, 8, C_Q, 512)
            wkv, wkv_r = self.wload(w_in_d[l], 0, 128, 8, C_K, 256)
            for i in range(NT):
                tsl = slice(i * 128, (i + 1) * 128)
                qb = i % 2
                pq, pq_r = self.psum()
                for k in range(8):
                    self.mm(pq, hT[:, k, tsl], wq[:, k, 0:512], k == 0, k == 7, reads=[wq_r, hT_r[k][i // 4]], writes=pq_r)
                pkv, pkv_r = self.psum()
                for k in range(8):
                    self.mm(pkv[:, 0:256], hT[:, k, tsl], wkv[:, k, 0:256], k == 0, k == 7,
                            reads=[wkv_r, hT_r[k][i // 4]], writes=pkv_r)

                def normrope(src, nh, gcol, dst, dst_r, src_r):
                    w = nh * 64
                    self.act(qsq[:, 0:w], src, AF.Square, reads=src_r, writes=qw_r)
                    self.op("dve", lambda e: e.tensor_reduce(out=st8[:, 0:nh], in_=qsq[:, 0:w].rearrange("p (h d) -> p h d", d=64),
                                                              axis=AX.X, op=ALU.add), reads=qw_r, writes=st_r)
                    self.act(st8[:, 0:nh], st8[:, 0:nh], AF.Sqrt, reads=st_r, writes=st_r, scale=1.0 / 64, bias=EPS)
                    self.op("dve", lambda e: e.reciprocal(st8[:, 0:nh], st8[:, 0:nh]), reads=st_r, writes=st_r)
                    h3 = lambda a: a.rearrange("p (h d) -> p h d", d=64)
                    self.tt("dve", h3(qn[:, 0:w]), h3(src), st8[:, 0:nh].unsqueeze(2).broadcast_to([128, nh, 64]), ALU.mult,
                            reads=[src_r, st_r], writes=qw_r)
                    self.tt("dve", h3(qn[:, 0:w]), h3(qn[:, 0:w]),
                            tabrp[:, gcol + l * 64: gcol + (l + 1) * 64].unsqueeze(1).broadcast_to([128, nh, 64]), ALU.mult,
                            reads=[qw_r, cR], writes=qw_r)
                    h4 = lambda a: a.rearrange("p (h a d) -> p h a d", a=2, d=32)
                    self.tt("dve", h4(t1[:, 0:w]), h4(qn[:, 0:w]),
                            ropec[:, i, :].unsqueeze(1).unsqueeze(1).broadcast_to([128, nh, 2, 32]), ALU.mult,
                            reads=[qw_r, cR], writes=qw_r)
                    for a in range(2):
                        self.tt("dve", h4(qsq[:, 0:w])[:, :, a, :], h4(qn[:, 0:w])[:, :, 1 - a, :],
                                ropes[:, i, a, :].unsqueeze(1).broadcast_to([128, nh, 32]), ALU.mult,
                                reads=[qw_r, cR], writes=qw_r)
                    self.tt("dve", dst, t1[:, 0:w], qsq[:, 0:w], ALU.add, reads=qw_r, writes=[qw_r, dst_r])

                if ATT_LEVEL < 2:
                    continue
                normrope(pq, 8, RP_QG, qr, qw_r, pq_r)
                if ATT_LEVEL < 3:
                    continue
                pb, pr = self.psum()
                pbb = pb.bitcast(BF16)
                for hd in range(8):
                    self.tr(pbb[0:64, hd * 128:(hd + 1) * 128], qr[:, hd * 64:(hd + 1) * 64], ident, reads=[qw_r, cR], writes=pr)
                self.cp("act", qTt[0:64, qb, :, :], pbb[0:64, 0:1024].rearrange("p (a b) -> p a b", a=8),
                        reads=pr, writes=qTt_r[qb])
                if ATT_LEVEL < 4:
                    continue
                for h in range(2):
                    self.cp("act", vpad[:, i, h, h * 64:(h + 1) * 64], pkv[:, 128 + h * 64: 192 + h * 64],
                            reads=pkv_r, writes=vpad_r[i])
                normrope(pkv[:, 0:128], 2, RP_KG, kr, kr_r, pkv_r)
                pb, pr = self.psum()
                pbb = pb.bitcast(BF16)
                for h in range(2):
                    self.tr(pbb[0:64, h * 128:(h + 1) * 128], kr[:, h * 64:(h + 1) * 64], ident, reads=[kr_r, cR], writes=pr)
                self.cp("act", kT[0:64, :, tsl], pbb[0:64, 0:256].rearrange("p (a b) -> p a b", a=2), reads=pr, writes=kT_r[i])
                if ATT_LEVEL < 5:
                    continue
                blks = [i] if i == 0 else [i - 1, i]
                pidx = []
                for h in range(2):
                    for blk in blks:
                        bs = slice(blk * 128, (blk + 1) * 128)
                        pi = len(pidx)
                        pidx.append((h, blk, pi))
                        ps_, ps_r = self.psum()
                        self.mm(ps_.rearrange("p (a b) -> p a b", a=4), kT[:, h, bs], qTt[:, qb, 4 * h:4 * h + 4, :],
                                True, True, reads=[kT_r[blk], qTt_r[qb]], writes=ps_r)
                        self.act(Pb[:, pi, :], ps_, AF.Exp, reads=ps_r, writes=P_r[pi], scale=0.125)
                        self.tt("dve", Pb[:, pi, :], Pb[:, pi, :], mcur if blk == i else mprev, ALU.mult,
                                reads=[P_r[pi], cR], writes=P_r[pi])
                if ATT_LEVEL < 6:
                    continue
                pnum, pnum_r = self.psum()
                pden, pden_r = self.psum()
                n = len(pidx)
                for q_, (h, blk, pi) in enumerate(pidx):
                    self.mm(pnum, vpad[:, blk, h, :], Pb[:, pi, :], q_ == 0, q_ == n - 1,
                            reads=[vpad_r[blk], P_r[pi]], writes=pnum_r)
                for q_, (h, blk, pi) in enumerate(pidx):
                    self.mm(pden, onesLR[h], Pb[:, pi, :], q_ == 0, q_ == n - 1, reads=[cR, P_r[pi]], writes=pden_r)
                d3 = lambda a: a.rearrange("p (r q) -> p r q", r=4)
                self.tt("dve", d3(den), d3(pden), skx.unsqueeze(2).broadcast_to([128, 4, 128]), ALU.add,
                        reads=[pden_r, skx_r], writes=den_r)
                self.op("dve", lambda e: e.reciprocal(den, den), reads=den_r, writes=den_r)
                self.tt("dve", yAT[:, :, tsl], d3(pnum), d3(den), ALU.mult, reads=[pnum_r, den_r], writes=yAT_r[i])
            self.tap("yAT%d" % l, yAT, yAT_r)
            if self.stop_after == "attn" and l == STOP_LAYER:
                break

            def wload_attn(ct):
                i_ = self.w_i
                self.w_i = (i_ + 1) % NW
                buf, res = self.wbufs[i_], self.wres[i_]
                for h in range(2):
                    src = w_br_d[l][1024 + h * 256:1024 + (h + 1) * 256, ct * 512:(ct + 1) * 512].rearrange(
                        "(r d) c -> d r c", d=64)
                    dst = buf[h * 64:(h + 1) * 64, 0:4, :]
                    self.op("pool", lambda e, dst=dst, src=src: e.dma_start(out=dst, in_=src), writes=res, dma=True)
                return buf, res

            merge(l, 1, 4, 128, lambda kc, ts: yAT[:, kc, ts], lambda kc, tg: [yAT_r[tg * 4 + i] for i in range(4)],
                  wload_attn, False)
            self.tap("m1_%d" % l, xT, xT_r)
            if self.stop_after == "attnmerge" and l == STOP_LAYER:
                break

            AY_.reset()
            AT_.reset()
            yPT = AY_.alloc([128, 4, S], BF16)
            yPT_r = rgrid(4, NG)
            self.fence(self.ay_live, yPT_r)
            self.ay_live = [yPT_r]
            PADW = 8
            pa = AT_.alloc([128, PADW + S], F32)
            pbuf = AT_.alloc([128, PADW + S], F32)
            ub = AT_.alloc([128, 16 + S], F32)
            pooled = AT_.alloc([128, S], BF16)
            inv16 = AT_.alloc([128, 16], F32)
            pa_r, pb_r, ub_r, pooled_r, inv_r = R(), R(), R(), rgrid(NG), R()
            at_new = [pa_r, pb_r, ub_r, pooled_r, inv_r]
            self.fence(self.at_live, at_new)
            self.at_live = at_new
            self.op("dve", lambda e: e.memset(pa[:, 0:PADW], 0.0), writes=pa_r)
            self.op("dve", lambda e: e.memset(pbuf[:, 0:PADW], 0.0), writes=pb_r)
            self.op("dve", lambda e: e.memset(ub[:, 0:16], 0.0), writes=ub_r)
            pcs, pcs_r = self.psum()
            self.mm(pcs[:, 0:128], ones_f, U_f, True, True, reads=cR, writes=pcs_r)
            self.op("dve", lambda e, pcs=pcs, inv16=inv16: e.reciprocal(inv16, pcs[:, 0:16]), reads=pcs_r, writes=inv_r)
            wpl, wpl_r = self.wload(w_in_d[l], 0, 128, 8, C_POOL, 512)
            for gi in range(4):
                w_ = (2, 4, 8, 16)[gi]
                for tg in range(NG):
                    ts = slice(tg * 512, (tg + 1) * 512)
                    pb_, pr = self.psum()
                    for k in range(8):
                        self.mm(pb_, wpl[:, k, gi * 128:(gi + 1) * 128], hT[:, k, ts], k == 0, k == 7,
                                reads=[wpl_r, hT_r[k][tg]], writes=pr)
                    self.cp("act", ub[:, 16 + tg * 512: 16 + (tg + 1) * 512], pb_, reads=pr, writes=ub_r)
                u_ = ub[:, 16:16 + S]
                src, src_r, sh = ub, ub_r, 1
                srcoff = 16
                bufs = [(pa, pa_r), (pbuf, pb_r)]
                bi = 0
                while sh < w_:
                    dstb, dst_r = bufs[bi]
                    bi ^= 1
                    self.tt("dve", dstb[:, PADW:PADW + S], src[:, srcoff:srcoff + S], src[:, srcoff - sh:srcoff - sh + S],
                            ALU.add, reads=src_r, writes=dst_r)
                    src, src_r, srcoff = dstb, dst_r, PADW
                    sh *= 2
                sw = src[:, srcoff:srcoff + S]
                for tg in range(NG):
                    ts = slice(tg * 512, (tg + 1) * 512)
                    self.stt(pooled[:, ts], sw[:, ts], 1.0 / w_, u_[:, ts], ALU.mult, ALU.subtract,
                             reads=[src_r, ub_r], writes=pooled_r[tg])
                self.tt("dve", pa[:, 0:w_ - 1], sw[:, 0:w_ - 1], inv16[:, 0:w_ - 1], ALU.mult,
                        reads=[src_r, inv_r], writes=pa_r)
                self.tt("dve", pooled[:, 0:w_ - 1], pa[:, 0:w_ - 1], u_[:, 0:w_ - 1], ALU.subtract,
                        reads=[pa_r, ub_r, pooled_r[0]], writes=pooled_r[0])
                self.op("dve", lambda e: e.memset(pa[:, 0:PADW], 0.0), reads=pa_r, writes=pa_r)
                for tg in range(NG):
                    ts = slice(tg * 512, (tg + 1) * 512)
                    pb_, pr = self.psum()
                    self.mm(pb_, poolw[:, l * 4 + gi, :], pooled[:, ts], True, True, reads=[cR, pooled_r[tg]], writes=pr)
                    self.act(yPT[:, gi, ts], pb_, AF.Identity, reads=[pr, cR], writes=yPT_r[gi][tg],
                             scale=fmcol(FM_PSC, l * 4 + gi))
            self.tap("yPT%d" % l, yPT, yPT_r)
            merge(l, 2, 4, 128, lambda kc, ts: yPT[:, kc, ts], lambda kc, tg: yPT_r[kc][tg],
                  lambda ct: self.wload(w_br_d[l], 1536, 128, 4, ct * 512, 512), False)
            self.tap("m2_%d" % l, xT, xT_r)
            if self.stop_after == "pool" and l == STOP_LAYER:
                break

            AY_.reset()
            AT_.reset()
            yCT = AY_.alloc([128, 4, S], BF16)
            cv = AY_.alloc([128, 4, S], BF16)
            yCT_r, cv_r = rgrid(4, NG), rgrid(4, NG)
            self.fence(self.ay_live, [yCT_r, cv_r])
            self.ay_live = [yCT_r, cv_r]
            up31 = AT_.alloc([128, 30 + S], BF16)
            diag31 = AT_.alloc([128, 31, 128], BF16)
            sg = AT_.alloc([128, 512], F32)
            mean = AT_.alloc([128, 512], F32)
            rstd2 = AT_.alloc([128, 512], F32)
            tn = AT_.alloc([128, 512], F32)
            cvsq = AT_.alloc([128, 512], BF16)
            up_r, dg_r, sg_r, mean_r, rs_r, tn_r, cvsq_r = R(), R(), R(), R(), R(), R(), R()
            at_new = [up_r, dg_r, sg_r, mean_r, rs_r, tn_r, cvsq_r]
            self.fence(self.at_live, at_new)
            self.at_live = at_new
            self.op("dve", lambda e: e.memset(up31[:, 0:30], 0.0), writes=up_r)
            for half in range(2):
                wa, wa_r = self.wload(w_in_d[l], 0, 128, 8, C_CONV + half * 256, 256)
                wg_, wg_r_ = self.wload(w_in_d[l], 0, 128, 8, C_CONV + 512 + half * 256, 256)
                for jj in range(2):
                    j = half * 2 + jj
                    cs2 = slice(jj * 128, (jj + 1) * 128)
                    for tg in range(NG):
                        ts = slice(tg * 512, (tg + 1) * 512)
                        pa_, par = self.psum()
                        pg_, pgr = self.psum()
                        for k in range(8):
                            self.mm(pg_, wg_[:, k, cs2], hT[:, k, ts], k == 0, k == 7, reads=[wg_r_, hT_r[k][tg]], writes=pgr)
                        for k in range(8):
                            self.mm(pa_, wa[:, k, cs2], hT[:, k, ts], k == 0, k == 7, reads=[wa_r, hT_r[k][tg]], writes=par)
                        self.act(sg, pg_, AF.Sigmoid, reads=pgr, writes=sg_r)
                        self.tt("dve", up31[:, 30 + tg * 512: 30 + (tg + 1) * 512], pa_, sg, ALU.mult,
                                reads=[par, sg_r], writes=up_r)
                    self.tt("dve", diag31, ident.unsqueeze(1).broadcast_to([128, 31, 128]),
                            tabfm[:, FM_DWW + (l * 4 + j) * 31: FM_DWW + (l * 4 + j + 1) * 31].unsqueeze(2).broadcast_to([128, 31, 128]),
                            ALU.mult, reads=cR, writes=dg_r)
                    for tg in range(NG):
                        ts = slice(tg * 512, (tg + 1) * 512)
                        pb_, pr = self.psum()
                        for tap_ in range(31):
                            self.mm(pb_, diag31[:, tap_, :], up31[:, tg * 512 + tap_: tg * 512 + tap_ + 512],
                                    tap_ == 0, tap_ == 30, reads=[dg_r, up_r], writes=pr)
                        self.act(cv[:, j, ts], pb_, AF.Identity, reads=[pr, cR], writes=cv_r[j][tg],
                                 bias=fmcol(FM_DWB, l * 4 + j))
            for tg in range(NG):
                ts = slice(tg * 512, (tg + 1) * 512)
                pm, pm_r = self.psum()
                pq2, pq2_r = self.psum()
                for j in range(4):
                    self.mm(pm, ones_bf, cv[:, j, ts], j == 0, j == 3, reads=[cR, cv_r[j][tg]], writes=pm_r)
                for j in range(4):
                    self.act(cvsq, cv[:, j, ts], AF.Square, reads=cv_r[j][tg], writes=cvsq_r)
                    self.mm(pq2, ones_bf, cvsq, j == 0, j == 3, reads=[cR, cvsq_r], writes=pq2_r)
                self.act(mean, pm, AF.Identity, reads=pm_r, writes=mean_r, scale=1.0 / 512)
                self.tt("dve", tn, mean, mean, ALU.mult, reads=mean_r, writes=tn_r)
                self.stt(rstd2, pq2, 1.0 / 512, tn, ALU.mult, ALU.subtract, reads=[pq2_r, tn_r], writes=rs_r)
                self.act(rstd2, rstd2, AF.Sqrt, reads=rs_r, writes=rs_r, bias=EPS)
                self.op("dve", lambda e: e.reciprocal(rstd2, rstd2), reads=rs_r, writes=rs_r)
                for j in range(4):
                    self.tt("dve", tn, cv[:, j, ts], mean, ALU.subtract, reads=[cv_r[j][tg], mean_r], writes=tn_r)
                    self.tt("dve", tn, tn, rstd2, ALU.mult, reads=[tn_r, rs_r], writes=tn_r)
                    self.act(yCT[:, j, ts], tn, AF.Silu, reads=[tn_r, cR], writes=yCT_r[j][tg],
                             scale=fmcol(FM_LNG, l * 4 + j), bias=fmcol(FM_LNB, l * 4 + j))
            self.tap("yCT%d" % l, yCT, yCT_r)
            merge(l, 3, 4, 128, lambda kc, ts: yCT[:, kc, ts], lambda kc, tg: yCT_r[kc][tg],
                  lambda ct: self.wload(w_br_d[l], 2048, 128, 4, ct * 512, 512), False)
            self.tap("m3_%d" % l, xT, xT_r)
            if self.stop_after == "conv" and l == STOP_LAYER:
                break

            for k in range(8):
                for tg in range(NG):
                    ts = slice(tg * 512, (tg + 1) * 512)
                    self.cp("act" if (k + tg) % 2 else "dve", hT[:, k, ts], xT[:, k, ts],
                            reads=xT_r[k][tg], writes=hT_r[k][tg])
            for k in range(8):
                self.op("sp", lambda e, k=k, x_src=x_src: e.dma_start(out=xT[:, k, :], in_=x_src[:, k, :]),
                        reads=xs_r[k], writes=xT_r[k], dma=True)
            for ct in range(2):
                wo, wo_r = self.wload(w_out_d[l], 0, 128, 8, ct * 512, 512)
                for f4 in range(4):
                    fo = ct * 4 + f4
                    for tg in range(NG):
                        ts = slice(tg * 512, (tg + 1) * 512)
                        pb_, pr = self.psum()
                        for k in range(8):
                            self.mm(pb_, wo[:, k, f4 * 128:(f4 + 1) * 128], hT[:, k, ts], k == 0, k == 7,
                                    reads=[wo_r, hT_r[k][tg]], writes=pr)
                        self.tt("dve", xT[:, fo, ts], xT[:, fo, ts], pb_, ALU.add, reads=[pr, xT_r[fo][tg]],
                                writes=xT_r[fo][tg])
            self.tap("xmid%d" % l, xT, xT_r)
            if self.stop_after == "wout" and l == STOP_LAYER:
                break

            rmsnorm(FM_GMLP + l * 8)
            AY_.reset()
            AT_.reset()
            aT = AY_.alloc([128, 8, S], BF16)
            aT_r = rgrid(8, NG)
            self.fence(self.ay_live, aT_r)
            self.ay_live = [aT_r]
            rl = [AT_.alloc([128, 512], F32) for _ in range(2)]
            rl_r = [R(), R()]
            self.fence(self.at_live, rl_r)
            self.at_live = rl_r
            for hb in range(4):
                for ct in range(2):
                    wu, wu_r = self.wload(w_up_d[l], 0, 128, 8, hb * 1024 + ct * 512, 512)
                    for f4 in range(4):
                        fc = ct * 4 + f4
                        for tg in range(NG):
                            ts = slice(tg * 512, (tg + 1) * 512)
                            pb_, pr = self.psum()
                            for k in range(8):
                                self.mm(pb_, wu[:, k, f4 * 128:(f4 + 1) * 128], hT[:, k, ts], k == 0, k == 7,
                                        reads=[wu_r, hT_r[k][tg]], writes=pr)
                            ri = (fc * NG + tg) % 2
                            self.act(rl[ri], pb_, AF.Relu, reads=pr, writes=rl_r[ri])
                            self.tt("dve", aT[:, fc, ts], rl[ri], rl[ri], ALU.mult, reads=rl_r[ri], writes=aT_r[fc][tg])
                for ct in range(2):
                    wd, wd_r = self.wload(w_dn_d[l], hb * 1024, 128, 8, ct * 512, 512)
                    for f4 in range(4):
                        fo = ct * 4 + f4
                        for tg in range(NG):
                            ts = slice(tg * 512, (tg + 1) * 512)
                            pb_, pr = self.psum()
                            for k in range(8):
                                self.mm(pb_, wd[:, k, f4 * 128:(f4 + 1) * 128], aT[:, k, ts], k == 0, k == 7,
                                        reads=[wd_r, aT_r[k][tg]], writes=pr)
                            self.tt("dve", xT[:, fo, ts], xT[:, fo, ts], pb_, ALU.add, reads=[pr, xT_r[fo][tg]],
                                    writes=xT_r[fo][tg])
            self.tap("xout%d" % l, xT, xT_r)

        out_v = out_d.rearrange("(k p) t -> p k t", p=128)
        for k in range(8):
            self.op("sp", lambda e, k=k: e.dma_start(out=out_v[:, k, :], in_=xT[:, k, :]), reads=xT_r[k], dma=True)
        sc.finalize()
        fw = [(cs, sc.counts[cs]) for cs in ("d_sp", "d_pool", "d_act") if sc.counts.get(cs, 0)]
        sems = {}
        for cs, n in sc.counts.items():
            sg = SEG_DMA if cs.startswith("d_") else SEG
            sems[cs] = [self.es.enter_context(nc.semaphore("s_%s_%d" % (cs, i))) for i in range((n + sg - 1) // sg)]
        with nc.Block() as block:
            sc.emit(nc, block, sems, fw)


def build_program(debug=None, nlayers=DEPTH, stop_after=None):
    b = Builder(debug=debug, nlayers=nlayers, stop_after=stop_after)
    nc = b.build()
    return nc, b


def _fm(v, nchunks):
    v = np.asarray(v, np.float32)
    return np.ascontiguousarray(v.reshape(L, nchunks, 128).transpose(2, 0, 1).reshape(128, L * nchunks))


def _rep(v):
    v = np.asarray(v, np.float32).reshape(1, -1)
    return np.ascontiguousarray(np.repeat(v, 128, axis=0))


def make_in_maps(inputs, ncores=8):
    import ml_dtypes
    f = lambda k: np.asarray(inputs[k], np.float32)
    x = f("x")
    cw = f("ssd_conv_w")
    cw_t = cw.reshape(L, 4, 12, 128).transpose(3, 0, 2, 1).reshape(128, L * 48)
    dww = f("conv_dw_w")
    dww_t = dww.reshape(L, 31, 4, 128).transpose(3, 0, 2, 1).reshape(128, L * 124)
    tab_fm = np.concatenate([
        _fm(f("norm_mix_g"), 8), _fm(f("norm_mlp_g"), 8), _fm(f("gate_b"), 32), cw_t,
        _fm(f("ssd_conv_b"), 12), _fm(f("pool_scale"), 4), dww_t, _fm(f("conv_dw_b"), 4),
        _fm(f("conv_ln_g"), 4), _fm(f("conv_ln_b"), 4)], axis=1).astype(np.float32)
    assert tab_fm.shape == (128, NFM), tab_fm.shape
    sk = f("attn_sinks")
    sk_t = np.zeros((128, L * 4), np.float32)
    for l in range(L):
        sk_t[0:64, l * 4:(l + 1) * 4] = sk[l, 0:4][None, :]
        sk_t[64:128, l * 4:(l + 1) * 4] = sk[l, 4:8][None, :]
    tab_rp = np.concatenate([_rep(f("ssd_dt_bias")), _rep(f("ssd_a_log")), _rep(f("ssd_d")),
                             _rep(f("q_norm_g")), _rep(f("k_norm_g")), sk_t], axis=1).astype(np.float32)
    assert tab_rp.shape == (128, NRP), tab_rp.shape
    gnorm_rep = np.ascontiguousarray(np.repeat(f("ssd_norm_g")[:, None, :], 128, axis=1))
    inv = (1.0 / (10000.0 ** (np.arange(0, 64, 2, dtype=np.float32) / np.float32(64.0)))).astype(np.float32)
    ang = (np.arange(S, dtype=np.float32)[:, None] * inv[None, :]).astype(np.float32)
    cos = np.cos(ang).astype(np.float32).reshape(NT, 128, 32).transpose(1, 0, 2)
    sin = np.sin(ang).astype(np.float32).reshape(NT, 128, 32).transpose(1, 0, 2)
    rope_c = np.ascontiguousarray(cos.reshape(128, NT * 32))
    rope_s2 = np.ascontiguousarray(np.stack([-sin, sin], axis=2).reshape(128, NT * 64))
    k = np.arange(128)
    U = (k[:, None] <= k[None, :]).astype(np.float32)
    Lm = (k[:, None] > k[None, :]).astype(np.float32)
    c_f32 = np.concatenate([U, Lm, np.ones((128, 128), np.float32)], axis=1)
    mcur = np.tile(U, (1, 4))
    mprev = np.tile(Lm, (1, 4))
    onl = np.zeros((128, 128), np.float32)
    onl[:, 0:64] = 1.0
    onr = np.zeros((128, 128), np.float32)
    onr[:, 64:128] = 1.0
    c_bf = np.concatenate([np.eye(128, dtype=np.float32), U, mcur, mprev, onl, onr], axis=1).astype(ml_dtypes.bfloat16)
    assert c_bf.shape == (128, NCB)
    pw = f("pool_w")
    pool_w = np.ascontiguousarray(pw.transpose(2, 0, 1, 3).reshape(128, L * 4 * 128))
    shared = {
        "w_in": f("w_in"), "w_branch": f("w_branch"), "w_out": f("w_out"),
        "w_up": f("w_mlp_up"), "w_down": f("w_mlp_down"),
        "tab_fm": tab_fm, "tab_rp": tab_rp, "gnorm_rep": gnorm_rep,
        "rope_c": rope_c, "rope_s2": rope_s2, "c_f32": c_f32, "c_bf": c_bf, "pool_w": pool_w,
    }
    maps = []
    for c in range(ncores):
        m = dict(shared)
        m["xT"] = np.ascontiguousarray(x[c].T)
        maps.append(m)
    return maps


def kernel(**inputs):
    nc, b = build_program()
    in_maps = make_in_maps(inputs)
    res = run_bass_kernel_spmd(nc, in_maps, core_ids=list(range(8)))
    out = np.stack([np.ascontiguousarray(r["yT"].T) for r in res.results], axis=0)
    return out.astype(np.float32)
```

```python
import numpy as np
from contextlib import ExitStack
import concourse.bass as bass
import concourse.mybir as mybir
from concourse.bass_utils import run_bass_kernel_spmd

F32 = mybir.dt.float32
BF16 = mybir.dt.bfloat16
ALU = mybir.AluOpType
AF = mybir.ActivationFunctionType
AX = mybir.AxisListType

D = 1024
S = 2048
DEPTH = 2
NT = S // 128
NG = S // 512
EPS = 1e-6
IN_WIDTH = 8976
C_Z, C_XS, C_B, C_C, C_DT, C_Q, C_K, C_V, C_POOL, C_CONV, C_GATE = (
    0, 1024, 2048, 2304, 2560, 2576, 3088, 3216, 3344, 3856, 4880)


class R:
    __slots__ = ("w", "rd")

    def __init__(self):
        self.w = None
        self.rd = {}


def rgrid(*shape):
    if len(shape) == 1:
        return [R() for _ in range(shape[0])]
    return [rgrid(*shape[1:]) for _ in range(shape[0])]


def flat(x):
    if isinstance(x, R):
        return [x]
    out = []
    for e in x:
        out.extend(flat(e))
    return out


import sys


def _where():
    f = sys._getframe(2)
    out = []
    while f is not None and len(out) < 4:
        out.append(f.f_lineno)
        f = f.f_back
    return out


class Op:
    __slots__ = ("eng", "cs", "fn", "deps", "sig", "cnt", "waits", "snap", "where")


ENGS = ("pe", "act", "dve", "pool", "sp")


class Sched:
    def __init__(self):
        self.ops = []

    def add(self, eng, fn, reads=(), writes=(), dma=False):
        op = Op()
        op.eng = eng
        op.cs = ("d_" + eng) if dma else eng
        op.fn = fn
        op.where = _where()
        op.sig = dma
        op.cnt = 0
        deps = {}
        rl = flat(reads)
        wl = flat(writes)
        for r in rl:
            if r.w is not None:
                deps[id(r.w)] = r.w
        for w in wl:
            if w.w is not None:
                deps[id(w.w)] = w.w
            for o in w.rd.values():
                deps[id(o)] = o
        if eng == "pe" and not dma:
            deps = {i: o for i, o in deps.items() if o.cs != "pe"}
        op.deps = list(deps.values())
        for r in rl:
            r.rd[op.cs] = op
        for w in wl:
            w.w = op
            w.rd = {}
        self.ops.append(op)
        return op

    def finalize(self):
        for op in self.ops:
            for d in op.deps:
                d.sig = True
        counts = {}
        for op in self.ops:
            if op.sig:
                counts[op.cs] = counts.get(op.cs, 0) + 1
            op.cnt = counts.get(op.cs, 0)
        known = {e: {} for e in ENGS}
        for op in self.ops:
            kn = known[op.eng]
            need = {}
            for d in op.deps:
                if d.cnt > need.get(d.cs, 0):
                    need[d.cs] = d.cnt
            waits = []
            for d in sorted(op.deps, key=lambda o: -o.cnt):
                if kn.get(d.cs, 0) >= d.cnt:
                    continue
                if need.get(d.cs, 0) != d.cnt:
                    continue
                waits.append((d.cs, d.cnt))
                for c, v in d.snap.items():
                    if kn.get(c, 0) < v:
                        kn[c] = v
                kn[d.cs] = max(kn.get(d.cs, 0), d.cnt)
            op.waits = waits
            if op.sig:
                sn = dict(kn)
                op.snap = sn
            else:
                op.snap = None
        self.counts = counts

    def emit(self, nc, block, sems, final_waits=()):
        by = {e: [] for e in ENGS}
        for op in self.ops:
            by[op.eng].append(op)

        def seg(cs, cnt):
            sg = SEG_DMA if cs.startswith("d_") else SEG
            inc = 16 if cs.startswith("d_") else 1
            i = (cnt - 1) // sg
            return sems[cs][i], (cnt - i * sg) * inc

        def run(e, ops):
            segdone = {}
            for op in ops:
                for cs, cnt in op.waits:
                    if cs.startswith("d_"):
                        s_idx = (cnt - 1) // SEG_DMA
                        for k in range(segdone.get(cs, 0), s_idx):
                            e.wait_ge(sems[cs][k], SEG_DMA * 16)
                        segdone[cs] = max(segdone.get(cs, 0), s_idx)
                    sm, val = seg(cs, cnt)
                    e.wait_ge(sm, val)
                if op.cs.startswith("d_") and op.cnt > MAX_DMA_OUT:
                    sm, val = seg(op.cs, op.cnt - MAX_DMA_OUT)
                    e.wait_ge(sm, val)
                try:
                    ins = op.fn(e)
                except Exception:
                    print("FAILED OP at lines", op.where, op.eng)
                    raise
                if op.sig:
                    inc = 16 if op.cs.startswith("d_") else 1
                    sm, _ = seg(op.cs, op.cnt)
                    ins.then_inc(sm, inc)

        @block.tensor
        def _(e):
            run(e, by["pe"])

        @block.scalar
        def _(e):
            run(e, by["act"])

        @block.vector
        def _(e):
            run(e, by["dve"])

        @block.gpsimd
        def _(e):
            run(e, by["pool"])

        @block.sync
        def _(e):
            run(e, by["sp"])
            for cs, n in final_waits:
                k = 0
                while k * SEG_DMA < n:
                    last = min(n, (k + 1) * SEG_DMA)
                    sm, val = seg(cs, last)
                    e.wait_ge(sm, val)
                    k += 1


import os as _os
SEG = int(_os.environ.get('SEG', '2000'))
SEG_DMA = int(_os.environ.get('SEG_DMA', '1000000'))
MAX_DMA_OUT = int(_os.environ.get('MAX_DMA_OUT', '4'))


import os
ATT_LEVEL = int(os.environ.get('ATT_LEVEL', '9'))
STOP_LAYER = int(os.environ.get('STOP_LAYER', '0'))
NW = 3
L = DEPTH
FM_GMIX, FM_GMLP, FM_GATEB, FM_CW, FM_CB, FM_PSC, FM_DWW, FM_DWB, FM_LNG, FM_LNB = (
    0, L * 8, L * 16, L * 48, L * 96, L * 108, L * 112, L * 236, L * 240, L * 244)
NFM = L * 248
RP_DTB, RP_ALOG, RP_D, RP_QG, RP_KG, RP_SINK = 0, L * 16, L * 32, L * 48, L * 112, L * 176
NRP = L * 180
CB_ID, CB_UT, CB_MCUR, CB_MPREV, CB_ONL, CB_ONR = 0, 128, 256, 768, 1280, 1408
NCB = 1536


class Arena:
    def __init__(self, b, name, nbytes):
        self.t = b.sb(name, [128, nbytes // 4], F32)
        self.n = nbytes
        self.off = 0

    def reset(self):
        self.off = 0

    def alloc(self, shape, dt):
        es = 4 if dt == F32 else 2
        n = 1
        for d in shape[1:]:
            n *= d
        nb = (n * es + 31) // 32 * 32
        off = self.off
        self.off += nb
        assert self.off <= self.n, (self.off, self.n)
        ap = self.t[:, off // 4:(off + nb) // 4]
        if dt != F32:
            ap = ap.bitcast(dt)
        ap = ap[:, :n]
        if len(shape) == 3:
            ap = ap.rearrange("p (a b) -> p a b", a=shape[1])
        elif len(shape) == 4:
            ap = ap.rearrange("p (a b c) -> p a b c", a=shape[1], b=shape[2])
        if shape[0] < 128:
            ap = ap[:shape[0]]
        return ap


class Builder:
    def __init__(self, debug=None, nlayers=DEPTH, stop_after=None):
        self.debug = debug or []
        self.nlayers = nlayers
        self.stop_after = stop_after
        self.nc = bass.Bass("TRN2", target_bir_lowering=False)
        self.sc = Sched()
        self.es = ExitStack()
        self.psum_i = 0
        self.w_i = 0
        self.dbg_d = {}

    def dram_in(self, name, shape, dtype=F32):
        return self.nc.dram_tensor(name, list(shape), dtype, kind="ExternalInput").ap()

    def sb(self, name, shape, dtype=F32):
        return self.es.enter_context(self.nc.sbuf_tensor(name, list(shape), dtype))[:]

    def op(self, eng, fn, reads=(), writes=(), dma=False):
        return self.sc.add(eng, fn, reads, writes, dma)

    def psum(self):
        i = self.psum_i
        self.psum_i = (i + 1) % 8
        return self.pbanks[i], self.pres[i]

    def mm(self, out, lhsT, rhs, start, stop, reads, writes):
        return self.op("pe", lambda e: e.matmul(out, lhsT, rhs, start=start, stop=stop), reads, writes)

    def tr(self, out, in_, ident, reads, writes):
        return self.op("pe", lambda e: e.transpose(out, in_, ident), reads, writes)

    def act(self, out, in_, func, reads, writes, bias=None, scale=None):
        kw = {}
        if bias is not None:
            kw["bias"] = bias
        if scale is not None:
            kw["scale"] = scale
        return self.op("act", lambda e: e.activation(out=out, in_=in_, func=func, **kw), reads, writes)

    def tt(self, eng, out, in0, in1, op, reads, writes):
        return self.op(eng, lambda e: e.tensor_tensor(out=out, in0=in0, in1=in1, op=op), reads, writes)

    def tsc(self, eng, out, in0, s1, s2, op0, op1, reads, writes):
        if s2 is None:
            return self.op(eng, lambda e: e.tensor_scalar(out, in0, s1, None, op0), reads, writes)
        return self.op(eng, lambda e: e.tensor_scalar(out, in0, s1, s2, op0, op1), reads, writes)

    def stt(self, out, in0, scalar, in1, op0, op1, reads, writes):
        return self.op("dve", lambda e: e.scalar_tensor_tensor(out=out, in0=in0, scalar=scalar, in1=in1,
                                                              op0=op0, op1=op1), reads, writes)

    def cp(self, eng, out, in_, reads, writes):
        if eng == "act":
            return self.op("act", lambda e: e.activation(out=out, in_=in_, func=AF.Copy), reads, writes)
        return self.op(eng, lambda e: e.tensor_copy(out=out, in_=in_), reads, writes)

    def fence(self, old, new):
        d = self.dummy
        self.op("dve", lambda e: e.memset(d, 0.0), writes=[old, new])

    def tap(self, name, ap, res):
        for n, shape, dt in self.debug:
            if n == name:
                dst = self.dbg_d[name]
                self.op("sp", lambda e: e.dma_start(out=dst, in_=ap), reads=res, dma=True)

    def wload(self, src2d, r0, kp, nk, c0, ncols):
        i = self.w_i
        self.w_i = (i + 1) % NW
        buf, res = self.wbufs[i], self.wres[i]
        src = src2d[r0:r0 + kp * nk, c0:c0 + ncols].rearrange("(k p) c -> p k c", p=kp)
        dst = buf[:kp, :nk, :ncols]
        self.op("pool", lambda e: e.dma_start(out=dst, in_=src), writes=res, dma=True)
        return buf, res

    def build(self):
        with self.es:
            self._build()
        return self.nc

    def _build(self):
        nc = self.nc
        sc = self.sc
        xT_d = self.dram_in("xT", [D, S])
        w_in_d = self.dram_in("w_in", [L, D, IN_WIDTH])
        w_br_d = self.dram_in("w_branch", [L, 2560, D])
        w_out_d = self.dram_in("w_out", [L, D, D])
        w_up_d = self.dram_in("w_up", [L, D, 4096])
        w_dn_d = self.dram_in("w_down", [L, 4096, D])
        tabfm_d = self.dram_in("tab_fm", [128, NFM])
        tabrp_d = self.dram_in("tab_rp", [128, NRP])
        gnorm_d = self.dram_in("gnorm_rep", [L, 128, 1024])
        ropec_d = self.dram_in("rope_c", [128, 16 * 32])
        ropes_d = self.dram_in("rope_s2", [128, 16 * 64])
        cf32_d = self.dram_in("c_f32", [128, 384])
        cbf_d = self.dram_in("c_bf", [128, NCB], BF16)
        poolw_d = self.dram_in("pool_w", [128, L * 4 * 128])
        out_d = nc.dram_tensor("yT", [D, S], F32, kind="ExternalOutput").ap()
        xs_d = nc.dram_tensor("x_spill", [D, S], F32, kind="ExternalOutput").ap()
        for name, shape, dt in self.debug:
            self.dbg_d[name] = nc.dram_tensor("dbg_" + name, list(shape), dt, kind="ExternalOutput").ap()

        AX_ = Arena(self, "arena_x", 65536)
        AH_ = Arena(self, "arena_h", 32768)
        AY_ = Arena(self, "arena_y", 32768)
        AT_ = Arena(self, "arena_t", 30720)
        xT = AX_.alloc([128, 8, S], F32)
        xT_r = rgrid(8, NG)
        hT = AH_.alloc([128, 8, S], BF16)
        hT_r = rgrid(8, NG)
        self.wbufs = [self.sb("wbuf%d" % i, [128, 8, 512], BF16) for i in range(NW)]
        self.wres = [R() for _ in range(NW)]
        tabfm = self.sb("tabfm", [128, NFM])
        tabrp = self.sb("tabrp", [128, NRP])
        gnorm = self.sb("gnorm", [128, 1024])
        gnorm_r = R()
        ropec = self.sb("ropec", [128, 16, 32])
        ropes = self.sb("ropes", [128, 16, 2, 32])
        cf32 = self.sb("cf32", [128, 384])
        cbf = self.sb("cbf", [128, NCB], BF16)
        poolw = self.sb("poolw", [128, L * 4, 128], BF16)
        gts = [self.sb("gt%d" % i, [128, 512]) for i in range(2)]
        gts_r = [R(), R()]
        self.dummy = self.sb("fence_dummy", [128, 8])
        cR = R()
        U_f = cf32[:, 0:128]
        L_f = cf32[:, 128:256]
        ones_f = cf32[:, 256:384]
        ident = cbf[:, CB_ID:CB_ID + 128]
        UT_b = cbf[:, CB_UT:CB_UT + 128]
        mcur = cbf[:, CB_MCUR:CB_MCUR + 512]
        mprev = cbf[:, CB_MPREV:CB_MPREV + 512]
        onesLR = [cbf[:, CB_ONL:CB_ONL + 128], cbf[:, CB_ONR:CB_ONR + 128]]
        ones_bf = self.sb("ones_bf", [128, 128], BF16)

        self.pbanks = [self.es.enter_context(nc.psum_tensor("ps%d" % i, [128, 512], F32))[:] for i in range(8)]
        self.pres = [R() for _ in range(8)]
        def fmcol(base, idx):
            return tabfm[:, base + idx: base + idx + 1]

        xT_dv = xT_d.rearrange("(k p) t -> p k t", p=128)
        for k in range(8):
            self.op("sp", lambda e, k=k: e.dma_start(out=xT[:, k, :], in_=xT_dv[:, k, :]), writes=xT_r[k], dma=True)
        for dst, src in ((tabfm, tabfm_d), (tabrp, tabrp_d), (cf32, cf32_d), (cbf, cbf_d),
                         (ropec, ropec_d.rearrange("p (t c) -> p t c", c=32)),
                         (ropes, ropes_d.rearrange("p (t a c) -> p t a c", a=2, c=32))):
            self.op("sp", lambda e, dst=dst, src=src: e.dma_start(out=dst, in_=src), writes=cR, dma=True)
        self.op("pool", lambda e: e.dma_start(out=poolw, in_=poolw_d.rearrange("p (g d) -> p g d", d=128)),
                writes=cR, dma=True)
        self.op("dve", lambda e: e.memset(ones_bf, 1.0), writes=cR)

        def rmsnorm(gbase):
            AT_.reset()
            sq = AT_.alloc([128, 8, 512], BF16)
            rstd = AT_.alloc([128, 512], F32)
            sq_r, rstd_r = R(), R()
            self.fence(self.at_live, [sq_r, rstd_r])
            self.at_live = [sq_r, rstd_r]
            for tg in range(NG):
                ts = slice(tg * 512, (tg + 1) * 512)
                self.act(sq, xT[:, :, ts], AF.Square, reads=[xT_r[k][tg] for k in range(8)], writes=sq_r)
                pb, pr = self.psum()
                for k in range(8):
                    self.mm(pb, ones_bf, sq[:, k, :], k == 0, k == 7, reads=[cR, sq_r], writes=pr)
                self.act(rstd, pb, AF.Sqrt, reads=pr, writes=rstd_r, scale=1.0 / D, bias=EPS)
                self.op("dve", lambda e: e.reciprocal(rstd, rstd), reads=rstd_r, writes=rstd_r)
                for k in range(8):
                    self.stt(hT[:, k, ts], xT[:, k, ts], fmcol(gbase, k), rstd, ALU.mult, ALU.mult,
                             reads=[xT_r[k][tg], cR, rstd_r], writes=hT_r[k][tg])

        def merge(l, b, nkc, kp, ysrc, yres, wload_b, first):
            for ct in range(2):
                wg, wg_r = self.wload(w_in_d[l], 0, 128, 8, C_GATE + b * 1024 + ct * 512, 512)
                wb, wb_r = wload_b(ct)
                for f4 in range(4):
                    fo = ct * 4 + f4
                    fs = slice(f4 * 128, (f4 + 1) * 128)
                    for tg in range(NG):
                        ts = slice(tg * 512, (tg + 1) * 512)
                        pg, pgr = self.psum()
                        py, pyr = self.psum()
                        for k in range(8):
                            self.mm(pg, wg[:, k, fs], hT[:, k, ts], k == 0, k == 7,
                                    reads=[wg_r, hT_r[k][tg]], writes=pgr)
                        for kc in range(nkc):
                            self.mm(py, wb[:kp, kc, fs], ysrc(kc, ts), kc == 0, kc == nkc - 1,
                                    reads=[wb_r, yres(kc, tg)], writes=pyr)
                        gi = (fo * NG + tg) % 2
                        gt, gt_r = gts[gi], gts_r[gi]
                        self.act(gt, pg, AF.Sigmoid, reads=[pgr, cR], writes=gt_r,
                                 bias=fmcol(FM_GATEB, l * 32 + b * 8 + fo))
                        if first:
                            self.tt("dve", xT[:, fo, ts], gt, py, ALU.mult, reads=[gt_r, pyr], writes=xT_r[fo][tg])
                        else:
                            self.tt("dve", gt, gt, py, ALU.mult, reads=[gt_r, pyr], writes=gt_r)
                            self.tt("pool", xT[:, fo, ts], xT[:, fo, ts], gt, ALU.add,
                                    reads=[gt_r, xT_r[fo][tg]], writes=xT_r[fo][tg])

        self.at_live = []
        self.ay_live = []
        xs_r = rgrid(8)

        for l in range(self.nlayers):
            if l > 0:
                self.op("sp", lambda e, l=l: e.dma_start(out=gnorm, in_=gnorm_d[l]), writes=gnorm_r, dma=True)
            else:
                self.op("sp", lambda e: e.dma_start(out=gnorm, in_=gnorm_d[0]), writes=gnorm_r, dma=True)
            rmsnorm(FM_GMIX + l * 8)
            self.tap("hT%d" % l, hT, hT_r)
            if self.stop_after == "norm1" and l == STOP_LAYER:
                break
            xs_dv = xs_d.rearrange("(k p) t -> p k t", p=128)
            if l > 0:
                for k in range(8):
                    self.op("sp", lambda e, k=k: e.dma_start(out=xs_dv[:, k, :], in_=xT[:, k, :]),
                            reads=xT_r[k], writes=xs_r[k], dma=True)
            x_src = xT_dv if l == 0 else xs_dv
            if self.stop_after == "spill" and l == STOP_LAYER:
                break

            AX_.reset()
            AT_.reset()
            AY_.reset()
            BT = AX_.alloc([128, 2, S], BF16)
            CT = AX_.alloc([128, 2, S], BF16)
            Btok = AX_.alloc([128, 2, NT, 128], BF16)
            cbm = AX_.alloc([128, 2, NT, 128], BF16)
            y2h = AX_.alloc([128, 4, NT, 128], BF16)
            ytok = AX_.alloc([128, NT, 128], F32)
            xs_tok = AX_.alloc([128, NT, 128], BF16)
            xdt = AX_.alloc([128, NT, 128], BF16)
            BT_r, CT_r, Btok_r, cbm_r = rgrid(2, NG), rgrid(2, NG), rgrid(2), rgrid(2)
            y2h_r, ytok_r, xs_tok_r, xdt_r = rgrid(4), rgrid(NT), R(), R()
            ax_new = [BT_r, CT_r, Btok_r, cbm_r, y2h_r, ytok_r, xs_tok_r, xdt_r]
            self.fence(xT_r, ax_new)

            upad = AT_.alloc([128, 3 + S], BF16)
            diag4 = AT_.alloc([128, 4, 128], BF16)
            xsT = AT_.alloc([128, S], BF16)
            xw = AT_.alloc([128, NT, 128], BF16)
            xsD = AT_.alloc([128, NT, 128], BF16)
            dtt = AT_.alloc([128, 256], F32)
            adt = AT_.alloc([128, 256], F32)
            eac = AT_.alloc([128, 256], F32)
            dsd = AT_.alloc([128, 256], F32)
            cdec = AT_.alloc([128, 256], F32)
            tmpa = AT_.alloc([128, 256], F32)
            tmpb = AT_.alloc([128, 256], F32)
            Abuf = AT_.alloc([128, 2, 128], F32)
            Ebuf = AT_.alloc([128, 2, 128], F32)
            Mbuf = AT_.alloc([128, 2, 128], BF16)
            Sst = AT_.alloc([128, 128], F32)
            Sbf = AT_.alloc([128, 128], BF16)
            yo = AT_.alloc([128, 128], F32)
            zs = AT_.alloc([128, 512], F32)
            ssq_hp = AT_.alloc([128, 16], F32)
            ssq_g = AT_.alloc([128, 16], F32)
            rstd_g = AT_.alloc([128, 16], F32)
            upad_r, diag4_r, xsT_r, xw_r, xsD_r = R(), R(), rgrid(NG), R(), R()
            dt_r, A_r, E_r, M_r, S_r, Sbf_r, yo_r, zs_r, ssq_r = R(), R(), R(), R(), R(), R(), R(), R(), R()
            A2_r, E2_r = R(), R()
            at_new = [upad_r, diag4_r, xsT_r, xw_r, xsD_r, dt_r, A_r, E_r, M_r, S_r, Sbf_r, yo_r, zs_r, ssq_r, A2_r, E2_r]
            self.fence(self.at_live, at_new)
            self.at_live = at_new
            ySSD = AY_.alloc([128, 8, S], BF16)
            ySSD_r = rgrid(8, NG)
            self.fence(self.ay_live, ySSD_r)
            self.ay_live = [ySSD_r]

            self.op("dve", lambda e: e.memset(upad[:, 0:3], 0.0), writes=upad_r)
            Ab = [Abuf, tmpa.rearrange("p (a b) -> p a b", a=2)]
            Eb = [Ebuf, tmpb.rearrange("p (a b) -> p a b", a=2)]
            A_rs, E_rs = [A_r, A2_r], [E_r, E2_r]
            Sb = [Sbf, zs[:, 0:64].bitcast(BF16)]
            Sb_rs = [Sbf_r, zs_r]

            wdt, wdt_r = self.wload(w_in_d[l], 0, 128, 8, C_DT, 16)
            pdt, pdt_r = self.psum()
            for tt_ in range(NT):
                for k in range(8):
                    self.mm(pdt[:, tt_ * 16:(tt_ + 1) * 16], hT[:, k, tt_ * 128:(tt_ + 1) * 128], wdt[:, k, 0:16],
                            k == 0, k == 7, reads=[wdt_r, hT_r[k][tt_ // 4]], writes=pdt_r)
            rep16 = lambda base: tabrp[:, base + l * 16: base + (l + 1) * 16].unsqueeze(1).broadcast_to([128, 16, 16])
            v3 = lambda a: a.rearrange("p (t h) -> p t h", h=16)
            self.tt("dve", v3(tmpa), v3(pdt[:, 0:256]), rep16(RP_DTB), ALU.add, reads=[pdt_r, cR], writes=dt_r)
            self.act(tmpb, tmpa, AF.Abs, reads=dt_r, writes=dt_r)
            self.act(tmpb, tmpb, AF.Exp, reads=dt_r, writes=dt_r, scale=-1.0)
            self.act(tmpb, tmpb, AF.Ln, reads=dt_r, writes=dt_r, bias=1.0)
            self.stt(dtt, tmpa, 0.0, tmpb, ALU.max, ALU.add, reads=dt_r, writes=dt_r)
            self.act(v3(tmpa), rep16(RP_ALOG), AF.Exp, reads=[dt_r, cR], writes=dt_r)
            self.stt(adt, dtt, -1.0, tmpa, ALU.mult, ALU.mult, reads=dt_r, writes=dt_r)
            pac, pac_r = self.psum()
            pal, pal_r = self.psum()
            for c in range(NT):
                cs_ = slice(c * 16, (c + 1) * 16)
                self.mm(pac[:, cs_], U_f, adt[:, cs_], True, True, reads=[cR, dt_r], writes=pac_r)
            for c in range(NT):
                cs_ = slice(c * 16, (c + 1) * 16)
                self.mm(pal[:, cs_], ones_f, adt[:, cs_], True, True, reads=[cR, dt_r], writes=pal_r)
            self.act(eac, pac[:, 0:256], AF.Exp, reads=pac_r, writes=dt_r)
            self.act(cdec, pal[:, 0:256], AF.Exp, reads=pal_r, writes=dt_r)
            self.cp("act", tmpa, pac[:, 0:256], reads=pac_r, writes=dt_r)
            self.tt("dve", tmpb, pal[:, 0:256], tmpa, ALU.subtract, reads=[pal_r, dt_r], writes=dt_r)
            self.act(dsd, tmpb, AF.Exp, reads=dt_r, writes=dt_r)
            self.tap("dt%d" % l, dtt, dt_r)
            self.tap("eac%d" % l, eac, dt_r)

            def conv4(fc, src_w, src_wr, wcol, dst_fn, dst_res_fn):
                self.tt("dve", diag4, ident.unsqueeze(1).broadcast_to([128, 4, 128]),
                        tabfm[:, FM_CW + (l * 12 + fc) * 4: FM_CW + (l * 12 + fc) * 4 + 4].unsqueeze(2).broadcast_to([128, 4, 128]),
                        ALU.mult, reads=cR, writes=diag4_r)
                for tg in range(NG):
                    ts = slice(tg * 512, (tg + 1) * 512)
                    pb, pr = self.psum()
                    for k in range(8):
                        self.mm(pb, src_w[:, k, wcol:wcol + 128], hT[:, k, ts], k == 0, k == 7,
                                reads=[src_wr, hT_r[k][tg]], writes=pr)
                    self.cp("act", upad[:, 3 + tg * 512: 3 + (tg + 1) * 512], pb, reads=pr, writes=upad_r)
                for tg in range(NG):
                    pb, pr = self.psum()
                    for tap_ in range(4):
                        self.mm(pb, diag4[:, tap_, :], upad[:, tg * 512 + tap_: tg * 512 + tap_ + 512],
                                tap_ == 0, tap_ == 3, reads=[diag4_r, upad_r], writes=pr)
                    self.act(dst_fn(tg), pb, AF.Silu, reads=[pr, cR], writes=dst_res_fn(tg),
                             bias=fmcol(FM_CB, l * 12 + fc))

            wbc, wbc_r = self.wload(w_in_d[l], 0, 128, 8, C_B, 512)
            for j in range(4):
                dst_t, dst_r = (BT, BT_r) if j < 2 else (CT, CT_r)
                g = j % 2
                conv4(8 + j, wbc, wbc_r, j * 128,
                      lambda tg, dst_t=dst_t, g=g: dst_t[:, g, tg * 512:(tg + 1) * 512],
                      lambda tg, dst_r=dst_r, g=g: dst_r[g][tg])
            for g in range(2):
                for t4 in range(4):
                    pb, pr = self.psum()
                    pbb = pb.bitcast(BF16)
                    for i4 in range(4):
                        tt_ = t4 * 4 + i4
                        self.tr(pbb[:, i4 * 128:(i4 + 1) * 128], BT[:, g, tt_ * 128:(tt_ + 1) * 128], ident,
                                reads=[BT_r[g][t4], cR], writes=pr)
                    self.cp("act", Btok[:, g, t4 * 4:(t4 + 1) * 4, :],
                            pbb[:, 0:512].rearrange("p (a b) -> p a b", a=4), reads=pr, writes=Btok_r[g])
                for t4 in range(4):
                    pb, pr = self.psum()
                    for i4 in range(4):
                        c = t4 * 4 + i4
                        self.mm(pb[:, i4 * 128:(i4 + 1) * 128], BT[:, g, c * 128:(c + 1) * 128],
                                CT[:, g, c * 128:(c + 1) * 128], True, True,
                                reads=[BT_r[g][t4], CT_r[g][t4]], writes=pr)
                    self.tt("dve", cbm[:, g, t4 * 4:(t4 + 1) * 4, :], pb.rearrange("p (a b) -> p a b", a=4),
                            UT_b.unsqueeze(1).broadcast_to([128, 4, 128]), ALU.mult, reads=[pr, cR], writes=cbm_r[g])
            self.tap("BT%d" % l, BT, BT_r)
            self.tap("cbm%d" % l, cbm, cbm_r)

            wxs = [None, None]
            wz = [None, None]
            for hp in range(8):
                g = hp // 4
                r0 = 2 * hp
                if hp % 4 == 0:
                    wxs_t, wxs_r = self.wload(w_in_d[l], 0, 128, 8, C_XS + (hp // 4) * 512, 512)
                conv4(hp, wxs_t, wxs_r, (hp % 4) * 128,
                      lambda tg: xsT[:, tg * 512:(tg + 1) * 512], lambda tg: xsT_r[tg])
                for t4 in range(4):
                    pb, pr = self.psum()
                    pbb = pb.bitcast(BF16)
                    for i4 in range(4):
                        tt_ = t4 * 4 + i4
                        self.tr(pbb[:, i4 * 128:(i4 + 1) * 128], xsT[:, tt_ * 128:(tt_ + 1) * 128], ident,
                                reads=[xsT_r[t4], cR], writes=pr)
                    self.cp("act", xs_tok[:, t4 * 4:(t4 + 1) * 4, :],
                            pbb[:, 0:512].rearrange("p (a b) -> p a b", a=4), reads=pr, writes=xs_tok_r)
                v4 = lambda a: a.rearrange("p t (h d) -> p t h d", h=2)
                hb = lambda a: a.rearrange("p (t h) -> p t h", h=16)[:, :, r0:r0 + 2].unsqueeze(3).broadcast_to([128, NT, 2, 64])
                self.tt("dve", v4(xdt), v4(xs_tok), hb(dtt), ALU.mult, reads=[xs_tok_r, dt_r], writes=xdt_r)
                self.tt("dve", v4(xw), v4(xdt), hb(dsd), ALU.mult, reads=[xdt_r, dt_r], writes=xw_r)
                dcol = tabrp[:, RP_D + l * 16 + r0: RP_D + l * 16 + r0 + 2]
                self.tt("dve", v4(xsD), v4(xs_tok),
                        dcol.unsqueeze(1).unsqueeze(3).broadcast_to([128, NT, 2, 64]), ALU.mult,
                        reads=[xs_tok_r, cR], writes=xsD_r)
                self.op("dve", lambda e: e.memset(Sst, 0.0), writes=S_r)

                def stage1(c):
                    pseg, pseg_r = self.psum()
                    bb = c % 2
                    col0 = c * 16 + r0
                    self.tt("dve", Ab[bb], L_f.unsqueeze(1).broadcast_to([128, 2, 128]),
                            adt[:, col0:col0 + 2].unsqueeze(2).broadcast_to([128, 2, 128]), ALU.mult,
                            reads=[cR, dt_r], writes=A_rs[bb])
                    for j in range(2):
                        self.mm(pseg[:, j * 128:(j + 1) * 128], Ab[bb][:, j, :], U_f, True, True,
                                reads=[A_rs[bb], cR], writes=pseg_r)
                    self.act(Eb[bb], pseg[:, 0:256].rearrange("p (a b) -> p a b", a=2), AF.Exp,
                             reads=pseg_r, writes=E_rs[bb])

                def stage2(c):
                    bb = c % 2
                    if c < NT - 1:
                        pst, pst_r = self.psum()
                        self.mm(pst[:, 0:128], Btok[:, g, c, :], xw[:, c, :], True, True,
                                reads=[Btok_r[g], xw_r], writes=pst_r)
                        for j in range(2):
                            col = c * 16 + r0 + j
                            self.stt(Sst[:, j * 64:(j + 1) * 64], Sst[:, j * 64:(j + 1) * 64], cdec[:, col:col + 1],
                                     pst[:, j * 64:(j + 1) * 64], ALU.mult, ALU.add,
                                     reads=[S_r, dt_r, pst_r], writes=S_r)
                        self.cp("act", Sb[(c + 1) % 2], Sst, reads=S_r, writes=Sb_rs[(c + 1) % 2])
                    self.tt("dve", Mbuf, Eb[bb], cbm[:, g, c, :].unsqueeze(1).broadcast_to([128, 2, 128]), ALU.mult,
                            reads=[E_rs[bb], cbm_r[g]], writes=M_r)
                    if c > 0:
                        pyo, pyo_r = self.psum()
                        self.mm(pyo[:, 0:128], CT[:, g, c * 128:(c + 1) * 128], Sb[c % 2], True, True,
                                reads=[CT_r[g][c // 4], Sb_rs[c % 2]], writes=pyo_r)
                        for j in range(2):
                            col = c * 16 + r0 + j
                            self.act(yo[:, j * 64:(j + 1) * 64], pyo[:, j * 64:(j + 1) * 64], AF.Identity,
                                     reads=[pyo_r, dt_r], writes=yo_r, scale=eac[:, col:col + 1])
                    pyd, pyd_r = self.psum()
                    self.mm(pyd[:, 0:128], ident, xsD[:, c, :], True, False, reads=[cR, xsD_r], writes=pyd_r)
                    for j in range(2):
                        self.mm(pyd[:, j * 64:(j + 1) * 64], Mbuf[:, j, :], xdt[:, c, j * 64:(j + 1) * 64],
                                False, j == 1, reads=[M_r, xdt_r], writes=pyd_r)
                    if c + 1 < NT:
                        stage1(c + 1)
                    if c > 0:
                        self.tt("dve", ytok[:, c, :], pyd[:, 0:128], yo, ALU.add, reads=[pyd_r, yo_r], writes=ytok_r[c])
                    else:
                        self.cp("act", ytok[:, c, :], pyd[:, 0:128], reads=pyd_r, writes=ytok_r[c])

                stage1(0)
                for c in range(NT):
                    stage2(c)
                if hp == 0:
                    self.tap("ytok%d" % l, ytok, ytok_r)
                if hp % 4 == 0:
                    wz_t, wz_r = self.wload(w_in_d[l], 0, 128, 8, C_Z + (hp // 4) * 512, 512)
                for t4 in range(4):
                    pb, pr = self.psum()
                    for i4 in range(4):
                        tt_ = t4 * 4 + i4
                        for k in range(8):
                            self.mm(pb[:, i4 * 128:(i4 + 1) * 128], hT[:, k, tt_ * 128:(tt_ + 1) * 128],
                                    wz_t[:, k, (hp % 4) * 128:(hp % 4 + 1) * 128], k == 0, k == 7,
                                    reads=[wz_r, hT_r[k][t4]], writes=pr)
                    self.act(zs, pb, AF.Silu, reads=pr, writes=zs_r)
                    ysl = ytok[:, t4 * 4:(t4 + 1) * 4, :]
                    z3 = zs.rearrange("p (a b) -> p a b", a=4)
                    yr_ = [ytok_r[t4 * 4 + i] for i in range(4)]
                    self.tt("dve", ysl, ysl, z3, ALU.mult, reads=[zs_r] + yr_, writes=yr_)
                    self.tt("dve", z3, ysl, ysl, ALU.mult, reads=yr_, writes=zs_r)
                    self.op("dve", lambda e, t4=t4, z3=z3: e.tensor_reduce(out=ssq_hp[:, t4 * 4:(t4 + 1) * 4], in_=z3,
                                                                           axis=AX.X, op=ALU.add),
                            reads=zs_r, writes=ssq_r)
                    self.cp("act", y2h[:, hp % 4, t4 * 4:(t4 + 1) * 4, :], ysl, reads=yr_, writes=y2h_r[hp % 4])
                if hp % 4 == 0:
                    self.cp("dve", ssq_g, ssq_hp, reads=ssq_r, writes=ssq_r)
                else:
                    self.tt("dve", ssq_g, ssq_g, ssq_hp, ALU.add, reads=ssq_r, writes=ssq_r)
                if hp % 4 == 3:
                    self.act(rstd_g, ssq_g, AF.Sqrt, reads=ssq_r, writes=ssq_r, scale=1.0 / 512, bias=EPS)
                    self.op("dve", lambda e: e.reciprocal(rstd_g, rstd_g), reads=ssq_r, writes=ssq_r)
                    for hq in range(4):
                        fc = g * 4 + hq
                        yv = y2h[:, hq, :, :]
                        self.tt("dve", yv, yv, rstd_g.unsqueeze(2).broadcast_to([128, NT, 128]), ALU.mult,
                                reads=[y2h_r[hq], ssq_r], writes=y2h_r[hq])
                        self.tt("dve", xsD, yv, gnorm[:, fc * 128:(fc + 1) * 128].unsqueeze(1).broadcast_to([128, NT, 128]),
                                ALU.mult, reads=[y2h_r[hq], gnorm_r], writes=xsD_r)
                        for t4 in range(4):
                            pb, pr = self.psum()
                            pbb = pb.bitcast(BF16)
                            for i4 in range(4):
                                tt_ = t4 * 4 + i4
                                self.tr(pbb[:, i4 * 128:(i4 + 1) * 128], xsD[:, tt_, :], ident,
                                        reads=[xsD_r, cR], writes=pr)
                            self.cp("act", ySSD[:, fc, t4 * 512:(t4 + 1) * 512], pbb[:, 0:512],
                                    reads=pr, writes=ySSD_r[fc][t4])
            self.tap("ySSD%d" % l, ySSD, ySSD_r)
            if self.stop_after == "ssd" and l == STOP_LAYER:
                break

            self.fence(ax_new, xT_r)
            merge(l, 0, 8, 128, lambda kc, ts: ySSD[:, kc, ts], lambda kc, tg: ySSD_r[kc][tg],
                  lambda ct: self.wload(w_br_d[l], 0, 128, 8, ct * 512, 512), True)
            self.tap("m0_%d" % l, xT, xT_r)
            if self.stop_after == "ssdmerge" and l == STOP_LAYER:
                break

            AY_.reset()
            AT_.reset()
            yAT = AY_.alloc([128, 4, S], BF16)
            kT = AY_.alloc([128, 2, S], BF16)
            yAT_r, kT_r = rgrid(NT), rgrid(NT)
            self.fence(self.ay_live, [yAT_r, kT_r])
            self.ay_live = [yAT_r, kT_r]
            vpad = AT_.alloc([128, NT, 2, 128], BF16)
            qTt = AT_.alloc([128, 2, 8, 128], BF16)
            qsq = AT_.alloc([128, 512], F32)
            qn = AT_.alloc([128, 512], F32)
            t1 = AT_.alloc([128, 512], F32)
            qr = AT_.alloc([128, 512], BF16)
            kr = AT_.alloc([128, 128], BF16)
            Pb = AT_.alloc([128, 4, 512], BF16)
            den = AT_.alloc([128, 512], F32)
            st8 = AT_.alloc([128, 8], F32)
            skx = AT_.alloc([128, 4], F32)
            vpad_r, qTt_r = rgrid(NT), rgrid(2)
            qw_r, P_r, den_r, st_r, skx_r, kr_r = R(), rgrid(4), R(), R(), R(), R()
            at_new = [vpad_r, qTt_r, qw_r, P_r, den_r, st_r, skx_r, kr_r]
            self.fence(self.at_live, at_new)
            self.at_live = at_new
            self.op("dve", lambda e: e.memset(vpad, 0.0), writes=vpad_r)
            self.op("dve", lambda e: e.memset(kT[64:128, :, :], 0.0), writes=kT_r)
            self.op("dve", lambda e: e.memset(qTt[64:128, :, :, :], 0.0), writes=qTt_r)
            self.act(skx, tabrp[:, RP_SINK + l * 4: RP_SINK + l * 4 + 4], AF.Exp, reads=cR, writes=skx_r)
            wq, wq_r = self.wload(w_in_d[l], 0, 128, 8, C_Q, 512)
            wkv, wkv_r = self.wload(w_in_d[l], 0, 128, 8, C_K, 256)
            def att_a(i):
                tsl = slice(i * 128, (i + 1) * 128)
                qb = i % 2
                pq, pq_r = self.psum()
                for k in range(8):
                    self.mm(pq, hT[:, k, tsl], wq[:, k, 0:512], k == 0, k == 7, reads=[wq_r, hT_r[k][i // 4]], writes=pq_r)
                pkv, pkv_r = self.psum()
                for k in range(8):
                    self.mm(pkv[:, 0:256], hT[:, k, tsl], wkv[:, k, 0:256], k == 0, k == 7,
                            reads=[wkv_r, hT_r[k][i // 4]], writes=pkv_r)

                def normrope(src, nh, gcol, dst, dst_r, src_r):
                    w = nh * 64
                    self.act(qsq[:, 0:w], src, AF.Square, reads=src_r, writes=qw_r)
                    self.op("dve", lambda e: e.tensor_reduce(out=st8[:, 0:nh], in_=qsq[:, 0:w].rearrange("p (h d) -> p h d", d=64),
                                                              axis=AX.X, op=ALU.add), reads=qw_r, writes=st_r)
                    self.act(st8[:, 0:nh], st8[:, 0:nh], AF.Sqrt, reads=st_r, writes=st_r, scale=1.0 / 64, bias=EPS)
                    self.op("dve", lambda e: e.reciprocal(st8[:, 0:nh], st8[:, 0:nh]), reads=st_r, writes=st_r)
                    h3 = lambda a: a.rearrange("p (h d) -> p h d", d=64)
                    self.tt("dve", h3(qn[:, 0:w]), h3(src), st8[:, 0:nh].unsqueeze(2).broadcast_to([128, nh, 64]), ALU.mult,
                            reads=[src_r, st_r], writes=qw_r)
                    self.tt("dve", h3(qn[:, 0:w]), h3(qn[:, 0:w]),
                            tabrp[:, gcol + l * 64: gcol + (l + 1) * 64].unsqueeze(1).broadcast_to([128, nh, 64]), ALU.mult,
                            reads=[qw_r, cR], writes=qw_r)
                    h4 = lambda a: a.rearrange("p (h a d) -> p h a d", a=2, d=32)
                    self.tt("dve", h4(t1[:, 0:w]), h4(qn[:, 0:w]),
                            ropec[:, i, :].unsqueeze(1).unsqueeze(1).broadcast_to([128, nh, 2, 32]), ALU.mult,
                            reads=[qw_r, cR], writes=qw_r)
                    for a in range(2):
                        self.tt("dve", h4(qsq[:, 0:w])[:, :, a, :], h4(qn[:, 0:w])[:, :, 1 - a, :],
                                ropes[:, i, a, :].unsqueeze(1).broadcast_to([128, nh, 32]), ALU.mult,
                                reads=[qw_r, cR], writes=qw_r)
                    self.tt("dve", dst, t1[:, 0:w], qsq[:, 0:w], ALU.add, reads=qw_r, writes=[qw_r, dst_r])

                if ATT_LEVEL < 2:
                    return
                normrope(pq, 8, RP_QG, qr, qw_r, pq_r)
                if ATT_LEVEL < 3:
                    return
                pb, pr = self.psum()
                pbb = pb.bitcast(BF16)
                for hd in range(8):
                    self.tr(pbb[0:64, hd * 128:(hd + 1) * 128], qr[:, hd * 64:(hd + 1) * 64], ident, reads=[qw_r, cR], writes=pr)
                self.cp("act", qTt[0:64, qb, :, :], pbb[0:64, 0:1024].rearrange("p (a b) -> p a b", a=8),
                        reads=pr, writes=qTt_r[qb])
                if ATT_LEVEL < 4:
                    return
                for h in range(2):
                    self.cp("act", vpad[:, i, h, h * 64:(h + 1) * 64], pkv[:, 128 + h * 64: 192 + h * 64],
                            reads=pkv_r, writes=vpad_r[i])
                normrope(pkv[:, 0:128], 2, RP_KG, kr, kr_r, pkv_r)
                pb, pr = self.psum()
                pbb = pb.bitcast(BF16)
                for h in range(2):
                    self.tr(pbb[0:64, h * 128:(h + 1) * 128], kr[:, h * 64:(h + 1) * 64], ident, reads=[kr_r, cR], writes=pr)
                self.cp("act", kT[0:64, :, tsl], pbb[0:64, 0:256].rearrange("p (a b) -> p a b", a=2), reads=pr, writes=kT_r[i])
            def att_b(i):
                tsl = slice(i * 128, (i + 1) * 128)
                qb = i % 2
                blks = [i] if i == 0 else [i - 1, i]
                pidx = []
                for h in range(2):
                    for blk in blks:
                        bs = slice(blk * 128, (blk + 1) * 128)
                        pi = len(pidx)
                        pidx.append((h, blk, pi))
                        ps_, ps_r = self.psum()
                        self.mm(ps_.rearrange("p (a b) -> p a b", a=4), kT[:, h, bs], qTt[:, qb, 4 * h:4 * h + 4, :],
                                True, True, reads=[kT_r[blk], qTt_r[qb]], writes=ps_r)
                        self.act(Pb[:, pi, :], ps_, AF.Exp, reads=ps_r, writes=P_r[pi], scale=0.125)
                        self.tt("dve", Pb[:, pi, :], Pb[:, pi, :], mcur if blk == i else mprev, ALU.mult,
                                reads=[P_r[pi], cR], writes=P_r[pi])
                if ATT_LEVEL < 6:
                    return
                pnum, pnum_r = self.psum()
                pden, pden_r = self.psum()
                n = len(pidx)
                for q_, (h, blk, pi) in enumerate(pidx):
                    self.mm(pnum, vpad[:, blk, h, :], Pb[:, pi, :], q_ == 0, q_ == n - 1,
                            reads=[vpad_r[blk], P_r[pi]], writes=pnum_r)
                for q_, (h, blk, pi) in enumerate(pidx):
                    self.mm(pden, onesLR[h], Pb[:, pi, :], q_ == 0, q_ == n - 1, reads=[cR, P_r[pi]], writes=pden_r)
                d3 = lambda a: a.rearrange("p (r q) -> p r q", r=4)
                self.tt("dve", d3(den), d3(pden), skx.unsqueeze(2).broadcast_to([128, 4, 128]), ALU.add,
                        reads=[pden_r, skx_r], writes=den_r)
                self.op("dve", lambda e: e.reciprocal(den, den), reads=den_r, writes=den_r)
                self.tt("dve", yAT[:, :, tsl], d3(pnum), d3(den), ALU.mult, reads=[pnum_r, den_r], writes=yAT_r[i])
            att_a(0)
            for i in range(NT):
                if i + 1 < NT:
                    att_a(i + 1)
                att_b(i)
            self.tap("yAT%d" % l, yAT, yAT_r)
            if self.stop_after == "attn" and l == STOP_LAYER:
                break

            def wload_attn(ct):
                i_ = self.w_i
                self.w_i = (i_ + 1) % NW
                buf, res = self.wbufs[i_], self.wres[i_]
                for h in range(2):
                    src = w_br_d[l][1024 + h * 256:1024 + (h + 1) * 256, ct * 512:(ct + 1) * 512].rearrange(
                        "(r d) c -> d r c", d=64)
                    dst = buf[h * 64:(h + 1) * 64, 0:4, :]
                    self.op("pool", lambda e, dst=dst, src=src: e.dma_start(out=dst, in_=src), writes=res, dma=True)
                return buf, res

            merge(l, 1, 4, 128, lambda kc, ts: yAT[:, kc, ts], lambda kc, tg: [yAT_r[tg * 4 + i] for i in range(4)],
                  wload_attn, False)
            self.tap("m1_%d" % l, xT, xT_r)
            if self.stop_after == "attnmerge" and l == STOP_LAYER:
                break

            AY_.reset()
            AT_.reset()
            yPT = AY_.alloc([128, 4, S], BF16)
            yPT_r = rgrid(4, NG)
            self.fence(self.ay_live, yPT_r)
            self.ay_live = [yPT_r]
            PADW = 8
            pa = AT_.alloc([128, PADW + S], F32)
            pbuf = AT_.alloc([128, PADW + S], F32)
            ub = AT_.alloc([128, 16 + S], F32)
            pooled = AT_.alloc([128, S], BF16)
            inv16 = AT_.alloc([128, 16], F32)
            pa_r, pb_r, ub_r, pooled_r, inv_r = R(), R(), R(), rgrid(NG), R()
            at_new = [pa_r, pb_r, ub_r, pooled_r, inv_r]
            self.fence(self.at_live, at_new)
            self.at_live = at_new
            self.op("dve", lambda e: e.memset(pa[:, 0:PADW], 0.0), writes=pa_r)
            self.op("dve", lambda e: e.memset(pbuf[:, 0:PADW], 0.0), writes=pb_r)
            self.op("dve", lambda e: e.memset(ub[:, 0:16], 0.0), writes=ub_r)
            pcs, pcs_r = self.psum()
            self.mm(pcs[:, 0:128], ones_f, U_f, True, True, reads=cR, writes=pcs_r)
            self.op("dve", lambda e, pcs=pcs, inv16=inv16: e.reciprocal(inv16, pcs[:, 0:16]), reads=pcs_r, writes=inv_r)
            wpl, wpl_r = self.wload(w_in_d[l], 0, 128, 8, C_POOL, 512)
            for gi in range(4):
                w_ = (2, 4, 8, 16)[gi]
                for tg in range(NG):
                    ts = slice(tg * 512, (tg + 1) * 512)
                    pb_, pr = self.psum()
                    for k in range(8):
                        self.mm(pb_, wpl[:, k, gi * 128:(gi + 1) * 128], hT[:, k, ts], k == 0, k == 7,
                                reads=[wpl_r, hT_r[k][tg]], writes=pr)
                    self.cp("act", ub[:, 16 + tg * 512: 16 + (tg + 1) * 512], pb_, reads=pr, writes=ub_r)
                u_ = ub[:, 16:16 + S]
                src, src_r, sh = ub, ub_r, 1
                srcoff = 16
                bufs = [(pa, pa_r), (pbuf, pb_r)]
                bi = 0
                while sh < w_:
                    dstb, dst_r = bufs[bi]
                    bi ^= 1
                    self.tt("dve", dstb[:, PADW:PADW + S], src[:, srcoff:srcoff + S], src[:, srcoff - sh:srcoff - sh + S],
                            ALU.add, reads=src_r, writes=dst_r)
                    src, src_r, srcoff = dstb, dst_r, PADW
                    sh *= 2
                sw = src[:, srcoff:srcoff + S]
                for tg in range(NG):
                    ts = slice(tg * 512, (tg + 1) * 512)
                    self.stt(pooled[:, ts], sw[:, ts], 1.0 / w_, u_[:, ts], ALU.mult, ALU.subtract,
                             reads=[src_r, ub_r], writes=pooled_r[tg])
                self.tt("dve", pa[:, 0:w_ - 1], sw[:, 0:w_ - 1], inv16[:, 0:w_ - 1], ALU.mult,
                        reads=[src_r, inv_r], writes=pa_r)
                self.tt("dve", pooled[:, 0:w_ - 1], pa[:, 0:w_ - 1], u_[:, 0:w_ - 1], ALU.subtract,
                        reads=[pa_r, ub_r, pooled_r[0]], writes=pooled_r[0])
                self.op("dve", lambda e: e.memset(pa[:, 0:PADW], 0.0), reads=pa_r, writes=pa_r)
                for tg in range(NG):
                    ts = slice(tg * 512, (tg + 1) * 512)
                    pb_, pr = self.psum()
                    self.mm(pb_, poolw[:, l * 4 + gi, :], pooled[:, ts], True, True, reads=[cR, pooled_r[tg]], writes=pr)
                    self.act(yPT[:, gi, ts], pb_, AF.Identity, reads=[pr, cR], writes=yPT_r[gi][tg],
                             scale=fmcol(FM_PSC, l * 4 + gi))
            self.tap("yPT%d" % l, yPT, yPT_r)
            merge(l, 2, 4, 128, lambda kc, ts: yPT[:, kc, ts], lambda kc, tg: yPT_r[kc][tg],
                  lambda ct: self.wload(w_br_d[l], 1536, 128, 4, ct * 512, 512), False)
            self.tap("m2_%d" % l, xT, xT_r)
            if self.stop_after == "pool" and l == STOP_LAYER:
                break

            AY_.reset()
            AT_.reset()
            yCT = AY_.alloc([128, 4, S], BF16)
            cv = AY_.alloc([128, 4, S], BF16)
            yCT_r, cv_r = rgrid(4, NG), rgrid(4, NG)
            self.fence(self.ay_live, [yCT_r, cv_r])
            self.ay_live = [yCT_r, cv_r]
            up31 = AT_.alloc([128, 30 + S], BF16)
            diag31 = AT_.alloc([128, 31, 128], BF16)
            sg = AT_.alloc([128, 512], F32)
            mean = AT_.alloc([128, 512], F32)
            rstd2 = AT_.alloc([128, 512], F32)
            tn = AT_.alloc([128, 512], F32)
            cvsq = AT_.alloc([128, 512], BF16)
            up_r, dg_r, sg_r, mean_r, rs_r, tn_r, cvsq_r = R(), R(), R(), R(), R(), R(), R()
            at_new = [up_r, dg_r, sg_r, mean_r, rs_r, tn_r, cvsq_r]
            self.fence(self.at_live, at_new)
            self.at_live = at_new
            self.op("dve", lambda e: e.memset(up31[:, 0:30], 0.0), writes=up_r)
            for half in range(2):
                wa, wa_r = self.wload(w_in_d[l], 0, 128, 8, C_CONV + half * 256, 256)
                wg_, wg_r_ = self.wload(w_in_d[l], 0, 128, 8, C_CONV + 512 + half * 256, 256)
                for jj in range(2):
                    j = half * 2 + jj
                    cs2 = slice(jj * 128, (jj + 1) * 128)
                    for tg in range(NG):
                        ts = slice(tg * 512, (tg + 1) * 512)
                        pa_, par = self.psum()
                        pg_, pgr = self.psum()
                        for k in range(8):
                            self.mm(pg_, wg_[:, k, cs2], hT[:, k, ts], k == 0, k == 7, reads=[wg_r_, hT_r[k][tg]], writes=pgr)
                        for k in range(8):
                            self.mm(pa_, wa[:, k, cs2], hT[:, k, ts], k == 0, k == 7, reads=[wa_r, hT_r[k][tg]], writes=par)
                        self.act(sg, pg_, AF.Sigmoid, reads=pgr, writes=sg_r)
                        self.tt("dve", up31[:, 30 + tg * 512: 30 + (tg + 1) * 512], pa_, sg, ALU.mult,
                                reads=[par, sg_r], writes=up_r)
                    self.tt("dve", diag31, ident.unsqueeze(1).broadcast_to([128, 31, 128]),
                            tabfm[:, FM_DWW + (l * 4 + j) * 31: FM_DWW + (l * 4 + j + 1) * 31].unsqueeze(2).broadcast_to([128, 31, 128]),
                            ALU.mult, reads=cR, writes=dg_r)
                    for tg in range(NG):
                        ts = slice(tg * 512, (tg + 1) * 512)
                        pb_, pr = self.psum()
                        for tap_ in range(31):
                            self.mm(pb_, diag31[:, tap_, :], up31[:, tg * 512 + tap_: tg * 512 + tap_ + 512],
                                    tap_ == 0, tap_ == 30, reads=[dg_r, up_r], writes=pr)
                        self.act(cv[:, j, ts], pb_, AF.Identity, reads=[pr, cR], writes=cv_r[j][tg],
                                 bias=fmcol(FM_DWB, l * 4 + j))
            for tg in range(NG):
                ts = slice(tg * 512, (tg + 1) * 512)
                pm, pm_r = self.psum()
                pq2, pq2_r = self.psum()
                for j in range(4):
                    self.mm(pm, ones_bf, cv[:, j, ts], j == 0, j == 3, reads=[cR, cv_r[j][tg]], writes=pm_r)
                for j in range(4):
                    self.act(cvsq, cv[:, j, ts], AF.Square, reads=cv_r[j][tg], writes=cvsq_r)
                    self.mm(pq2, ones_bf, cvsq, j == 0, j == 3, reads=[cR, cvsq_r], writes=pq2_r)
                self.act(mean, pm, AF.Identity, reads=pm_r, writes=mean_r, scale=1.0 / 512)
                self.tt("dve", tn, mean, mean, ALU.mult, reads=mean_r, writes=tn_r)
                self.stt(rstd2, pq2, 1.0 / 512, tn, ALU.mult, ALU.subtract, reads=[pq2_r, tn_r], writes=rs_r)
                self.act(rstd2, rstd2, AF.Sqrt, reads=rs_r, writes=rs_r, bias=EPS)
                self.op("dve", lambda e: e.reciprocal(rstd2, rstd2), reads=rs_r, writes=rs_r)
                for j in range(4):
                    self.tt("dve", tn, cv[:, j, ts], mean, ALU.subtract, reads=[cv_r[j][tg], mean_r], writes=tn_r)
                    self.tt("dve", tn, tn, rstd2, ALU.mult, reads=[tn_r, rs_r], writes=tn_r)
                    self.act(yCT[:, j, ts], tn, AF.Silu, reads=[tn_r, cR], writes=yCT_r[j][tg],
                             scale=fmcol(FM_LNG, l * 4 + j), bias=fmcol(FM_LNB, l * 4 + j))
            self.tap("yCT%d" % l, yCT, yCT_r)
            merge(l, 3, 4, 128, lambda kc, ts: yCT[:, kc, ts], lambda kc, tg: yCT_r[kc][tg],
                  lambda ct: self.wload(w_br_d[l], 2048, 128, 4, ct * 512, 512), False)
            self.tap("m3_%d" % l, xT, xT_r)
            if self.stop_after == "conv" and l == STOP_LAYER:
                break

            for k in range(8):
                for tg in range(NG):
                    ts = slice(tg * 512, (tg + 1) * 512)
                    self.cp("act" if (k + tg) % 2 else "dve", hT[:, k, ts], xT[:, k, ts],
                            reads=xT_r[k][tg], writes=hT_r[k][tg])
            for k in range(8):
                self.op("sp", lambda e, k=k, x_src=x_src: e.dma_start(out=xT[:, k, :], in_=x_src[:, k, :]),
                        reads=xs_r[k], writes=xT_r[k], dma=True)
            for ct in range(2):
                wo, wo_r = self.wload(w_out_d[l], 0, 128, 8, ct * 512, 512)
                for f4 in range(4):
                    fo = ct * 4 + f4
                    for tg in range(NG):
                        ts = slice(tg * 512, (tg + 1) * 512)
                        pb_, pr = self.psum()
                        for k in range(8):
                            self.mm(pb_, wo[:, k, f4 * 128:(f4 + 1) * 128], hT[:, k, ts], k == 0, k == 7,
                                    reads=[wo_r, hT_r[k][tg]], writes=pr)
                        self.tt("dve", xT[:, fo, ts], xT[:, fo, ts], pb_, ALU.add, reads=[pr, xT_r[fo][tg]],
                                writes=xT_r[fo][tg])
            self.tap("xmid%d" % l, xT, xT_r)
            if self.stop_after == "wout" and l == STOP_LAYER:
                break

            rmsnorm(FM_GMLP + l * 8)
            AY_.reset()
            AT_.reset()
            aT = AY_.alloc([128, 8, S], BF16)
            aT_r = rgrid(8, NG)
            self.fence(self.ay_live, aT_r)
            self.ay_live = [aT_r]
            rl = [AT_.alloc([128, 512], F32) for _ in range(2)]
            rl_r = [R(), R()]
            self.fence(self.at_live, rl_r)
            self.at_live = rl_r
            for hb in range(4):
                for ct in range(2):
                    wu, wu_r = self.wload(w_up_d[l], 0, 128, 8, hb * 1024 + ct * 512, 512)
                    for f4 in range(4):
                        fc = ct * 4 + f4
                        for tg in range(NG):
                            ts = slice(tg * 512, (tg + 1) * 512)
                            pb_, pr = self.psum()
                            for k in range(8):
                                self.mm(pb_, wu[:, k, f4 * 128:(f4 + 1) * 128], hT[:, k, ts], k == 0, k == 7,
                                        reads=[wu_r, hT_r[k][tg]], writes=pr)
                            ri = (fc * NG + tg) % 2
                            self.act(rl[ri], pb_, AF.Relu, reads=pr, writes=rl_r[ri])
                            self.tt("dve", aT[:, fc, ts], rl[ri], rl[ri], ALU.mult, reads=rl_r[ri], writes=aT_r[fc][tg])
                for ct in range(2):
                    wd, wd_r = self.wload(w_dn_d[l], hb * 1024, 128, 8, ct * 512, 512)
                    for f4 in range(4):
                        fo = ct * 4 + f4
                        for tg in range(NG):
                            ts = slice(tg * 512, (tg + 1) * 512)
                            pb_, pr = self.psum()
                            for k in range(8):
                                self.mm(pb_, wd[:, k, f4 * 128:(f4 + 1) * 128], aT[:, k, ts], k == 0, k == 7,
                                        reads=[wd_r, aT_r[k][tg]], writes=pr)
                            self.tt("dve", xT[:, fo, ts], xT[:, fo, ts], pb_, ALU.add, reads=[pr, xT_r[fo][tg]],
                                    writes=xT_r[fo][tg])
            self.tap("xout%d" % l, xT, xT_r)

        out_v = out_d.rearrange("(k p) t -> p k t", p=128)
        for k in range(8):
            self.op("sp", lambda e, k=k: e.dma_start(out=out_v[:, k, :], in_=xT[:, k, :]), reads=xT_r[k], dma=True)
        sc.finalize()
        fw = [(cs, sc.counts[cs]) for cs in ("d_sp", "d_pool", "d_act") if sc.counts.get(cs, 0)]
        sems = {}
        for cs, n in sc.counts.items():
            sg = SEG_DMA if cs.startswith("d_") else SEG
            sems[cs] = [self.es.enter_context(nc.semaphore("s_%s_%d" % (cs, i))) for i in range((n + sg - 1) // sg)]
        with nc.Block() as block:
            sc.emit(nc, block, sems, fw)


def build_program(debug=None, nlayers=DEPTH, stop_after=None):
    b = Builder(debug=debug, nlayers=nlayers, stop_after=stop_after)
    nc = b.build()
    return nc, b


def _fm(v, nchunks):
    v = np.asarray(v, np.float32)
    return np.ascontiguousarray(v.reshape(L, nchunks, 128).transpose(2, 0, 1).reshape(128, L * nchunks))


def _rep(v):
    v = np.asarray(v, np.float32).reshape(1, -1)
    return np.ascontiguousarray(np.repeat(v, 128, axis=0))


def make_in_maps(inputs, ncores=8):
    import ml_dtypes
    f = lambda k: np.asarray(inputs[k], np.float32)
    x = f("x")
    cw = f("ssd_conv_w")
    cw_t = cw.reshape(L, 4, 12, 128).transpose(3, 0, 2, 1).reshape(128, L * 48)
    dww = f("conv_dw_w")
    dww_t = dww.reshape(L, 31, 4, 128).transpose(3, 0, 2, 1).reshape(128, L * 124)
    tab_fm = np.concatenate([
        _fm(f("norm_mix_g"), 8), _fm(f("norm_mlp_g"), 8), _fm(f("gate_b"), 32), cw_t,
        _fm(f("ssd_conv_b"), 12), _fm(f("pool_scale"), 4), dww_t, _fm(f("conv_dw_b"), 4),
        _fm(f("conv_ln_g"), 4), _fm(f("conv_ln_b"), 4)], axis=1).astype(np.float32)
    assert tab_fm.shape == (128, NFM), tab_fm.shape
    sk = f("attn_sinks")
    sk_t = np.zeros((128, L * 4), np.float32)
    for l in range(L):
        sk_t[0:64, l * 4:(l + 1) * 4] = sk[l, 0:4][None, :]
        sk_t[64:128, l * 4:(l + 1) * 4] = sk[l, 4:8][None, :]
    tab_rp = np.concatenate([_rep(f("ssd_dt_bias")), _rep(f("ssd_a_log")), _rep(f("ssd_d")),
                             _rep(f("q_norm_g")), _rep(f("k_norm_g")), sk_t], axis=1).astype(np.float32)
    assert tab_rp.shape == (128, NRP), tab_rp.shape
    gnorm_rep = np.ascontiguousarray(np.repeat(f("ssd_norm_g")[:, None, :], 128, axis=1))
    inv = (1.0 / (10000.0 ** (np.arange(0, 64, 2, dtype=np.float32) / np.float32(64.0)))).astype(np.float32)
    ang = (np.arange(S, dtype=np.float32)[:, None] * inv[None, :]).astype(np.float32)
    cos = np.cos(ang).astype(np.float32).reshape(NT, 128, 32).transpose(1, 0, 2)
    sin = np.sin(ang).astype(np.float32).reshape(NT, 128, 32).transpose(1, 0, 2)
    rope_c = np.ascontiguousarray(cos.reshape(128, NT * 32))
    rope_s2 = np.ascontiguousarray(np.stack([-sin, sin], axis=2).reshape(128, NT * 64))
    k = np.arange(128)
    U = (k[:, None] <= k[None, :]).astype(np.float32)
    Lm = (k[:, None] > k[None, :]).astype(np.float32)
    c_f32 = np.concatenate([U, Lm, np.ones((128, 128), np.float32)], axis=1)
    mcur = np.tile(U, (1, 4))
    mprev = np.tile(Lm, (1, 4))
    onl = np.zeros((128, 128), np.float32)
    onl[:, 0:64] = 1.0
    onr = np.zeros((128, 128), np.float32)
    onr[:, 64:128] = 1.0
    c_bf = np.concatenate([np.eye(128, dtype=np.float32), U, mcur, mprev, onl, onr], axis=1).astype(ml_dtypes.bfloat16)
    assert c_bf.shape == (128, NCB)
    pw = f("pool_w")
    pool_w = np.ascontiguousarray(pw.transpose(2, 0, 1, 3).reshape(128, L * 4 * 128))
    shared = {
        "w_in": f("w_in"), "w_branch": f("w_branch"), "w_out": f("w_out"),
        "w_up": f("w_mlp_up"), "w_down": f("w_mlp_down"),
        "tab_fm": tab_fm, "tab_rp": tab_rp, "gnorm_rep": gnorm_rep,
        "rope_c": rope_c, "rope_s2": rope_s2, "c_f32": c_f32, "c_bf": c_bf, "pool_w": pool_w,
    }
    maps = []
    for c in range(ncores):
        m = dict(shared)
        m["xT"] = np.ascontiguousarray(x[c].T)
        maps.append(m)
    return maps


def kernel(**inputs):
    nc, b = build_program()
    in_maps = make_in_maps(inputs)
    res = run_bass_kernel_spmd(nc, in_maps, core_ids=list(range(8)))
    out = np.stack([np.ascontiguousarray(r["yT"].T) for r in res.results], axis=0)
    return out.astype(np.float32)
```

```python
import numpy as np
from contextlib import ExitStack
import concourse.bass as bass
import concourse.mybir as mybir
from concourse.bass_utils import run_bass_kernel_spmd

F32 = mybir.dt.float32
BF16 = mybir.dt.bfloat16
ALU = mybir.AluOpType
AF = mybir.ActivationFunctionType
AX = mybir.AxisListType

D = 1024
S = 2048
DEPTH = 2
NT = S // 128
NG = S // 512
EPS = 1e-6
IN_WIDTH = 8976
C_Z, C_XS, C_B, C_C, C_DT, C_Q, C_K, C_V, C_POOL, C_CONV, C_GATE = (
    0, 1024, 2048, 2304, 2560, 2576, 3088, 3216, 3344, 3856, 4880)


class R:
    __slots__ = ("w", "rd")

    def __init__(self):
        self.w = None
        self.rd = {}


def rgrid(*shape):
    if len(shape) == 1:
        return [R() for _ in range(shape[0])]
    return [rgrid(*shape[1:]) for _ in range(shape[0])]


def flat(x):
    if isinstance(x, R):
        return [x]
    out = []
    for e in x:
        out.extend(flat(e))
    return out


import sys


def _where():
    f = sys._getframe(2)
    out = []
    while f is not None and len(out) < 4:
        out.append(f.f_lineno)
        f = f.f_back
    return out


class Op:
    __slots__ = ("eng", "cs", "fn", "deps", "sig", "cnt", "waits", "snap", "where")


ENGS = ("pe", "act", "dve", "pool", "sp")


class Sched:
    def __init__(self):
        self.ops = []

    def add(self, eng, fn, reads=(), writes=(), dma=False):
        op = Op()
        op.eng = eng
        op.cs = ("d_" + eng) if dma else eng
        op.fn = fn
        op.where = _where()
        op.sig = dma
        op.cnt = 0
        deps = {}
        rl = flat(reads)
        wl = flat(writes)
        for r in rl:
            if r.w is not None:
                deps[id(r.w)] = r.w
        for w in wl:
            if w.w is not None:
                deps[id(w.w)] = w.w
            for o in w.rd.values():
                deps[id(o)] = o
        if eng == "pe" and not dma:
            deps = {i: o for i, o in deps.items() if o.cs != "pe"}
        op.deps = list(deps.values())
        for r in rl:
            r.rd[op.cs] = op
        for w in wl:
            w.w = op
            w.rd = {}
        self.ops.append(op)
        return op

    def finalize(self):
        for op in self.ops:
            for d in op.deps:
                d.sig = True
        counts = {}
        for op in self.ops:
            if op.sig:
                counts[op.cs] = counts.get(op.cs, 0) + 1
            op.cnt = counts.get(op.cs, 0)
        known = {e: {} for e in ENGS}
        for op in self.ops:
            kn = known[op.eng]
            need = {}
            for d in op.deps:
                if d.cnt > need.get(d.cs, 0):
                    need[d.cs] = d.cnt
            waits = []
            for d in sorted(op.deps, key=lambda o: -o.cnt):
                if kn.get(d.cs, 0) >= d.cnt:
                    continue
                if need.get(d.cs, 0) != d.cnt:
                    continue
                waits.append((d.cs, d.cnt))
                for c, v in d.snap.items():
                    if kn.get(c, 0) < v:
                        kn[c] = v
                kn[d.cs] = max(kn.get(d.cs, 0), d.cnt)
            op.waits = waits
            if op.sig:
                sn = dict(kn)
                op.snap = sn
            else:
                op.snap = None
        self.counts = counts

    def emit(self, nc, block, sems, final_waits=()):
        by = {e: [] for e in ENGS}
        for op in self.ops:
            by[op.eng].append(op)

        def seg(cs, cnt):
            sg = SEG_DMA if cs.startswith("d_") else SEG
            inc = 16 if cs.startswith("d_") else 1
            i = (cnt - 1) // sg
            return sems[cs][i], (cnt - i * sg) * inc

        def run(e, ops):
            segdone = {}
            for op in ops:
                for cs, cnt in op.waits:
                    if cs.startswith("d_"):
                        s_idx = (cnt - 1) // SEG_DMA
                        for k in range(segdone.get(cs, 0), s_idx):
                            e.wait_ge(sems[cs][k], SEG_DMA * 16)
                        segdone[cs] = max(segdone.get(cs, 0), s_idx)
                    sm, val = seg(cs, cnt)
                    e.wait_ge(sm, val)
                if op.cs.startswith("d_") and op.cnt > MAX_DMA_OUT:
                    sm, val = seg(op.cs, op.cnt - MAX_DMA_OUT)
                    e.wait_ge(sm, val)
                try:
                    ins = op.fn(e)
                except Exception:
                    print("FAILED OP at lines", op.where, op.eng)
                    raise
                if op.sig:
                    inc = 16 if op.cs.startswith("d_") else 1
                    sm, _ = seg(op.cs, op.cnt)
                    ins.then_inc(sm, inc)

        @block.tensor
        def _(e):
            run(e, by["pe"])

        @block.scalar
        def _(e):
            run(e, by["act"])

        @block.vector
        def _(e):
            run(e, by["dve"])

        @block.gpsimd
        def _(e):
            run(e, by["pool"])

        @block.sync
        def _(e):
            run(e, by["sp"])
            for cs, n in final_waits:
                k = 0
                while k * SEG_DMA < n:
                    last = min(n, (k + 1) * SEG_DMA)
                    sm, val = seg(cs, last)
                    e.wait_ge(sm, val)
                    k += 1


import os as _os
SEG = int(_os.environ.get('SEG', '2000'))
SEG_DMA = int(_os.environ.get('SEG_DMA', '1000000'))
MAX_DMA_OUT = int(_os.environ.get('MAX_DMA_OUT', '4'))


import os
ATT_LEVEL = int(os.environ.get('ATT_LEVEL', '9'))
STOP_LAYER = int(os.environ.get('STOP_LAYER', '0'))
NW = 3
L = DEPTH
FM_GMIX, FM_GMLP, FM_GATEB, FM_CW, FM_CB, FM_PSC, FM_DWW, FM_DWB, FM_LNG, FM_LNB = (
    0, L * 8, L * 16, L * 48, L * 96, L * 108, L * 112, L * 236, L * 240, L * 244)
NFM = L * 248
RP_DTB, RP_ALOG, RP_D, RP_QG, RP_KG, RP_SINK = 0, L * 16, L * 32, L * 48, L * 112, L * 176
NRP = L * 180
CB_ID, CB_UT, CB_MCUR, CB_MPREV, CB_ONL, CB_ONR = 0, 128, 256, 768, 1280, 1408
NCB = 1536


class Arena:
    def __init__(self, b, name, nbytes):
        self.t = b.sb(name, [128, nbytes // 4], F32)
        self.n = nbytes
        self.off = 0

    def reset(self):
        self.off = 0

    def alloc(self, shape, dt):
        es = 4 if dt == F32 else 2
        n = 1
        for d in shape[1:]:
            n *= d
        nb = (n * es + 31) // 32 * 32
        off = self.off
        self.off += nb
        assert self.off <= self.n, (self.off, self.n)
        ap = self.t[:, off // 4:(off + nb) // 4]
        if dt != F32:
            ap = ap.bitcast(dt)
        ap = ap[:, :n]
        if len(shape) == 3:
            ap = ap.rearrange("p (a b) -> p a b", a=shape[1])
        elif len(shape) == 4:
            ap = ap.rearrange("p (a b c) -> p a b c", a=shape[1], b=shape[2])
        if shape[0] < 128:
            ap = ap[:shape[0]]
        return ap


class Builder:
    def __init__(self, debug=None, nlayers=DEPTH, stop_after=None):
        self.debug = debug or []
        self.nlayers = nlayers
        self.stop_after = stop_after
        self.nc = bass.Bass("TRN2", target_bir_lowering=False)
        self.sc = Sched()
        self.es = ExitStack()
        self.psum_i = 0
        self.w_i = 0
        self.dbg_d = {}

    def dram_in(self, name, shape, dtype=F32):
        return self.nc.dram_tensor(name, list(shape), dtype, kind="ExternalInput").ap()

    def sb(self, name, shape, dtype=F32):
        return self.es.enter_context(self.nc.sbuf_tensor(name, list(shape), dtype))[:]

    def op(self, eng, fn, reads=(), writes=(), dma=False):
        return self.sc.add(eng, fn, reads, writes, dma)

    def psum(self):
        i = self.psum_i
        self.psum_i = (i + 1) % 8
        return self.pbanks[i], self.pres[i]

    def mm(self, out, lhsT, rhs, start, stop, reads, writes):
        return self.op("pe", lambda e: e.matmul(out, lhsT, rhs, start=start, stop=stop), reads, writes)

    def tr(self, out, in_, ident, reads, writes):
        return self.op("pe", lambda e: e.transpose(out, in_, ident), reads, writes)

    def act(self, out, in_, func, reads, writes, bias=None, scale=None):
        kw = {}
        if bias is not None:
            kw["bias"] = bias
        if scale is not None:
            kw["scale"] = scale
        return self.op("act", lambda e: e.activation(out=out, in_=in_, func=func, **kw), reads, writes)

    def tt(self, eng, out, in0, in1, op, reads, writes):
        return self.op(eng, lambda e: e.tensor_tensor(out=out, in0=in0, in1=in1, op=op), reads, writes)

    def tsc(self, eng, out, in0, s1, s2, op0, op1, reads, writes):
        if s2 is None:
            return self.op(eng, lambda e: e.tensor_scalar(out, in0, s1, None, op0), reads, writes)
        return self.op(eng, lambda e: e.tensor_scalar(out, in0, s1, s2, op0, op1), reads, writes)

    def stt(self, out, in0, scalar, in1, op0, op1, reads, writes):
        return self.op("dve", lambda e: e.scalar_tensor_tensor(out=out, in0=in0, scalar=scalar, in1=in1,
                                                              op0=op0, op1=op1), reads, writes)

    def cp(self, eng, out, in_, reads, writes):
        if eng == "act":
            return self.op("act", lambda e: e.activation(out=out, in_=in_, func=AF.Copy), reads, writes)
        return self.op(eng, lambda e: e.tensor_copy(out=out, in_=in_), reads, writes)

    def fence(self, old, new):
        d = self.dummy
        self.op("dve", lambda e: e.memset(d, 0.0), writes=[old, new])

    def tap(self, name, ap, res):
        for n, shape, dt in self.debug:
            if n == name:
                dst = self.dbg_d[name]
                self.op("sp", lambda e: e.dma_start(out=dst, in_=ap), reads=res, dma=True)

    def wload(self, src2d, r0, kp, nk, c0, ncols):
        i = self.w_i
        self.w_i = (i + 1) % NW
        buf, res = self.wbufs[i], self.wres[i]
        src = src2d[r0:r0 + kp * nk, c0:c0 + ncols].rearrange("(k p) c -> p k c", p=kp)
        dst = buf[:kp, :nk, :ncols]
        self.op("pool", lambda e: e.dma_start(out=dst, in_=src), writes=res, dma=True)
        return buf, res

    def build(self):
        with self.es:
            self._build()
        return self.nc

    def _build(self):
        nc = self.nc
        sc = self.sc
        xT_d = self.dram_in("xT", [D, S])
        w_in_d = self.dram_in("w_in", [L, D, IN_WIDTH])
        w_br_d = self.dram_in("w_branch", [L, 2560, D])
        w_out_d = self.dram_in("w_out", [L, D, D])
        w_up_d = self.dram_in("w_up", [L, D, 4096])
        w_dn_d = self.dram_in("w_down", [L, 4096, D])
        tabfm_d = self.dram_in("tab_fm", [128, NFM])
        tabrp_d = self.dram_in("tab_rp", [128, NRP])
        gnorm_d = self.dram_in("gnorm_rep", [L, 128, 1024])
        ropec_d = self.dram_in("rope_c", [128, 16 * 32])
        ropes_d = self.dram_in("rope_s2", [128, 16 * 64])
        cf32_d = self.dram_in("c_f32", [128, 384])
        cbf_d = self.dram_in("c_bf", [128, NCB], BF16)
        poolw_d = self.dram_in("pool_w", [128, L * 4 * 128])
        out_d = nc.dram_tensor("yT", [D, S], F32, kind="ExternalOutput").ap()
        xs_d = nc.dram_tensor("x_spill", [D, S], F32, kind="ExternalOutput").ap()
        for name, shape, dt in self.debug:
            self.dbg_d[name] = nc.dram_tensor("dbg_" + name, list(shape), dt, kind="ExternalOutput").ap()

        AX_ = Arena(self, "arena_x", 65536)
        AH_ = Arena(self, "arena_h", 32768)
        AY_ = Arena(self, "arena_y", 32768)
        AT_ = Arena(self, "arena_t", 30720)
        xT = AX_.alloc([128, 8, S], F32)
        xT_r = rgrid(8, NG)
        hT = AH_.alloc([128, 8, S], BF16)
        hT_r = rgrid(8, NG)
        self.wbufs = [self.sb("wbuf%d" % i, [128, 8, 512], BF16) for i in range(NW)]
        self.wres = [R() for _ in range(NW)]
        tabfm = self.sb("tabfm", [128, NFM])
        tabrp = self.sb("tabrp", [128, NRP])
        gnorm = self.sb("gnorm", [128, 1024])
        gnorm_r = R()
        ropec = self.sb("ropec", [128, 16, 32])
        ropes = self.sb("ropes", [128, 16, 2, 32])
        cf32 = self.sb("cf32", [128, 384])
        cbf = self.sb("cbf", [128, NCB], BF16)
        poolw = self.sb("poolw", [128, L * 4, 128], BF16)
        gts = [self.sb("gt%d" % i, [128, 512]) for i in range(2)]
        gts_r = [R(), R()]
        self.dummy = self.sb("fence_dummy", [128, 8])
        cR = R()
        U_f = cf32[:, 0:128]
        L_f = cf32[:, 128:256]
        ones_f = cf32[:, 256:384]
        ident = cbf[:, CB_ID:CB_ID + 128]
        UT_b = cbf[:, CB_UT:CB_UT + 128]
        mcur = cbf[:, CB_MCUR:CB_MCUR + 512]
        mprev = cbf[:, CB_MPREV:CB_MPREV + 512]
        onesLR = [cbf[:, CB_ONL:CB_ONL + 128], cbf[:, CB_ONR:CB_ONR + 128]]
        ones_bf = self.sb("ones_bf", [128, 128], BF16)

        self.pbanks = [self.es.enter_context(nc.psum_tensor("ps%d" % i, [128, 512], F32))[:] for i in range(8)]
        self.pres = [R() for _ in range(8)]
        def fmcol(base, idx):
            return tabfm[:, base + idx: base + idx + 1]

        xT_dv = xT_d.rearrange("(k p) t -> p k t", p=128)
        for k in range(8):
            self.op("sp", lambda e, k=k: e.dma_start(out=xT[:, k, :], in_=xT_dv[:, k, :]), writes=xT_r[k], dma=True)
        for dst, src in ((tabfm, tabfm_d), (tabrp, tabrp_d), (cf32, cf32_d), (cbf, cbf_d),
                         (ropec, ropec_d.rearrange("p (t c) -> p t c", c=32)),
                         (ropes, ropes_d.rearrange("p (t a c) -> p t a c", a=2, c=32))):
            self.op("sp", lambda e, dst=dst, src=src: e.dma_start(out=dst, in_=src), writes=cR, dma=True)
        self.op("pool", lambda e: e.dma_start(out=poolw, in_=poolw_d.rearrange("p (g d) -> p g d", d=128)),
                writes=cR, dma=True)
        self.op("dve", lambda e: e.memset(ones_bf, 1.0), writes=cR)

        def rmsnorm(gbase):
            AT_.reset()
            sq = AT_.alloc([128, 8, 512], BF16)
            rstd = AT_.alloc([128, 512], F32)
            sq_r, rstd_r = R(), R()
            self.fence(self.at_live, [sq_r, rstd_r])
            self.at_live = [sq_r, rstd_r]
            for tg in range(NG):
                ts = slice(tg * 512, (tg + 1) * 512)
                self.act(sq, xT[:, :, ts], AF.Square, reads=[xT_r[k][tg] for k in range(8)], writes=sq_r)
                pb, pr = self.psum()
                for k in range(8):
                    self.mm(pb, ones_bf, sq[:, k, :], k == 0, k == 7, reads=[cR, sq_r], writes=pr)
                self.act(rstd, pb, AF.Sqrt, reads=pr, writes=rstd_r, scale=1.0 / D, bias=EPS)
                self.op("dve", lambda e: e.reciprocal(rstd, rstd), reads=rstd_r, writes=rstd_r)
                for k in range(8):
                    self.stt(hT[:, k, ts], xT[:, k, ts], fmcol(gbase, k), rstd, ALU.mult, ALU.mult,
                             reads=[xT_r[k][tg], cR, rstd_r], writes=hT_r[k][tg])

        def merge(l, b, nkc, kp, ysrc, yres, wload_b, first):
            for ct in range(2):
                wg, wg_r = self.wload(w_in_d[l], 0, 128, 8, C_GATE + b * 1024 + ct * 512, 512)
                wb, wb_r = wload_b(ct)
                for f4 in range(4):
                    fo = ct * 4 + f4
                    fs = slice(f4 * 128, (f4 + 1) * 128)
                    for tg in range(NG):
                        ts = slice(tg * 512, (tg + 1) * 512)
                        pg, pgr = self.psum()
                        py, pyr = self.psum()
                        for k in range(8):
                            self.mm(pg, wg[:, k, fs], hT[:, k, ts], k == 0, k == 7,
                                    reads=[wg_r, hT_r[k][tg]], writes=pgr)
                        for kc in range(nkc):
                            self.mm(py, wb[:kp, kc, fs], ysrc(kc, ts), kc == 0, kc == nkc - 1,
                                    reads=[wb_r, yres(kc, tg)], writes=pyr)
                        gi = (fo * NG + tg) % 2
                        gt, gt_r = gts[gi], gts_r[gi]
                        self.act(gt, pg, AF.Sigmoid, reads=[pgr, cR], writes=gt_r,
                                 bias=fmcol(FM_GATEB, l * 32 + b * 8 + fo))
                        if first:
                            self.tt("dve", xT[:, fo, ts], gt, py, ALU.mult, reads=[gt_r, pyr], writes=xT_r[fo][tg])
                        else:
                            self.tt("dve", gt, gt, py, ALU.mult, reads=[gt_r, pyr], writes=gt_r)
                            self.tt("pool", xT[:, fo, ts], xT[:, fo, ts], gt, ALU.add,
                                    reads=[gt_r, xT_r[fo][tg]], writes=xT_r[fo][tg])

        self.at_live = []
        self.ay_live = []
        xs_r = rgrid(8)

        for l in range(self.nlayers):
            if l > 0:
                self.op("sp", lambda e, l=l: e.dma_start(out=gnorm, in_=gnorm_d[l]), writes=gnorm_r, dma=True)
            else:
                self.op("sp", lambda e: e.dma_start(out=gnorm, in_=gnorm_d[0]), writes=gnorm_r, dma=True)
            rmsnorm(FM_GMIX + l * 8)
            self.tap("hT%d" % l, hT, hT_r)
            if self.stop_after == "norm1" and l == STOP_LAYER:
                break
            xs_dv = xs_d.rearrange("(k p) t -> p k t", p=128)
            if l > 0:
                for k in range(8):
                    self.op("sp", lambda e, k=k: e.dma_start(out=xs_dv[:, k, :], in_=xT[:, k, :]),
                            reads=xT_r[k], writes=xs_r[k], dma=True)
            x_src = xT_dv if l == 0 else xs_dv
            if self.stop_after == "spill" and l == STOP_LAYER:
                break

            AX_.reset()
            AT_.reset()
            AY_.reset()
            BT = AX_.alloc([128, 2, S], BF16)
            CT = AX_.alloc([128, 2, S], BF16)
            Btok = AX_.alloc([128, 2, NT, 128], BF16)
            cbm = AX_.alloc([128, 2, NT, 128], BF16)
            y2h = AX_.alloc([128, 4, NT, 128], BF16)
            ytok = AX_.alloc([128, NT, 128], F32)
            xs_tok = AX_.alloc([128, NT, 128], BF16)
            xdt = AX_.alloc([128, NT, 128], BF16)
            BT_r, CT_r, Btok_r, cbm_r = rgrid(2, NG), rgrid(2, NG), rgrid(2), rgrid(2)
            y2h_r, ytok_r, xs_tok_r, xdt_r = rgrid(4), rgrid(NT), R(), R()
            ax_new = [BT_r, CT_r, Btok_r, cbm_r, y2h_r, ytok_r, xs_tok_r, xdt_r]
            self.fence(xT_r, ax_new)

            upad = AT_.alloc([128, 3 + S], BF16)
            diag4 = AT_.alloc([128, 4, 128], BF16)
            xsT = AT_.alloc([128, S], BF16)
            xw = AT_.alloc([128, NT, 128], BF16)
            xsD = AT_.alloc([128, NT, 128], BF16)
            dtt = AT_.alloc([128, 256], F32)
            adt = AT_.alloc([128, 256], F32)
            eac = AT_.alloc([128, 256], F32)
            dsd = AT_.alloc([128, 256], F32)
            cdec = AT_.alloc([128, 256], F32)
            tmpa = AT_.alloc([128, 256], F32)
            tmpb = AT_.alloc([128, 256], F32)
            Abuf = AT_.alloc([128, 2, 128], F32)
            Ebuf = AT_.alloc([128, 2, 128], F32)
            Mbuf = AT_.alloc([128, 2, 128], BF16)
            Sst = AT_.alloc([128, 128], F32)
            Sbf = AT_.alloc([128, 128], BF16)
            yo = AT_.alloc([128, 128], F32)
            zs = AT_.alloc([128, 512], F32)
            ssq_hp = AT_.alloc([128, 16], F32)
            ssq_g = AT_.alloc([128, 16], F32)
            rstd_g = AT_.alloc([128, 16], F32)
            upad_r, diag4_r, xsT_r, xw_r, xsD_r = R(), R(), rgrid(NG), R(), R()
            dt_r, A_r, E_r, M_r, S_r, Sbf_r, yo_r, zs_r, ssq_r = R(), R(), R(), R(), R(), R(), R(), R(), R()
            A2_r, E2_r = R(), R()
            at_new = [upad_r, diag4_r, xsT_r, xw_r, xsD_r, dt_r, A_r, E_r, M_r, S_r, Sbf_r, yo_r, zs_r, ssq_r, A2_r, E2_r]
            self.fence(self.at_live, at_new)
            self.at_live = at_new
            ySSD = AY_.alloc([128, 8, S], BF16)
            ySSD_r = rgrid(8, NG)
            self.fence(self.ay_live, ySSD_r)
            self.ay_live = [ySSD_r]

            self.op("dve", lambda e: e.memset(upad[:, 0:3], 0.0), writes=upad_r)
            Ab = [Abuf, tmpa.rearrange("p (a b) -> p a b", a=2)]
            Eb = [Ebuf, tmpb.rearrange("p (a b) -> p a b", a=2)]
            A_rs, E_rs = [A_r, A2_r], [E_r, E2_r]
            Sb = [Sbf, zs[:, 0:64].bitcast(BF16)]
            Sb_rs = [Sbf_r, zs_r]

            wdt, wdt_r = self.wload(w_in_d[l], 0, 128, 8, C_DT, 16)
            pdt, pdt_r = self.psum()
            for tt_ in range(NT):
                for k in range(8):
                    self.mm(pdt[:, tt_ * 16:(tt_ + 1) * 16], hT[:, k, tt_ * 128:(tt_ + 1) * 128], wdt[:, k, 0:16],
                            k == 0, k == 7, reads=[wdt_r, hT_r[k][tt_ // 4]], writes=pdt_r)
            rep16 = lambda base: tabrp[:, base + l * 16: base + (l + 1) * 16].unsqueeze(1).broadcast_to([128, 16, 16])
            v3 = lambda a: a.rearrange("p (t h) -> p t h", h=16)
            self.tt("dve", v3(tmpa), v3(pdt[:, 0:256]), rep16(RP_DTB), ALU.add, reads=[pdt_r, cR], writes=dt_r)
            self.act(tmpb, tmpa, AF.Abs, reads=dt_r, writes=dt_r)
            self.act(tmpb, tmpb, AF.Exp, reads=dt_r, writes=dt_r, scale=-1.0)
            self.act(tmpb, tmpb, AF.Ln, reads=dt_r, writes=dt_r, bias=1.0)
            self.stt(dtt, tmpa, 0.0, tmpb, ALU.max, ALU.add, reads=dt_r, writes=dt_r)
            self.act(v3(tmpa), rep16(RP_ALOG), AF.Exp, reads=[dt_r, cR], writes=dt_r)
            self.stt(adt, dtt, -1.0, tmpa, ALU.mult, ALU.mult, reads=dt_r, writes=dt_r)
            pac, pac_r = self.psum()
            pal, pal_r = self.psum()
            for c in range(NT):
                cs_ = slice(c * 16, (c + 1) * 16)
                self.mm(pac[:, cs_], U_f, adt[:, cs_], True, True, reads=[cR, dt_r], writes=pac_r)
            for c in range(NT):
                cs_ = slice(c * 16, (c + 1) * 16)
                self.mm(pal[:, cs_], ones_f, adt[:, cs_], True, True, reads=[cR, dt_r], writes=pal_r)
            self.act(eac, pac[:, 0:256], AF.Exp, reads=pac_r, writes=dt_r)
            self.act(cdec, pal[:, 0:256], AF.Exp, reads=pal_r, writes=dt_r)
            self.cp("act", tmpa, pac[:, 0:256], reads=pac_r, writes=dt_r)
            self.tt("dve", tmpb, pal[:, 0:256], tmpa, ALU.subtract, reads=[pal_r, dt_r], writes=dt_r)
            self.act(dsd, tmpb, AF.Exp, reads=dt_r, writes=dt_r)
            self.tap("dt%d" % l, dtt, dt_r)
            self.tap("eac%d" % l, eac, dt_r)

            def conv4(fc, src_w, src_wr, wcol, dst_fn, dst_res_fn):
                self.tt("dve", diag4, ident.unsqueeze(1).broadcast_to([128, 4, 128]),
                        tabfm[:, FM_CW + (l * 12 + fc) * 4: FM_CW + (l * 12 + fc) * 4 + 4].unsqueeze(2).broadcast_to([128, 4, 128]),
                        ALU.mult, reads=cR, writes=diag4_r)
                for tg in range(NG):
                    ts = slice(tg * 512, (tg + 1) * 512)
                    pb, pr = self.psum()
                    for k in range(8):
                        self.mm(pb, src_w[:, k, wcol:wcol + 128], hT[:, k, ts], k == 0, k == 7,
                                reads=[src_wr, hT_r[k][tg]], writes=pr)
                    self.cp("act", upad[:, 3 + tg * 512: 3 + (tg + 1) * 512], pb, reads=pr, writes=upad_r)
                for tg in range(NG):
                    pb, pr = self.psum()
                    for tap_ in range(4):
                        self.mm(pb, diag4[:, tap_, :], upad[:, tg * 512 + tap_: tg * 512 + tap_ + 512],
                                tap_ == 0, tap_ == 3, reads=[diag4_r, upad_r], writes=pr)
                    self.act(dst_fn(tg), pb, AF.Silu, reads=[pr, cR], writes=dst_res_fn(tg),
                             bias=fmcol(FM_CB, l * 12 + fc))

            wbc, wbc_r = self.wload(w_in_d[l], 0, 128, 8, C_B, 512)
            for j in range(4):
                dst_t, dst_r = (BT, BT_r) if j < 2 else (CT, CT_r)
                g = j % 2
                conv4(8 + j, wbc, wbc_r, j * 128,
                      lambda tg, dst_t=dst_t, g=g: dst_t[:, g, tg * 512:(tg + 1) * 512],
                      lambda tg, dst_r=dst_r, g=g: dst_r[g][tg])
            for g in range(2):
                for t4 in range(4):
                    pb, pr = self.psum()
                    pbb = pb.bitcast(BF16)
                    for i4 in range(4):
                        tt_ = t4 * 4 + i4
                        self.tr(pbb[:, i4 * 128:(i4 + 1) * 128], BT[:, g, tt_ * 128:(tt_ + 1) * 128], ident,
                                reads=[BT_r[g][t4], cR], writes=pr)
                    self.cp("act", Btok[:, g, t4 * 4:(t4 + 1) * 4, :],
                            pbb[:, 0:512].rearrange("p (a b) -> p a b", a=4), reads=pr, writes=Btok_r[g])
                for t4 in range(4):
                    pb, pr = self.psum()
                    for i4 in range(4):
                        c = t4 * 4 + i4
                        self.mm(pb[:, i4 * 128:(i4 + 1) * 128], BT[:, g, c * 128:(c + 1) * 128],
                                CT[:, g, c * 128:(c + 1) * 128], True, True,
                                reads=[BT_r[g][t4], CT_r[g][t4]], writes=pr)
                    self.tt("dve", cbm[:, g, t4 * 4:(t4 + 1) * 4, :], pb.rearrange("p (a b) -> p a b", a=4),
                            UT_b.unsqueeze(1).broadcast_to([128, 4, 128]), ALU.mult, reads=[pr, cR], writes=cbm_r[g])
            self.tap("BT%d" % l, BT, BT_r)
            self.tap("cbm%d" % l, cbm, cbm_r)

            def hp_pre(hp):
                g = hp // 4
                r0 = 2 * hp
                if hp % 4 == 0:
                    wh['xs'] = self.wload(w_in_d[l], 0, 128, 8, C_XS + (hp // 4) * 512, 512)
                conv4(hp, wh['xs'][0], wh['xs'][1], (hp % 4) * 128,
                      lambda tg: xsT[:, tg * 512:(tg + 1) * 512], lambda tg: xsT_r[tg])
                for t4 in range(4):
                    pb, pr = self.psum()
                    pbb = pb.bitcast(BF16)
                    for i4 in range(4):
                        tt_ = t4 * 4 + i4
                        self.tr(pbb[:, i4 * 128:(i4 + 1) * 128], xsT[:, tt_ * 128:(tt_ + 1) * 128], ident,
                                reads=[xsT_r[t4], cR], writes=pr)
                    self.cp("act", xs_tok[:, t4 * 4:(t4 + 1) * 4, :],
                            pbb[:, 0:512].rearrange("p (a b) -> p a b", a=4), reads=pr, writes=xs_tok_r)
                v4 = lambda a: a.rearrange("p t (h d) -> p t h d", h=2)
                hb = lambda a: a.rearrange("p (t h) -> p t h", h=16)[:, :, r0:r0 + 2].unsqueeze(3).broadcast_to([128, NT, 2, 64])
                self.tt("dve", v4(xdt), v4(xs_tok), hb(dtt), ALU.mult, reads=[xs_tok_r, dt_r], writes=xdt_r)
                self.tt("dve", v4(xw), v4(xdt), hb(dsd), ALU.mult, reads=[xdt_r, dt_r], writes=xw_r)
                dcol = tabrp[:, RP_D + l * 16 + r0: RP_D + l * 16 + r0 + 2]
                self.tt("dve", v4(xsD), v4(xs_tok),
                        dcol.unsqueeze(1).unsqueeze(3).broadcast_to([128, NT, 2, 64]), ALU.mult,
                        reads=[xs_tok_r, cR], writes=xsD_r)
            def hp_scan(hp):
                g = hp // 4
                r0 = 2 * hp
                self.op("dve", lambda e: e.memset(Sst, 0.0), writes=S_r)

                def stage1(c):
                    pseg, pseg_r = self.psum()
                    bb = c % 2
                    col0 = c * 16 + r0
                    self.tt("dve", Ab[bb], L_f.unsqueeze(1).broadcast_to([128, 2, 128]),
                            adt[:, col0:col0 + 2].unsqueeze(2).broadcast_to([128, 2, 128]), ALU.mult,
                            reads=[cR, dt_r], writes=A_rs[bb])
                    for j in range(2):
                        self.mm(pseg[:, j * 128:(j + 1) * 128], Ab[bb][:, j, :], U_f, True, True,
                                reads=[A_rs[bb], cR], writes=pseg_r)
                    self.act(Eb[bb], pseg[:, 0:256].rearrange("p (a b) -> p a b", a=2), AF.Exp,
                             reads=pseg_r, writes=E_rs[bb])

                def stage2(c):
                    bb = c % 2
                    if c < NT - 1:
                        pst, pst_r = self.psum()
                        self.mm(pst[:, 0:128], Btok[:, g, c, :], xw[:, c, :], True, True,
                                reads=[Btok_r[g], xw_r], writes=pst_r)
                        for j in range(2):
                            col = c * 16 + r0 + j
                            self.stt(Sst[:, j * 64:(j + 1) * 64], Sst[:, j * 64:(j + 1) * 64], cdec[:, col:col + 1],
                                     pst[:, j * 64:(j + 1) * 64], ALU.mult, ALU.add,
                                     reads=[S_r, dt_r, pst_r], writes=S_r)
                        self.cp("act", Sb[(c + 1) % 2], Sst, reads=S_r, writes=Sb_rs[(c + 1) % 2])
                    self.tt("dve", Mbuf, Eb[bb], cbm[:, g, c, :].unsqueeze(1).broadcast_to([128, 2, 128]), ALU.mult,
                            reads=[E_rs[bb], cbm_r[g]], writes=M_r)
                    if c > 0:
                        pyo, pyo_r = self.psum()
                        self.mm(pyo[:, 0:128], CT[:, g, c * 128:(c + 1) * 128], Sb[c % 2], True, True,
                                reads=[CT_r[g][c // 4], Sb_rs[c % 2]], writes=pyo_r)
                        for j in range(2):
                            col = c * 16 + r0 + j
                            self.act(yo[:, j * 64:(j + 1) * 64], pyo[:, j * 64:(j + 1) * 64], AF.Identity,
                                     reads=[pyo_r, dt_r], writes=yo_r, scale=eac[:, col:col + 1])
                    pyd, pyd_r = self.psum()
                    self.mm(pyd[:, 0:128], ident, xsD[:, c, :], True, False, reads=[cR, xsD_r], writes=pyd_r)
                    for j in range(2):
                        self.mm(pyd[:, j * 64:(j + 1) * 64], Mbuf[:, j, :], xdt[:, c, j * 64:(j + 1) * 64],
                                False, j == 1, reads=[M_r, xdt_r], writes=pyd_r)
                    if c + 1 < NT:
                        stage1(c + 1)
                    if c > 0:
                        self.tt("dve", ytok[:, c, :], pyd[:, 0:128], yo, ALU.add, reads=[pyd_r, yo_r], writes=ytok_r[c])
                    else:
                        self.cp("act", ytok[:, c, :], pyd[:, 0:128], reads=pyd_r, writes=ytok_r[c])

                stage1(0)
                for c in range(NT):
                    stage2(c)
            def hp_post(hp):
                g = hp // 4
                r0 = 2 * hp
                if hp == 0:
                    self.tap("ytok%d" % l, ytok, ytok_r)
                if hp % 4 == 0:
                    wh['z'] = self.wload(w_in_d[l], 0, 128, 8, C_Z + (hp // 4) * 512, 512)
                for t4 in range(4):
                    pb, pr = self.psum()
                    for i4 in range(4):
                        tt_ = t4 * 4 + i4
                        for k in range(8):
                            self.mm(pb[:, i4 * 128:(i4 + 1) * 128], hT[:, k, tt_ * 128:(tt_ + 1) * 128],
                                    wh['z'][0][:, k, (hp % 4) * 128:(hp % 4 + 1) * 128], k == 0, k == 7,
                                    reads=[wh['z'][1], hT_r[k][t4]], writes=pr)
                    self.act(zs, pb, AF.Silu, reads=pr, writes=zs_r)
                    ysl = ytok[:, t4 * 4:(t4 + 1) * 4, :]
                    z3 = zs.rearrange("p (a b) -> p a b", a=4)
                    yr_ = [ytok_r[t4 * 4 + i] for i in range(4)]
                    self.tt("dve", ysl, ysl, z3, ALU.mult, reads=[zs_r] + yr_, writes=yr_)
                    self.tt("dve", z3, ysl, ysl, ALU.mult, reads=yr_, writes=zs_r)
                    self.op("dve", lambda e, t4=t4, z3=z3: e.tensor_reduce(out=ssq_hp[:, t4 * 4:(t4 + 1) * 4], in_=z3,
                                                                           axis=AX.X, op=ALU.add),
                            reads=zs_r, writes=ssq_r)
                    self.cp("act", y2h[:, hp % 4, t4 * 4:(t4 + 1) * 4, :], ysl, reads=yr_, writes=y2h_r[hp % 4])
                if hp % 4 == 0:
                    self.cp("dve", ssq_g, ssq_hp, reads=ssq_r, writes=ssq_r)
                else:
                    self.tt("dve", ssq_g, ssq_g, ssq_hp, ALU.add, reads=ssq_r, writes=ssq_r)
                if hp % 4 == 3:
                    self.act(rstd_g, ssq_g, AF.Sqrt, reads=ssq_r, writes=ssq_r, scale=1.0 / 512, bias=EPS)
                    self.op("dve", lambda e: e.reciprocal(rstd_g, rstd_g), reads=ssq_r, writes=ssq_r)
                    for hq in range(4):
                        fc = g * 4 + hq
                        yv = y2h[:, hq, :, :]
                        self.tt("dve", yv, yv, rstd_g.unsqueeze(2).broadcast_to([128, NT, 128]), ALU.mult,
                                reads=[y2h_r[hq], ssq_r], writes=y2h_r[hq])
                        self.tt("dve", xs_tok, yv, gnorm[:, fc * 128:(fc + 1) * 128].unsqueeze(1).broadcast_to([128, NT, 128]),
                                ALU.mult, reads=[y2h_r[hq], gnorm_r], writes=xs_tok_r)
                        for t4 in range(4):
                            pb, pr = self.psum()
                            pbb = pb.bitcast(BF16)
                            for i4 in range(4):
                                tt_ = t4 * 4 + i4
                                self.tr(pbb[:, i4 * 128:(i4 + 1) * 128], xs_tok[:, tt_, :], ident,
                                        reads=[xs_tok_r, cR], writes=pr)
                            self.cp("act", ySSD[:, fc, t4 * 512:(t4 + 1) * 512], pbb[:, 0:512],
                                    reads=pr, writes=ySSD_r[fc][t4])
            wh = {}
            hp_pre(0)
            for hp in range(8):
                hp_scan(hp)
                if hp + 1 < 8:
                    hp_pre(hp + 1)
                hp_post(hp)
            self.tap("ySSD%d" % l, ySSD, ySSD_r)
            if self.stop_after == "ssd" and l == STOP_LAYER:
                break

            self.fence(ax_new, xT_r)
            merge(l, 0, 8, 128, lambda kc, ts: ySSD[:, kc, ts], lambda kc, tg: ySSD_r[kc][tg],
                  lambda ct: self.wload(w_br_d[l], 0, 128, 8, ct * 512, 512), True)
            self.tap("m0_%d" % l, xT, xT_r)
            if self.stop_after == "ssdmerge" and l == STOP_LAYER:
                break

            AY_.reset()
            AT_.reset()
            yAT = AY_.alloc([128, 4, S], BF16)
            kT = AY_.alloc([128, 2, S], BF16)
            yAT_r, kT_r = rgrid(NT), rgrid(NT)
            self.fence(self.ay_live, [yAT_r, kT_r])
            self.ay_live = [yAT_r, kT_r]
            vpad = AT_.alloc([128, NT, 2, 128], BF16)
            qTt = AT_.alloc([128, 2, 8, 128], BF16)
            qsq = AT_.alloc([128, 512], F32)
            qn = AT_.alloc([128, 512], F32)
            t1 = AT_.alloc([128, 512], F32)
            qr = AT_.alloc([128, 512], BF16)
            kr = AT_.alloc([128, 128], BF16)
            Pb = AT_.alloc([128, 4, 512], BF16)
            den = AT_.alloc([128, 512], F32)
            st8 = AT_.alloc([128, 8], F32)
            skx = AT_.alloc([128, 4], F32)
            vpad_r, qTt_r = rgrid(NT), rgrid(2)
            qw_r, P_r, den_r, st_r, skx_r, kr_r = R(), rgrid(4), R(), R(), R(), R()
            at_new = [vpad_r, qTt_r, qw_r, P_r, den_r, st_r, skx_r, kr_r]
            self.fence(self.at_live, at_new)
            self.at_live = at_new
            self.op("dve", lambda e: e.memset(vpad, 0.0), writes=vpad_r)
            self.op("dve", lambda e: e.memset(kT[64:128, :, :], 0.0), writes=kT_r)
            self.op("dve", lambda e: e.memset(qTt[64:128, :, :, :], 0.0), writes=qTt_r)
            self.act(skx, tabrp[:, RP_SINK + l * 4: RP_SINK + l * 4 + 4], AF.Exp, reads=cR, writes=skx_r)
            wq, wq_r = self.wload(w_in_d[l], 0, 128, 8, C_Q, 512)
            wkv, wkv_r = self.wload(w_in_d[l], 0, 128, 8, C_K, 256)
            def att_a(i):
                tsl = slice(i * 128, (i + 1) * 128)
                qb = i % 2
                pq, pq_r = self.psum()
                for k in range(8):
                    self.mm(pq, hT[:, k, tsl], wq[:, k, 0:512], k == 0, k == 7, reads=[wq_r, hT_r[k][i // 4]], writes=pq_r)
                pkv, pkv_r = self.psum()
                for k in range(8):
                    self.mm(pkv[:, 0:256], hT[:, k, tsl], wkv[:, k, 0:256], k == 0, k == 7,
                            reads=[wkv_r, hT_r[k][i // 4]], writes=pkv_r)

                def normrope(src, nh, gcol, dst, dst_r, src_r):
                    w = nh * 64
                    self.act(qsq[:, 0:w], src, AF.Square, reads=src_r, writes=qw_r)
                    self.op("dve", lambda e: e.tensor_reduce(out=st8[:, 0:nh], in_=qsq[:, 0:w].rearrange("p (h d) -> p h d", d=64),
                                                              axis=AX.X, op=ALU.add), reads=qw_r, writes=st_r)
                    self.act(st8[:, 0:nh], st8[:, 0:nh], AF.Sqrt, reads=st_r, writes=st_r, scale=1.0 / 64, bias=EPS)
                    self.op("dve", lambda e: e.reciprocal(st8[:, 0:nh], st8[:, 0:nh]), reads=st_r, writes=st_r)
                    h3 = lambda a: a.rearrange("p (h d) -> p h d", d=64)
                    self.tt("dve", h3(qn[:, 0:w]), h3(src), st8[:, 0:nh].unsqueeze(2).broadcast_to([128, nh, 64]), ALU.mult,
                            reads=[src_r, st_r], writes=qw_r)
                    self.tt("dve", h3(qn[:, 0:w]), h3(qn[:, 0:w]),
                            tabrp[:, gcol + l * 64: gcol + (l + 1) * 64].unsqueeze(1).broadcast_to([128, nh, 64]), ALU.mult,
                            reads=[qw_r, cR], writes=qw_r)
                    h4 = lambda a: a.rearrange("p (h a d) -> p h a d", a=2, d=32)
                    self.tt("dve", h4(t1[:, 0:w]), h4(qn[:, 0:w]),
                            ropec[:, i, :].unsqueeze(1).unsqueeze(1).broadcast_to([128, nh, 2, 32]), ALU.mult,
                            reads=[qw_r, cR], writes=qw_r)
                    for a in range(2):
                        self.tt("dve", h4(qsq[:, 0:w])[:, :, a, :], h4(qn[:, 0:w])[:, :, 1 - a, :],
                                ropes[:, i, a, :].unsqueeze(1).broadcast_to([128, nh, 32]), ALU.mult,
                                reads=[qw_r, cR], writes=qw_r)
                    self.tt("dve", dst, t1[:, 0:w], qsq[:, 0:w], ALU.add, reads=qw_r, writes=[qw_r, dst_r])

                if ATT_LEVEL < 2:
                    return
                normrope(pq, 8, RP_QG, qr, qw_r, pq_r)
                if ATT_LEVEL < 3:
                    return
                pb, pr = self.psum()
                pbb = pb.bitcast(BF16)
                for hd in range(8):
                    self.tr(pbb[0:64, hd * 128:(hd + 1) * 128], qr[:, hd * 64:(hd + 1) * 64], ident, reads=[qw_r, cR], writes=pr)
                self.cp("act", qTt[0:64, qb, :, :], pbb[0:64, 0:1024].rearrange("p (a b) -> p a b", a=8),
                        reads=pr, writes=qTt_r[qb])
                if ATT_LEVEL < 4:
                    return
                for h in range(2):
                    self.cp("act", vpad[:, i, h, h * 64:(h + 1) * 64], pkv[:, 128 + h * 64: 192 + h * 64],
                            reads=pkv_r, writes=vpad_r[i])
                normrope(pkv[:, 0:128], 2, RP_KG, kr, kr_r, pkv_r)
                pb, pr = self.psum()
                pbb = pb.bitcast(BF16)
                for h in range(2):
                    self.tr(pbb[0:64, h * 128:(h + 1) * 128], kr[:, h * 64:(h + 1) * 64], ident, reads=[kr_r, cR], writes=pr)
                self.cp("act", kT[0:64, :, tsl], pbb[0:64, 0:256].rearrange("p (a b) -> p a b", a=2), reads=pr, writes=kT_r[i])
            def att_b(i):
                tsl = slice(i * 128, (i + 1) * 128)
                qb = i % 2
                blks = [i] if i == 0 else [i - 1, i]
                pidx = []
                for h in range(2):
                    for blk in blks:
                        bs = slice(blk * 128, (blk + 1) * 128)
                        pi = len(pidx)
                        pidx.append((h, blk, pi))
                        ps_, ps_r = self.psum()
                        self.mm(ps_.rearrange("p (a b) -> p a b", a=4), kT[:, h, bs], qTt[:, qb, 4 * h:4 * h + 4, :],
                                True, True, reads=[kT_r[blk], qTt_r[qb]], writes=ps_r)
                        self.act(Pb[:, pi, :], ps_, AF.Exp, reads=ps_r, writes=P_r[pi], scale=0.125)
                        self.tt("dve", Pb[:, pi, :], Pb[:, pi, :], mcur if blk == i else mprev, ALU.mult,
                                reads=[P_r[pi], cR], writes=P_r[pi])
                if ATT_LEVEL < 6:
                    return
                pnum, pnum_r = self.psum()
                pden, pden_r = self.psum()
                n = len(pidx)
                for q_, (h, blk, pi) in enumerate(pidx):
                    self.mm(pnum, vpad[:, blk, h, :], Pb[:, pi, :], q_ == 0, q_ == n - 1,
                            reads=[vpad_r[blk], P_r[pi]], writes=pnum_r)
                for q_, (h, blk, pi) in enumerate(pidx):
                    self.mm(pden, onesLR[h], Pb[:, pi, :], q_ == 0, q_ == n - 1, reads=[cR, P_r[pi]], writes=pden_r)
                d3 = lambda a: a.rearrange("p (r q) -> p r q", r=4)
                self.tt("dve", d3(den), d3(pden), skx.unsqueeze(2).broadcast_to([128, 4, 128]), ALU.add,
                        reads=[pden_r, skx_r], writes=den_r)
                self.op("dve", lambda e: e.reciprocal(den, den), reads=den_r, writes=den_r)
                self.tt("dve", yAT[:, :, tsl], d3(pnum), d3(den), ALU.mult, reads=[pnum_r, den_r], writes=yAT_r[i])
            att_a(0)
            for i in range(NT):
                if i + 1 < NT:
                    att_a(i + 1)
                att_b(i)
            self.tap("yAT%d" % l, yAT, yAT_r)
            if self.stop_after == "attn" and l == STOP_LAYER:
                break

            def wload_attn(ct):
                i_ = self.w_i
                self.w_i = (i_ + 1) % NW
                buf, res = self.wbufs[i_], self.wres[i_]
                for h in range(2):
                    src = w_br_d[l][1024 + h * 256:1024 + (h + 1) * 256, ct * 512:(ct + 1) * 512].rearrange(
                        "(r d) c -> d r c", d=64)
                    dst = buf[h * 64:(h + 1) * 64, 0:4, :]
                    self.op("pool", lambda e, dst=dst, src=src: e.dma_start(out=dst, in_=src), writes=res, dma=True)
                return buf, res

            merge(l, 1, 4, 128, lambda kc, ts: yAT[:, kc, ts], lambda kc, tg: [yAT_r[tg * 4 + i] for i in range(4)],
                  wload_attn, False)
            self.tap("m1_%d" % l, xT, xT_r)
            if self.stop_after == "attnmerge" and l == STOP_LAYER:
                break

            AY_.reset()
            AT_.reset()
            yPT = AY_.alloc([128, 4, S], BF16)
            yPT_r = rgrid(4, NG)
            self.fence(self.ay_live, yPT_r)
            self.ay_live = [yPT_r]
            PADW = 8
            pa = AT_.alloc([128, PADW + S], F32)
            pbuf = AT_.alloc([128, PADW + S], F32)
            ub = AT_.alloc([128, 16 + S], F32)
            pooled = AT_.alloc([128, S], BF16)
            inv16 = AT_.alloc([128, 16], F32)
            pa_r, pb_r, ub_r, pooled_r, inv_r = R(), R(), R(), rgrid(NG), R()
            at_new = [pa_r, pb_r, ub_r, pooled_r, inv_r]
            self.fence(self.at_live, at_new)
            self.at_live = at_new
            self.op("dve", lambda e: e.memset(pa[:, 0:PADW], 0.0), writes=pa_r)
            self.op("dve", lambda e: e.memset(pbuf[:, 0:PADW], 0.0), writes=pb_r)
            self.op("dve", lambda e: e.memset(ub[:, 0:16], 0.0), writes=ub_r)
            pcs, pcs_r = self.psum()
            self.mm(pcs[:, 0:128], ones_f, U_f, True, True, reads=cR, writes=pcs_r)
            self.op("dve", lambda e, pcs=pcs, inv16=inv16: e.reciprocal(inv16, pcs[:, 0:16]), reads=pcs_r, writes=inv_r)
            wpl, wpl_r = self.wload(w_in_d[l], 0, 128, 8, C_POOL, 512)
            for gi in range(4):
                w_ = (2, 4, 8, 16)[gi]
                for tg in range(NG):
                    ts = slice(tg * 512, (tg + 1) * 512)
                    pb_, pr = self.psum()
                    for k in range(8):
                        self.mm(pb_, wpl[:, k, gi * 128:(gi + 1) * 128], hT[:, k, ts], k == 0, k == 7,
                                reads=[wpl_r, hT_r[k][tg]], writes=pr)
                    self.cp("act", ub[:, 16 + tg * 512: 16 + (tg + 1) * 512], pb_, reads=pr, writes=ub_r)
                u_ = ub[:, 16:16 + S]
                src, src_r, sh = ub, ub_r, 1
                srcoff = 16
                bufs = [(pa, pa_r), (pbuf, pb_r)]
                bi = 0
                while sh < w_:
                    dstb, dst_r = bufs[bi]
                    bi ^= 1
                    self.tt("dve", dstb[:, PADW:PADW + S], src[:, srcoff:srcoff + S], src[:, srcoff - sh:srcoff - sh + S],
                            ALU.add, reads=src_r, writes=dst_r)
                    src, src_r, srcoff = dstb, dst_r, PADW
                    sh *= 2
                sw = src[:, srcoff:srcoff + S]
                for tg in range(NG):
                    ts = slice(tg * 512, (tg + 1) * 512)
                    self.stt(pooled[:, ts], sw[:, ts], 1.0 / w_, u_[:, ts], ALU.mult, ALU.subtract,
                             reads=[src_r, ub_r], writes=pooled_r[tg])
                self.tt("dve", pa[:, 0:w_ - 1], sw[:, 0:w_ - 1], inv16[:, 0:w_ - 1], ALU.mult,
                        reads=[src_r, inv_r], writes=pa_r)
                self.tt("dve", pooled[:, 0:w_ - 1], pa[:, 0:w_ - 1], u_[:, 0:w_ - 1], ALU.subtract,
                        reads=[pa_r, ub_r, pooled_r[0]], writes=pooled_r[0])
                self.op("dve", lambda e: e.memset(pa[:, 0:PADW], 0.0), reads=pa_r, writes=pa_r)
                for tg in range(NG):
                    ts = slice(tg * 512, (tg + 1) * 512)
                    pb_, pr = self.psum()
                    self.mm(pb_, poolw[:, l * 4 + gi, :], pooled[:, ts], True, True, reads=[cR, pooled_r[tg]], writes=pr)
                    self.act(yPT[:, gi, ts], pb_, AF.Identity, reads=[pr, cR], writes=yPT_r[gi][tg],
                             scale=fmcol(FM_PSC, l * 4 + gi))
            self.tap("yPT%d" % l, yPT, yPT_r)
            merge(l, 2, 4, 128, lambda kc, ts: yPT[:, kc, ts], lambda kc, tg: yPT_r[kc][tg],
                  lambda ct: self.wload(w_br_d[l], 1536, 128, 4, ct * 512, 512), False)
            self.tap("m2_%d" % l, xT, xT_r)
            if self.stop_after == "pool" and l == STOP_LAYER:
                break

            AY_.reset()
            AT_.reset()
            yCT = AY_.alloc([128, 4, S], BF16)
            cv = AY_.alloc([128, 4, S], BF16)
            yCT_r, cv_r = rgrid(4, NG), rgrid(4, NG)
            self.fence(self.ay_live, [yCT_r, cv_r])
            self.ay_live = [yCT_r, cv_r]
            up31 = AT_.alloc([128, 30 + S], BF16)
            diag31 = AT_.alloc([128, 31, 128], BF16)
            sg = AT_.alloc([128, 512], F32)
            mean = AT_.alloc([128, 512], F32)
            rstd2 = AT_.alloc([128, 512], F32)
            tn = AT_.alloc([128, 512], F32)
            cvsq = AT_.alloc([128, 512], BF16)
            up_r, dg_r, sg_r, mean_r, rs_r, tn_r, cvsq_r = R(), R(), R(), R(), R(), R(), R()
            at_new = [up_r, dg_r, sg_r, mean_r, rs_r, tn_r, cvsq_r]
            self.fence(self.at_live, at_new)
            self.at_live = at_new
            self.op("dve", lambda e: e.memset(up31[:, 0:30], 0.0), writes=up_r)
            for half in range(2):
                wa, wa_r = self.wload(w_in_d[l], 0, 128, 8, C_CONV + half * 256, 256)
                wg_, wg_r_ = self.wload(w_in_d[l], 0, 128, 8, C_CONV + 512 + half * 256, 256)
                for jj in range(2):
                    j = half * 2 + jj
                    cs2 = slice(jj * 128, (jj + 1) * 128)
                    for tg in range(NG):
                        ts = slice(tg * 512, (tg + 1) * 512)
                        pa_, par = self.psum()
                        pg_, pgr = self.psum()
                        for k in range(8):
                            self.mm(pg_, wg_[:, k, cs2], hT[:, k, ts], k == 0, k == 7, reads=[wg_r_, hT_r[k][tg]], writes=pgr)
                        for k in range(8):
                            self.mm(pa_, wa[:, k, cs2], hT[:, k, ts], k == 0, k == 7, reads=[wa_r, hT_r[k][tg]], writes=par)
                        self.act(sg, pg_, AF.Sigmoid, reads=pgr, writes=sg_r)
                        self.tt("dve", up31[:, 30 + tg * 512: 30 + (tg + 1) * 512], pa_, sg, ALU.mult,
                                reads=[par, sg_r], writes=up_r)
                    self.tt("dve", diag31, ident.unsqueeze(1).broadcast_to([128, 31, 128]),
                            tabfm[:, FM_DWW + (l * 4 + j) * 31: FM_DWW + (l * 4 + j + 1) * 31].unsqueeze(2).broadcast_to([128, 31, 128]),
                            ALU.mult, reads=cR, writes=dg_r)
                    for tg in range(NG):
                        ts = slice(tg * 512, (tg + 1) * 512)
                        pb_, pr = self.psum()
                        for tap_ in range(31):
                            self.mm(pb_, diag31[:, tap_, :], up31[:, tg * 512 + tap_: tg * 512 + tap_ + 512],
                                    tap_ == 0, tap_ == 30, reads=[dg_r, up_r], writes=pr)
                        self.act(cv[:, j, ts], pb_, AF.Identity, reads=[pr, cR], writes=cv_r[j][tg],
                                 bias=fmcol(FM_DWB, l * 4 + j))
            for tg in range(NG):
                ts = slice(tg * 512, (tg + 1) * 512)
                pm, pm_r = self.psum()
                pq2, pq2_r = self.psum()
                for j in range(4):
                    self.mm(pm, ones_bf, cv[:, j, ts], j == 0, j == 3, reads=[cR, cv_r[j][tg]], writes=pm_r)
                for j in range(4):
                    self.act(cvsq, cv[:, j, ts], AF.Square, reads=cv_r[j][tg], writes=cvsq_r)
                    self.mm(pq2, ones_bf, cvsq, j == 0, j == 3, reads=[cR, cvsq_r], writes=pq2_r)
                self.act(mean, pm, AF.Identity, reads=pm_r, writes=mean_r, scale=1.0 / 512)
                self.tt("dve", tn, mean, mean, ALU.mult, reads=mean_r, writes=tn_r)
                self.stt(rstd2, pq2, 1.0 / 512, tn, ALU.mult, ALU.subtract, reads=[pq2_r, tn_r], writes=rs_r)
                self.act(rstd2, rstd2, AF.Sqrt, reads=rs_r, writes=rs_r, bias=EPS)
                self.op("dve", lambda e: e.reciprocal(rstd2, rstd2), reads=rs_r, writes=rs_r)
                for j in range(4):
                    self.tt("dve", tn, cv[:, j, ts], mean, ALU.subtract, reads=[cv_r[j][tg], mean_r], writes=tn_r)
                    self.tt("dve", tn, tn, rstd2, ALU.mult, reads=[tn_r, rs_r], writes=tn_r)
                    self.act(yCT[:, j, ts], tn, AF.Silu, reads=[tn_r, cR], writes=yCT_r[j][tg],
                             scale=fmcol(FM_LNG, l * 4 + j), bias=fmcol(FM_LNB, l * 4 + j))
            self.tap("yCT%d" % l, yCT, yCT_r)
            merge(l, 3, 4, 128, lambda kc, ts: yCT[:, kc, ts], lambda kc, tg: yCT_r[kc][tg],
                  lambda ct: self.wload(w_br_d[l], 2048, 128, 4, ct * 512, 512), False)
            self.tap("m3_%d" % l, xT, xT_r)
            if self.stop_after == "conv" and l == STOP_LAYER:
                break

            for k in range(8):
                for tg in range(NG):
                    ts = slice(tg * 512, (tg + 1) * 512)
                    self.cp("act" if (k + tg) % 2 else "dve", hT[:, k, ts], xT[:, k, ts],
                            reads=xT_r[k][tg], writes=hT_r[k][tg])
            for k in range(8):
                self.op("sp", lambda e, k=k, x_src=x_src: e.dma_start(out=xT[:, k, :], in_=x_src[:, k, :]),
                        reads=xs_r[k], writes=xT_r[k], dma=True)
            for ct in range(2):
                wo, wo_r = self.wload(w_out_d[l], 0, 128, 8, ct * 512, 512)
                for f4 in range(4):
                    fo = ct * 4 + f4
                    for tg in range(NG):
                        ts = slice(tg * 512, (tg + 1) * 512)
                        pb_, pr = self.psum()
                        for k in range(8):
                            self.mm(pb_, wo[:, k, f4 * 128:(f4 + 1) * 128], hT[:, k, ts], k == 0, k == 7,
                                    reads=[wo_r, hT_r[k][tg]], writes=pr)
                        self.tt("dve", xT[:, fo, ts], xT[:, fo, ts], pb_, ALU.add, reads=[pr, xT_r[fo][tg]],
                                writes=xT_r[fo][tg])
            self.tap("xmid%d" % l, xT, xT_r)
            if self.stop_after == "wout" and l == STOP_LAYER:
                break

            rmsnorm(FM_GMLP + l * 8)
            AY_.reset()
            AT_.reset()
            aT = AY_.alloc([128, 8, S], BF16)
            aT_r = rgrid(8, NG)
            self.fence(self.ay_live, aT_r)
            self.ay_live = [aT_r]
            rl = [AT_.alloc([128, 512], F32) for _ in range(2)]
            rl_r = [R(), R()]
            self.fence(self.at_live, rl_r)
            self.at_live = rl_r
            for hb in range(4):
                for ct in range(2):
                    wu, wu_r = self.wload(w_up_d[l], 0, 128, 8, hb * 1024 + ct * 512, 512)
                    for f4 in range(4):
                        fc = ct * 4 + f4
                        for tg in range(NG):
                            ts = slice(tg * 512, (tg + 1) * 512)
                            pb_, pr = self.psum()
                            for k in range(8):
                                self.mm(pb_, wu[:, k, f4 * 128:(f4 + 1) * 128], hT[:, k, ts], k == 0, k == 7,
                                        reads=[wu_r, hT_r[k][tg]], writes=pr)
                            ri = (fc * NG + tg) % 2
                            self.act(rl[ri], pb_, AF.Relu, reads=pr, writes=rl_r[ri])
                            self.tt("dve", aT[:, fc, ts], rl[ri], rl[ri], ALU.mult, reads=rl_r[ri], writes=aT_r[fc][tg])
                for ct in range(2):
                    wd, wd_r = self.wload(w_dn_d[l], hb * 1024, 128, 8, ct * 512, 512)
                    for f4 in range(4):
                        fo = ct * 4 + f4
                        for tg in range(NG):
                            ts = slice(tg * 512, (tg + 1) * 512)
                            pb_, pr = self.psum()
                            for k in range(8):
                                self.mm(pb_, wd[:, k, f4 * 128:(f4 + 1) * 128], aT[:, k, ts], k == 0, k == 7,
                                        reads=[wd_r, aT_r[k][tg]], writes=pr)
                            self.tt("dve", xT[:, fo, ts], xT[:, fo, ts], pb_, ALU.add, reads=[pr, xT_r[fo][tg]],
                                    writes=xT_r[fo][tg])
            self.tap("xout%d" % l, xT, xT_r)

        out_v = out_d.rearrange("(k p) t -> p k t", p=128)
        for k in range(8):
            self.op("sp", lambda e, k=k: e.dma_start(out=out_v[:, k, :], in_=xT[:, k, :]), reads=xT_r[k], dma=True)
        sc.finalize()
        fw = [(cs, sc.counts[cs]) for cs in ("d_sp", "d_pool", "d_act") if sc.counts.get(cs, 0)]
        sems = {}
        for cs, n in sc.counts.items():
            sg = SEG_DMA if cs.startswith("d_") else SEG
            sems[cs] = [self.es.enter_context(nc.semaphore("s_%s_%d" % (cs, i))) for i in range((n + sg - 1) // sg)]
        with nc.Block() as block:
            sc.emit(nc, block, sems, fw)


def build_program(debug=None, nlayers=DEPTH, stop_after=None):
    b = Builder(debug=debug, nlayers=nlayers, stop_after=stop_after)
    nc = b.build()
    return nc, b


def _fm(v, nchunks):
    v = np.asarray(v, np.float32)
    return np.ascontiguousarray(v.reshape(L, nchunks, 128).transpose(2, 0, 1).reshape(128, L * nchunks))


def _rep(v):
    v = np.asarray(v, np.float32).reshape(1, -1)
    return np.ascontiguousarray(np.repeat(v, 128, axis=0))


def make_in_maps(inputs, ncores=8):
    import ml_dtypes
    f = lambda k: np.asarray(inputs[k], np.float32)
    x = f("x")
    cw = f("ssd_conv_w")
    cw_t = cw.reshape(L, 4, 12, 128).transpose(3, 0, 2, 1).reshape(128, L * 48)
    dww = f("conv_dw_w")
    dww_t = dww.reshape(L, 31, 4, 128).transpose(3, 0, 2, 1).reshape(128, L * 124)
    tab_fm = np.concatenate([
        _fm(f("norm_mix_g"), 8), _fm(f("norm_mlp_g"), 8), _fm(f("gate_b"), 32), cw_t,
        _fm(f("ssd_conv_b"), 12), _fm(f("pool_scale"), 4), dww_t, _fm(f("conv_dw_b"), 4),
        _fm(f("conv_ln_g"), 4), _fm(f("conv_ln_b"), 4)], axis=1).astype(np.float32)
    assert tab_fm.shape == (128, NFM), tab_fm.shape
    sk = f("attn_sinks")
    sk_t = np.zeros((128, L * 4), np.float32)
    for l in range(L):
        sk_t[0:64, l * 4:(l + 1) * 4] = sk[l, 0:4][None, :]
        sk_t[64:128, l * 4:(l + 1) * 4] = sk[l, 4:8][None, :]
    tab_rp = np.concatenate([_rep(f("ssd_dt_bias")), _rep(f("ssd_a_log")), _rep(f("ssd_d")),
                             _rep(f("q_norm_g")), _rep(f("k_norm_g")), sk_t], axis=1).astype(np.float32)
    assert tab_rp.shape == (128, NRP), tab_rp.shape
    gnorm_rep = np.ascontiguousarray(np.repeat(f("ssd_norm_g")[:, None, :], 128, axis=1))
    inv = (1.0 / (10000.0 ** (np.arange(0, 64, 2, dtype=np.float32) / np.float32(64.0)))).astype(np.float32)
    ang = (np.arange(S, dtype=np.float32)[:, None] * inv[None, :]).astype(np.float32)
    cos = np.cos(ang).astype(np.float32).reshape(NT, 128, 32).transpose(1, 0, 2)
    sin = np.sin(ang).astype(np.float32).reshape(NT, 128, 32).transpose(1, 0, 2)
    rope_c = np.ascontiguousarray(cos.reshape(128, NT * 32))
    rope_s2 = np.ascontiguousarray(np.stack([-sin, sin], axis=2).reshape(128, NT * 64))
    k = np.arange(128)
    U = (k[:, None] <= k[None, :]).astype(np.float32)
    Lm = (k[:, None] > k[None, :]).astype(np.float32)
    c_f32 = np.concatenate([U, Lm, np.ones((128, 128), np.float32)], axis=1)
    mcur = np.tile(U, (1, 4))
    mprev = np.tile(Lm, (1, 4))
    onl = np.zeros((128, 128), np.float32)
    onl[:, 0:64] = 1.0
    onr = np.zeros((128, 128), np.float32)
    onr[:, 64:128] = 1.0
    c_bf = np.concatenate([np.eye(128, dtype=np.float32), U, mcur, mprev, onl, onr], axis=1).astype(ml_dtypes.bfloat16)
    assert c_bf.shape == (128, NCB)
    pw = f("pool_w")
    pool_w = np.ascontiguousarray(pw.transpose(2, 0, 1, 3).reshape(128, L * 4 * 128))
    shared = {
        "w_in": f("w_in"), "w_branch": f("w_branch"), "w_out": f("w_out"),
        "w_up": f("w_mlp_up"), "w_down": f("w_mlp_down"),
        "tab_fm": tab_fm, "tab_rp": tab_rp, "gnorm_rep": gnorm_rep,
        "rope_c": rope_c, "rope_s2": rope_s2, "c_f32": c_f32, "c_bf": c_bf, "pool_w": pool_w,
    }
    maps = []
    for c in range(ncores):
        m = dict(shared)
        m["xT"] = np.ascontiguousarray(x[c].T)
        maps.append(m)
    return maps


def kernel(**inputs):
    nc, b = build_program()
    in_maps = make_in_maps(inputs)
    res = run_bass_kernel_spmd(nc, in_maps, core_ids=list(range(8)))
    out = np.stack([np.ascontiguousarray(r["yT"].T) for r in res.results], axis=0)
    return out.astype(np.float32)
```
